# Optimizing a Trainium2 kernel written in Bass

```python
import math, functools
import jax
import jax.numpy as jnp
from jax import lax
import numpy as np

D_MODEL = 1024
BATCH = 32
SEQ = 256
DEPTH = 4
DEC_BATCH = 4
DEC_SEQ = 1024
PAST_LEN = 256

GRID_W = 64
N_DIR = 2
D_MIX = D_MODEL
GROUP_W = D_MIX // 4
HEAD_DIM = 64
N_HEADS_G = GROUP_W // HEAD_DIM
D_FF = 4 * D_MODEL
MOD_CH = 6
NORM_EPS = 1e-6
LB_FLOOR = 1e-30
RW_LORA_W = 16
RW_LORA_A = 16
RW_LORA_G = 32
RW_GN_EPS = 64e-5
RW_COLS = 3 * GROUP_W + N_DIR * RW_LORA_W + N_DIR * RW_LORA_A + RW_LORA_G
RW_SPLITS = [GROUP_W, 2 * GROUP_W, 3 * GROUP_W, 3 * GROUP_W + N_DIR * RW_LORA_W,
             3 * GROUP_W + N_DIR * RW_LORA_W + N_DIR * RW_LORA_A]
HG_CHUNK = 16
HG_COLS = 5 * GROUP_W
SSD_GROUPS = 2
SSD_STATE = 128
SSD_CHUNK = 64
SS_CONV_CH = GROUP_W + 2 * SSD_GROUPS * SSD_STATE
SS_COLS = GROUP_W + SS_CONV_CH + N_DIR * N_HEADS_G
LRU_C = 8.0
LR_COLS = 2 * GROUP_W
CONV_W = 4
CONV_PAD_L = 2
CONV_PAD_R = 1
P_IN = RW_COLS + HG_COLS + SS_COLS + LR_COLS

kernel_name = 'hybrid_bidir_recurrent_diffusion_step'


def rms_norm(x, g):
    xf = x.astype(jnp.float32)
    y = xf * lax.rsqrt(jnp.mean(xf * xf, axis=-1, keepdims=True) + NORM_EPS)
    return (y * g.astype(jnp.float32)).astype(x.dtype)


def flip_if(z, d):
    return jnp.flip(z, axis=1) if d == 1 else z


def seq_centred_shift(p):
    half = p.shape[-1] // 2
    prev = jnp.pad(p[:, :-1, :half], ((0, 0), (1, 0), (0, 0)))
    nxt = jnp.pad(p[:, 1:, half:], ((0, 0), (0, 1), (0, 0)))
    return jnp.concatenate([prev, nxt], axis=-1)


def grid_q_shift(p, rows):
    b, t, ch = p.shape
    g = p.reshape(b, rows, GRID_W, ch)
    q = ch // 4
    left = jnp.pad(g[:, :, :-1, :q], ((0, 0), (0, 0), (1, 0), (0, 0)))
    right = jnp.pad(g[:, :, 1:, q:2 * q], ((0, 0), (0, 0), (0, 1), (0, 0)))
    up = jnp.pad(g[:, :-1, :, 2 * q:3 * q], ((0, 0), (1, 0), (0, 0), (0, 0)))
    down = jnp.pad(g[:, 1:, :, 3 * q:], ((0, 0), (0, 1), (0, 0), (0, 0)))
    return jnp.concatenate([left, right, up, down], axis=-1).reshape(b, t, ch)


def centred_dwconv(x, w, b):
    t = x.shape[1]
    xp = jnp.pad(x, ((0, 0), (CONV_PAD_L, CONV_PAD_R), (0, 0)))
    y = b + xp[:, 0:t] * w[0]
    for j in range(1, CONV_W):
        y = y + xp[:, j:j + t] * w[j]
    return y


def modulation(cond, w, b):
    m = jax.nn.silu(cond) @ w + b
    return m.reshape(cond.shape[0], MOD_CH, D_MODEL)


def rwkv7_scan(r, decay, k, v, kk, a, s0):
    def step(s, inp):
        r_t, w_t, k_t, v_t, kk_t, a_t = inp
        sa = jnp.einsum('bhvk,bhk->bhv', s, -kk_t)
        s = s * w_t[:, :, None, :] + sa[..., None] * (kk_t * a_t)[:, :, None, :] + v_t[..., None] * k_t[:, :, None, :]
        return s, jnp.einsum('bhvk,bhk->bhv', s, r_t)
    xs = tuple(jnp.moveaxis(z, 1, 0) for z in (r, decay, k, v, kk, a))
    s_fin, y = lax.scan(step, s0, xs)
    return jnp.moveaxis(y, 0, 1), s_fin


def rwkv7_mix(p, lw, shift_fn, s0):
    bsz, t, _ = p.shape
    heads = lambda z: z.reshape(bsz, t, N_HEADS_G, HEAD_DIM)
    p = (p + (shift_fn(p) - p) * lw['rw_mu']).astype(jnp.float32)
    r, k, v, wd, ad, gd = jnp.split(p, RW_SPLITS, axis=-1)
    g = jax.nn.sigmoid(gd) @ lw['rw_g2']
    kk = heads(k * lw['rw_kk'])
    kk = kk / jnp.maximum(jnp.sqrt(jnp.sum(kk * kk, axis=-1, keepdims=True)), 1e-12)
    rh, vh = heads(r), heads(v)
    ys, ks, finals = [], [], []
    for d in range(N_DIR):
        w_lora = jnp.tanh(wd[..., d * RW_LORA_W:(d + 1) * RW_LORA_W]) @ lw['rw_w2'][d]
        w_log = -jax.nn.softplus(-(lw['rw_w0'][d] + w_lora)) - 0.5
        decay = jnp.exp(-jnp.exp(w_log))
        a = jax.nn.sigmoid(lw['rw_a0'][d] + ad[..., d * RW_LORA_A:(d + 1) * RW_LORA_A] @ lw['rw_a2'][d])
        k_d = k * (1.0 + (a - 1.0) * lw['rw_ka'])
        seqs = [flip_if(z, d) for z in (rh, heads(decay), heads(k_d), vh, kk, heads(a))]
        y_d, s_d = rwkv7_scan(*seqs, s0[:, d].astype(jnp.float32))
        ys.append(flip_if(y_d, d))
        ks.append(k_d)
        finals.append(s_d)
    y = ys[0] + ys[1]
    mu = jnp.mean(y, axis=-1, keepdims=True)
    var = jnp.mean(jnp.square(y - mu), axis=-1, keepdims=True)
    y = ((y - mu) * lax.rsqrt(var + RW_GN_EPS)).reshape(bsz, t, GROUP_W) * lw['rw_ln_w'] + lw['rw_ln_b']
    bonus = jnp.sum(rh * heads(ks[0] + ks[1]) * lw['rw_rk'], axis=-1, keepdims=True) * vh
    out = (y + bonus.reshape(bsz, t, GROUP_W)) * g
    return out, jnp.stack(finals, axis=1)


def gla_chunk_scan(q, k, v, logf, s0):
    bsz, t, h, dk = q.shape
    dv = v.shape[-1]
    nc = t // HG_CHUNK
    q, k, logf = (z.reshape(bsz, nc, HG_CHUNK, h, dk) for z in (q, k, logf))
    v = v.reshape(bsz, nc, HG_CHUNK, h, dv)
    b = jnp.cumsum(logf, axis=2)
    tri = jnp.tril(jnp.ones((HG_CHUNK, HG_CHUNK), dtype=bool))[None, None, :, :, None, None]
    diff = b[:, :, :, None] - b[:, :, None, :]
    rel = jnp.where(tri, jnp.exp(jnp.where(tri, diff, 0.0)), 0.0)
    att = jnp.einsum('bnthk,bntshk,bnshk->bnhts', q, rel, k)
    o_intra = jnp.einsum('bnhts,bnshv->bnthv', att, v)
    b_last = b[:, :, -1]
    q_dec = q * jnp.exp(b)
    k_dec = k * jnp.exp(b_last[:, :, None] - b)
    ds = jnp.einsum('bnshk,bnshv->bnhkv', k_dec, v)

    def step(s, inp):
        dec, d_s = inp
        return dec[..., None] * s + d_s, s
    s_fin, s_prev = lax.scan(step, s0, (jnp.moveaxis(jnp.exp(b_last), 1, 0), jnp.moveaxis(ds, 1, 0)))
    o_inter = jnp.einsum('bnthk,nbhkv->bnthv', q_dec, s_prev)
    return (o_intra + o_inter).reshape(bsz, t, h, dv), s_fin


def hgrn2_mix(p, lw, lb, s0):
    bsz, t, _ = p.shape
    heads = lambda z: z.reshape(bsz, t, N_HEADS_G, HEAD_DIM)
    p = p.astype(jnp.float32)
    q, i, f_fw, f_bw, g = jnp.split(p, 5, axis=-1)
    qh, ih = heads(jax.nn.silu(q)), heads(i)
    ys, finals = [], []
    for d, f_raw in enumerate((f_fw, f_bw)):
        lb_d = lb[d]
        logf = jnp.logaddexp(jnp.log(jnp.maximum(lb_d, LB_FLOOR)), jnp.log1p(-lb_d) + jax.nn.log_sigmoid(f_raw))
        k_d = -jnp.expm1(logf)
        seqs = [flip_if(z, d) for z in (qh, heads(k_d), ih, heads(logf))]
        y_d, s_d = gla_chunk_scan(*seqs, s0[:, d].astype(jnp.float32))
        ys.append(flip_if(y_d, d))
        finals.append(s_d)
    o = (ys[0] + ys[1]).reshape(bsz, t, GROUP_W)
    out = rms_norm(o, lw['hg_norm']) * jax.nn.silu(g)
    return out, jnp.stack(finals, axis=1)


def ssd_chunk_scan(x, dt, a_neg, bm, cm, s0):
    bsz, t, h, pd = x.shape
    nc = t // SSD_CHUNK
    rep = h // SSD_GROUPS
    bh = jnp.repeat(bm, rep, axis=2).reshape(bsz, nc, SSD_CHUNK, h, SSD_STATE)
    ch = jnp.repeat(cm, rep, axis=2).reshape(bsz, nc, SSD_CHUNK, h, SSD_STATE)
    x = x.reshape(bsz, nc, SSD_CHUNK, h, pd)
    dt = dt.reshape(bsz, nc, SSD_CHUNK, h)
    acum = jnp.cumsum(dt * a_neg, axis=2)
    tri = jnp.tril(jnp.ones((SSD_CHUNK, SSD_CHUNK), dtype=bool))[None, None, :, :, None]
    seg = acum[:, :, :, None, :] - acum[:, :, None, :, :]
    decay_ts = jnp.where(tri, jnp.exp(jnp.where(tri, seg, 0.0)), 0.0)
    xdt = x * dt[..., None]
    cb = jnp.einsum('bzthn,bzshn->bztsh', ch, bh)
    y_diag = jnp.einsum('bztsh,bzshp->bzthp', cb * decay_ts, xdt)
    a_last = acum[:, :, -1]
    ds = jnp.einsum('bzshn,bzsh,bzshp->bzhpn', bh, jnp.exp(a_last[:, :, None] - acum), xdt)

    def step(s, inp):
        dec, d_s = inp
        return dec[..., None, None] * s + d_s, s
    s_fin, s_prev = lax.scan(step, s0, (jnp.moveaxis(jnp.exp(a_last), 1, 0), jnp.moveaxis(ds, 1, 0)))
    y_off = jnp.einsum('bzthn,zbhpn->bzthp', ch, s_prev) * jnp.exp(acum)[..., None]
    return (y_diag + y_off).reshape(bsz, t, h, pd), s_fin


def ssd_mix(p, lw, s0):
    bsz, t, _ = p.shape
    z, xbc, dtr = jnp.split(p, [GROUP_W, GROUP_W + SS_CONV_CH], axis=-1)
    xbc = jax.nn.silu(centred_dwconv(xbc, lw['ss_conv_w'], lw['ss_conv_b'])).astype(jnp.float32)
    xs, bm, cm = jnp.split(xbc, [GROUP_W, GROUP_W + SSD_GROUPS * SSD_STATE], axis=-1)
    xh = xs.reshape(bsz, t, N_HEADS_G, HEAD_DIM)
    bm = bm.reshape(bsz, t, SSD_GROUPS, SSD_STATE)
    cm = cm.reshape(bsz, t, SSD_GROUPS, SSD_STATE)
    ys, finals = [], []
    for d in range(N_DIR):
        dt = jax.nn.softplus(dtr[..., d * N_HEADS_G:(d + 1) * N_HEADS_G].astype(jnp.float32) + lw['ss_dt_bias'][d])
        a_neg = -jnp.exp(lw['ss_A_log'][d].astype(jnp.float32))
        y_d, s_d = ssd_chunk_scan(flip_if(xh, d), flip_if(dt, d), a_neg, flip_if(bm, d), flip_if(cm, d),
                                  s0[:, d].astype(jnp.float32))
        ys.append(flip_if(y_d, d))
        finals.append(s_d)
    y = ys[0] + ys[1] + lw['ss_D'][:, None] * xh
    out = rms_norm(y.reshape(bsz, t, GROUP_W) * jax.nn.silu(z.astype(jnp.float32)), lw['ss_norm'])
    return out, jnp.stack(finals, axis=1)


def diag_linear_scan(a, b, h0):
    b = b.at[:, 0].add(a[:, 0] * h0)

    def comb(l, r):
        return (l[0] * r[0], r[0] * l[1] + r[1])
    _, h = lax.associative_scan(comb, (a, b), axis=1)
    return h, h[:, -1]


def rglru_mix(p, lw, s0):
    bsz, t, _ = p.shape
    ybr, xbr = jnp.split(p, 2, axis=-1)
    xc = centred_dwconv(xbr, lw['lr_conv_w'], lw['lr_conv_b']).astype(jnp.float32)
    xh = xc.reshape(bsz, t, N_HEADS_G, HEAD_DIM)
    hs, finals = [], []
    for d in range(N_DIR):
        r = jax.nn.sigmoid(jnp.einsum('bthi,hij->bthj', xh, lw['lr_wa'][d]).reshape(bsz, t, GROUP_W) + lw['lr_ba'][d])
        i = jax.nn.sigmoid(jnp.einsum('bthi,hij->bthj', xh, lw['lr_wx'][d]).reshape(bsz, t, GROUP_W) + lw['lr_bx'][d])
        log_a = -LRU_C * r * jax.nn.softplus(-lw['lr_lam'][d].astype(jnp.float32))
        a = jnp.exp(log_a)
        b = jnp.sqrt(jnp.maximum(-jnp.expm1(2.0 * log_a), 0.0)) * (i * xc)
        h_d, s_d = diag_linear_scan(flip_if(a, d), flip_if(b, d), s0[:, d].astype(jnp.float32))
        hs.append(flip_if(h_d, d))
        finals.append(s_d)
    out = (hs[0] + hs[1]) * jax.nn.gelu(ybr.astype(jnp.float32))
    return out, jnp.stack(finals, axis=1)


def trunk_layer(x, mod, lw, lb, shift_fn, states):
    sh1, sc1, g1, sh2, sc2, g2 = (mod[:, i, None, :] for i in range(MOD_CH))
    h = rms_norm(x, lw['g_pre1']) * (1 + sc1) + sh1
    proj = h @ lw['w_in']
    p_rw, p_hg, p_ss, p_lr = jnp.split(proj, [RW_COLS, RW_COLS + HG_COLS, RW_COLS + HG_COLS + SS_COLS], axis=-1)
    s_rw, s_hg, s_ss, s_lr = states
    y_rw, f_rw = rwkv7_mix(p_rw, lw, shift_fn, s_rw)
    y_hg, f_hg = hgrn2_mix(p_hg, lw, lb, s_hg)
    y_ss, f_ss = ssd_mix(p_ss, lw, s_ss)
    y_lr, f_lr = rglru_mix(p_lr, lw, s_lr)
    mix = jnp.concatenate([y_rw, y_hg, y_ss, y_lr], axis=-1).astype(x.dtype) @ lw['w_out']
    x = x + g1 * rms_norm(mix, lw['g_post1'])
    h = rms_norm(x, lw['g_pre2']) * (1 + sc2) + sh2
    ff = jnp.square(jax.nn.relu(h @ lw['w1'])) @ lw['w2']
    x = x + g2 * rms_norm(ff, lw['g_post2'])
    return x, (f_rw, f_hg, f_ss, f_lr)


def setup_inputs(seed: int = 0) -> dict:
    key = jax.random.key(seed)
    ks = iter(jax.random.split(key, 64))
    f32 = jnp.float32

    def nrm(shape, scale=1.0):
        return scale * jax.random.normal(next(ks), shape, f32)

    def uni(shape, lo, hi):
        return jax.random.uniform(next(ks), shape, f32, lo, hi)
    H = N_HEADS_G
    dt0 = jnp.exp(uni((DEPTH, N_DIR, H), math.log(1e-3), math.log(1e-1)))
    lam_u = uni((DEPTH, N_DIR, GROUP_W), 0.9, 0.999) ** (1.0 / LRU_C)
    return {
        'x_prompt': nrm((BATCH, SEQ, D_MODEL)),
        'x_sample': nrm((DEC_BATCH, DEC_SEQ, D_MODEL)),
        'state_rwkv': nrm((DEC_BATCH, DEPTH, N_DIR, H, HEAD_DIM, HEAD_DIM), 0.5),
        'state_hgrn': nrm((DEC_BATCH, DEPTH, N_DIR, H, HEAD_DIM, HEAD_DIM), 0.5),
        'state_ssd': nrm((DEC_BATCH, DEPTH, N_DIR, H, HEAD_DIM, SSD_STATE), 0.5),
        'state_lru': nrm((DEC_BATCH, DEPTH, N_DIR, GROUP_W), 0.5),
        'c': nrm((DEC_BATCH, D_MODEL)),
        'c_ctx': nrm((D_MODEL,)),
        'mod_w': nrm((DEPTH, D_MODEL, MOD_CH * D_MODEL), 0.5 * D_MODEL ** -0.5),
        'mod_b': nrm((DEPTH, MOD_CH * D_MODEL), 0.01),
        'norm_g': 1.0 + nrm((DEPTH, 4, D_MODEL), 0.02),
        'w_in': nrm((DEPTH, D_MODEL, P_IN), D_MODEL ** -0.5),
        'w_out': nrm((DEPTH, D_MIX, D_MODEL), D_MIX ** -0.5),
        'rw_mu': uni((DEPTH, RW_COLS), 0.0, 1.0),
        'rw_w0': uni((DEPTH, N_DIR, GROUP_W), -4.0, -1.0),
        'rw_w2': nrm((DEPTH, N_DIR, RW_LORA_W, GROUP_W), 0.1),
        'rw_a0': nrm((DEPTH, N_DIR, GROUP_W), 0.1),
        'rw_a2': nrm((DEPTH, N_DIR, RW_LORA_A, GROUP_W), 0.1),
        'rw_g2': nrm((DEPTH, RW_LORA_G, GROUP_W), RW_LORA_G ** -0.5),
        'rw_kk': 0.85 + nrm((DEPTH, GROUP_W), 0.02),
        'rw_ka': 1.0 + nrm((DEPTH, GROUP_W), 0.02),
        'rw_rk': nrm((DEPTH, H, HEAD_DIM), 0.1),
        'rw_ln_w': 1.0 + nrm((DEPTH, GROUP_W), 0.02),
        'rw_ln_b': nrm((DEPTH, GROUP_W), 0.01),
        'hg_lb': 1.0 + nrm((DEPTH, N_DIR, GROUP_W), 0.1),
        'hg_norm': 1.0 + nrm((DEPTH, GROUP_W), 0.02),
        'ss_conv_w': nrm((DEPTH, CONV_W, SS_CONV_CH), CONV_W ** -0.5),
        'ss_conv_b': nrm((DEPTH, SS_CONV_CH), 0.01),
        'ss_dt_bias': dt0 + jnp.log(-jnp.expm1(-dt0)),
        'ss_A_log': jnp.log(uni((DEPTH, N_DIR, H), 1.0, 16.0)),
        'ss_D': 1.0 + nrm((DEPTH, H), 0.02),
        'ss_norm': 1.0 + nrm((DEPTH, GROUP_W), 0.02),
        'lr_conv_w': nrm((DEPTH, CONV_W, GROUP_W), CONV_W ** -0.5),
        'lr_conv_b': nrm((DEPTH, GROUP_W), 0.01),
        'lr_wa': nrm((DEPTH, N_DIR, H, HEAD_DIM, HEAD_DIM), HEAD_DIM ** -0.5),
        'lr_ba': nrm((DEPTH, N_DIR, GROUP_W), 0.01),
        'lr_wx': nrm((DEPTH, N_DIR, H, HEAD_DIM, HEAD_DIM), HEAD_DIM ** -0.5),
        'lr_bx': nrm((DEPTH, N_DIR, GROUP_W), 0.01),
        'lr_lam': jnp.log(lam_u) - jnp.log1p(-lam_u),
        'mlp_w1': nrm((DEPTH, D_MODEL, D_FF), D_MODEL ** -0.5),
        'mlp_w2': nrm((DEPTH, D_FF, D_MODEL), D_FF ** -0.5),
    }


def reference(x_prompt, x_sample, state_rwkv, state_hgrn, state_ssd, state_lru, c, c_ctx,
              mod_w, mod_b, norm_g, w_in, w_out,
              rw_mu, rw_w0, rw_w2, rw_a0, rw_a2, rw_g2, rw_kk, rw_ka, rw_rk, rw_ln_w, rw_ln_b,
              hg_lb, hg_norm,
              ss_conv_w, ss_conv_b, ss_dt_bias, ss_A_log, ss_D, ss_norm,
              lr_conv_w, lr_conv_b, lr_wa, lr_ba, lr_wx, lr_bx, lr_lam,
              mlp_w1, mlp_w2):
    f32 = jnp.float32
    lb_soft = jax.nn.softmax(hg_lb.astype(f32), axis=0)
    lower_bounds = jnp.cumsum(lb_soft, axis=0) - lb_soft[0]
    rows = x_sample.shape[1] // GRID_W
    latent_shift = functools.partial(grid_q_shift, rows=rows)
    bp = x_prompt.shape[0]
    zero_states = (jnp.zeros((bp, N_DIR, N_HEADS_G, HEAD_DIM, HEAD_DIM), f32),
                   jnp.zeros((bp, N_DIR, N_HEADS_G, HEAD_DIM, HEAD_DIM), f32),
                   jnp.zeros((bp, N_DIR, N_HEADS_G, HEAD_DIM, SSD_STATE), f32),
                   jnp.zeros((bp, N_DIR, GROUP_W), f32))
    xp, xs = x_prompt, x_sample
    new_rw, new_hg, new_ss, new_lr = [], [], [], []
    for l in range(DEPTH):
        lw = {
            'w_in': w_in[l], 'w_out': w_out[l],
            'g_pre1': norm_g[l, 0], 'g_post1': norm_g[l, 1], 'g_pre2': norm_g[l, 2], 'g_post2': norm_g[l, 3],
            'w1': mlp_w1[l], 'w2': mlp_w2[l],
            'rw_mu': rw_mu[l], 'rw_w0': rw_w0[l], 'rw_w2': rw_w2[l], 'rw_a0': rw_a0[l], 'rw_a2': rw_a2[l],
            'rw_g2': rw_g2[l], 'rw_kk': rw_kk[l], 'rw_ka': rw_ka[l], 'rw_rk': rw_rk[l],
            'rw_ln_w': rw_ln_w[l], 'rw_ln_b': rw_ln_b[l],
            'hg_norm': hg_norm[l],
            'ss_conv_w': ss_conv_w[l], 'ss_conv_b': ss_conv_b[l], 'ss_dt_bias': ss_dt_bias[l],
            'ss_A_log': ss_A_log[l], 'ss_D': ss_D[l], 'ss_norm': ss_norm[l],
            'lr_conv_w': lr_conv_w[l], 'lr_conv_b': lr_conv_b[l], 'lr_wa': lr_wa[l], 'lr_ba': lr_ba[l],
            'lr_wx': lr_wx[l], 'lr_bx': lr_bx[l], 'lr_lam': lr_lam[l],
        }
        mod_ctx = modulation(c_ctx[None, :], mod_w[l], mod_b[l])
        mod_lat = modulation(c, mod_w[l], mod_b[l])
        xp, (f_rw, f_hg, f_ss, f_lr) = trunk_layer(xp, mod_ctx, lw, lower_bounds[l], seq_centred_shift, zero_states)
        new_rw.append(f_rw)
        new_hg.append(f_hg)
        new_ss.append(f_ss)
        new_lr.append(f_lr)
        cached = (state_rwkv[:, l], state_hgrn[:, l], state_ssd[:, l], state_lru[:, l])
        xs, _ = trunk_layer(xs, mod_lat, lw, lower_bounds[l], latent_shift, cached)
    new_state_rwkv = jnp.stack(new_rw, axis=1).astype(x_prompt.dtype)
    new_state_hgrn = jnp.stack(new_hg, axis=1).astype(x_prompt.dtype)
    new_state_ssd = jnp.stack(new_ss, axis=1).astype(x_prompt.dtype)
    new_state_lru = jnp.stack(new_lr, axis=1).astype(x_prompt.dtype)
    return (xp, xs, new_state_rwkv, new_state_hgrn, new_state_ssd, new_state_lru)
```

```python
import numpy as np
import concourse.bass as bass
import concourse.mybir as mybir
from concourse.bass_utils import run_bass_kernel_spmd

F32 = mybir.dt.float32
BF16 = mybir.dt.bfloat16
ALU = mybir.AluOpType
AF = mybir.ActivationFunctionType

D = 1024
DEPTH = 4
NT = 1024
DFF = 4096
NCORES = 8
WINP = 3840
EPS = 1e-6

C_RW, C_HG, C_SS, C_LR = 0, 8, 18, 26


def _win_colmap():
    m = []
    m.append((0, 0, 768))
    m.append((768 + 0, 768, 16))
    m.append((768 + 32, 784, 16))
    m.append((768 + 64, 832, 32))
    m.append((896 + 0, 800, 16))
    m.append((896 + 32, 816, 16))
    m.append((1024, 864, 1280))
    m.append((2304, 2144, 1024))
    m.append((3328, 3176, 512))
    return m


_PV = {}
_NPV = 0


def _pv_add(name, ncols):
    global _NPV
    _PV[name] = (_NPV, ncols)
    _NPV += ncols


for _l in range(DEPTH):
    for _n, _c in (("g_pre1", 8), ("g_post1", 8), ("g_pre2", 8), ("g_post2", 8), ("mod_b", 48),
                   ("lr_conv_w", 8), ("lr_conv_b", 2), ("lr_ba", 4), ("lr_bx", 4), ("lr_lam", 4),
                   ("ss_conv_w", 24), ("ss_conv_b", 6), ("ss_D", 2), ("ss_norm", 2),
                   ("hg_lb", 4), ("hg_norm", 2)):
        _pv_add(f"{_n}{_l}", _c)
for _l in range(DEPTH):
    for _n, _c in (("rw_mu", 8), ("rw_w0", 4), ("rw_a0", 4), ("rw_kk", 2), ("rw_ka", 2), ("rw_rk", 2), ("rw_ln_w", 2), ("rw_ln_b", 2)):
        _pv_add(f"{_n}{_l}", _c)
_pv_add("mP", 16)
_pv_add("mS", 32)
_pv_add("cond", 16)
_pv_add("st_lr", DEPTH * 4)


def pvc(name, l=None):
    return _PV[name if l is None else f"{name}{l}"][0]


class Tok:
    __slots__ = ("w", "r")

    def __init__(self):
        self.w = None
        self.r = {}


class FW:
    CH = 30000

    def __init__(self, nc):
        self.nc = nc
        self.eng = {"pe": nc.tensor, "act": nc.scalar, "dve": nc.vector, "pool": nc.gpsimd, "sp": nc.sync}
        self.sems = {k: [] for k in self.eng}
        self.cnt = {k: 0 for k in self.eng}
        self.waited = {k: {} for k in self.eng}
        self.dq = {}
        self.ndma = {}
        self.nsem = 0

    def _newsem(self, name):
        self.nsem += 1
        return self.nc.alloc_semaphore(f"{name}_{self.nsem}")

    def _wait(self, e, dep):
        if dep is None:
            return
        sem, val, owner = dep
        if owner == "pe" and e == "pe":
            return
        key = id(sem)
        if self.waited[e].get(key, 0) >= val:
            return
        self.eng[e].wait_ge(sem, val)
        self.waited[e][key] = val

    def _deps(self, e, R, W):
        for t in R:
            self._wait(e, t.w)
        for t in W:
            self._wait(e, t.w)
            for d in t.r.values():
                self._wait(e, d)

    def _commit(self, dep, R, W):
        k = id(dep[0])
        for t in R:
            o = t.r.get(k)
            if o is None or o[1] < dep[1]:
                t.r[k] = dep
        for t in W:
            t.w = dep
            t.r = {}

    def op(self, e, fn, R=(), W=()):
        self._deps(e, R, W)
        n = self.cnt[e]
        ch, loc = divmod(n, self.CH)
        if ch >= len(self.sems[e]):
            self.sems[e].append(self._newsem(e))
        sem = self.sems[e][ch]
        inst = fn(self.eng[e])
        inst.then_inc(sem, 1)
        self.cnt[e] = n + 1
        dep = (sem, loc + 1, e)
        self._commit(dep, R, W)
        return dep

    def dma(self, q, out, in_, R=(), W=(), P=8, **kw):
        self._deps(q, R, W)
        if q not in self.dq:
            self.dq[q] = [[self._newsem("d" + q), 0] for _ in range(P)]
            self.ndma[q] = 0
        i = self.ndma[q] % P
        self.ndma[q] += 1
        slot = self.dq[q][i]
        if slot[1] > 0:
            self._wait(q, (slot[0], slot[1], "dma"))
        inst = self.eng[q].dma_start(out=out, in_=in_, **kw)
        slot[1] += 16
        inst.then_inc(slot[0], 16)
        dep = (slot[0], slot[1], "dma")
        self._commit(dep, R, W)
        return dep

    def finish(self, toks, e="sp"):
        for t in toks:
            self._wait(e, t.w)


class Prog:
    def __init__(self, depth=DEPTH):
        self.depth = depth
        nc = self.nc = bass.Bass("TRN2", target_bir_lowering=False)
        self.fw = FW(nc)
        self.outs = []
        dt = nc.dram_tensor
        self.x_d = {p: dt("x" + p, [D, NT], F32, kind="ExternalInput").ap() for p in "PS"}
        self.y_d = {p: dt("y" + p, [D, NT], F32, kind="ExternalOutput").ap() for p in "PS"}
        self.pv_d = dt("pv", [128, _NPV], F32, kind="ExternalInput").ap()
        self.w_in_d = dt("w_in_p", [DEPTH, D, WINP], F32, kind="ExternalInput").ap()
        self.w_out_d = dt("w_out", [DEPTH, D, D], F32, kind="ExternalInput").ap()
        self.w1_d = dt("w1", [DEPTH, D, DFF], F32, kind="ExternalInput").ap()
        self.w2_d = dt("w2", [DEPTH, DFF, D], F32, kind="ExternalInput").ap()
        self.modw_d = dt("mod_w", [DEPTH, D, 6 * D], F32, kind="ExternalInput").ap()
        self.lrw_d = dt("lr_w", [DEPTH, 2, 2, 2, 128, 128], F32, kind="ExternalInput").ap()
        self.wdt_d = dt("w_dt", [DEPTH, D, 8], F32, kind="ExternalInput").ap()
        self.ssb_d = dt("ssb", [DEPTH, 128, 16], F32, kind="ExternalInput").ap()
        self.cst_d = dt("cst", [128, 6, 128], F32, kind="ExternalInput").ap()
        self.rwlw_d = dt("rw_lw", [DEPTH, 5, 128, 256], F32, kind="ExternalInput").ap()
        self.st_ss_d = dt("st_ss", [DEPTH, 2, 4, 128, 64], F32, kind="ExternalInput").ap()
        self.st_hg_d = dt("st_hg", [DEPTH, 2, 2, 128, 64], F32, kind="ExternalInput").ap()
        self.st_rw_d = dt("st_rw", [DEPTH, 2, 2, 128, 64], F32, kind="ExternalInput").ap()
        self.RAd = {m: dt("RAd_" + m, [4, 6, NT, 128], F32).ap() for m in ("hg", "rw")}
        self.RBd = dt("RBd", [2, 2, NT, 64], F32).ap()
        self.Yd = dt("Yd", [4, 4, NT + 8, 64], F32).ap()
        self.o_lr_d = dt("o_lr", [128, 4 * DEPTH * 4], F32, kind="ExternalOutput").ap()
        self.o_rw_d = dt("o_rw", [4, DEPTH, 2, 2, 128, 64], F32, kind="ExternalOutput").ap()
        self.o_hg_d = dt("o_hg", [4, DEPTH, 2, 2, 128, 64], F32, kind="ExternalOutput").ap()
        self.o_ss_d = dt("o_ss", [4, DEPTH, 2, 4, 128, 64], F32, kind="ExternalOutput").ap()
        sb = nc.alloc_sbuf_tensor
        self.pv = sb("pv_sb", [128, _NPV], F32)
        self.t_pv = Tok()
        self.x = sb("x_sb", [128, 8, NT], F32)
        self.t_x = Tok()
        self.h = sb("h_sb", [128, 8, NT], BF16)
        self.t_h = Tok()
        self.mix = sb("mix_sb", [128, 8, NT], BF16)
        self.t_mix = [Tok() for _ in range(8)]
        self.wst = [sb(f"wst{i}", [128, 2048], F32) for i in range(2)]
        self.wbf = [sb(f"wbf{i}", [128, 2048], BF16) for i in range(2)]
        self.t_wst = [Tok(), Tok()]
        self.t_wbf = [Tok(), Tok()]
        self.nw = 0
        self.scr = sb("scr", [128, 8 * NT], F32)
        self.big = sb("big", [128, 16 * NT], F32)
        self.t_scr = Tok()
        self.t_big = Tok()
        self.rstd = sb("rstd", [128, NT], F32)
        self.t_rstd = Tok()
        self.rtmp = [self.rstd[:, 0:512], self.rstd[:, 512:1024]]
        self.t_rtmp = [self.t_rstd, self.t_rstd]
        self.sq = [sb("sq0", [128, NT], BF16)] * 2
        self.t_sq = [Tok()] * 2
        self.ones_bf = sb("ones_bf", [128, 128], BF16)
        self.t_const = Tok()
        self.modall = sb("modall", [128, DEPTH, 2, 48], F32)
        self.t_mod = Tok()
        self.cols = sb("cols", [128, 2, 48], F32)
        self.t_cols = Tok()
        self.scond = sb("scond", [128, 16], F32)
        self.lru_fin = sb("lru_fin", [128, 4 * DEPTH * 4], F32)
        self.t_lrufin = Tok()
        self.zero_o = sb("zero_o", [128, 64], F32)
        self.cst = sb("cst_sb", [128, 6, 128], F32)
        self.ones_f = sb("ones_f", [128, 128], F32)
        self.ss_S = sb("ss_S", [128, 8, 64], F32)
        self.t_ssS = [Tok() for _ in range(8)]
        self.ss_small = self.scr[:, 4096:4608].rearrange("p (i f) -> p i f", f=64)
        self.rwc = sb("rwc", [128, 64], F32)
        self.t_rwc = Tok()
        self.ssb = sb("ssb_sb", [128, 16], F32)
        self.wdt = sb("wdt_sb", [128, 8, 8], F32)
        self.wdt_bf = sb("wdt_bf", [128, 8, 8], BF16)
        self.sc_S = sb("sc_S", [128, 4, 64], F32)
        self.t_scS = [Tok() for _ in range(4)]
        self.sc_x = self.big[0:4, 16128:16384].rearrange("p (a f) -> p a f", f=64)
        self.t_scx = [Tok() for _ in range(4)]
        self.t_ps1 = [Tok() for _ in range(4)]
        self.t_ps2 = [[Tok(), Tok()] for _ in range(4)]
        self.hgl = sb("hgl", [128, DEPTH, 8], F32)
        self.t_hgl = Tok()
        self.ttile = [sb(f"ttile{i}", [128, 128], F32) for i in range(2)]
        self.t_ttile = [Tok(), Tok()]
        self.ytile = [sb(f"ytile{i}", [128, 2, 128], F32) for i in range(2)]
        self.t_ytile = [Tok(), Tok()]
        self.ntt = 0
        self.ps = [nc.alloc_psum_tensor(f"ps{i}", [128, 512], F32) for i in range(8)]
        self.t_ps = [Tok() for _ in range(8)]
        self.nps = 0

    def psum(self, lo=4, hi=6):
        i = lo + self.nps % (hi - lo)
        self.nps += 1
        return self.ps[i], self.t_ps[i]

    def wload(self, view, nk, ncl, convert=True):
        fw = self.fw
        i = self.nw % 2
        self.nw += 1
        st = self.wst[i][:, 0:nk * ncl].rearrange("p (k c) -> p k c", c=ncl)
        fw.dma("sp", st, view, W=[self.t_wst[i]])
        if not convert:
            return st, self.t_wst[i]
        bf = self.wbf[i][:, 0:nk * ncl].rearrange("p (k c) -> p k c", c=ncl)
        fw.op("pool", lambda e: e.tensor_copy(out=bf, in_=st), R=[self.t_wst[i]], W=[self.t_wbf[i]])
        return bf, self.t_wbf[i]

    def dense_fm(self, wl, K, c0, ncols, act, t_act, evac):
        fw = self.fw
        nk = K // 128
        wv = wl.rearrange("(k p) c -> p k c", p=128)
        if nk <= 8:
            bc = 2048 // nk
            bc = min(bc, ncols)
            for b0 in range(0, ncols, bc):
                wb, tw = self.wload(wv[:, :, c0 + b0:c0 + b0 + bc], nk, bc)
                for j in range(bc // 128):
                    for half in range(2):
                        ps, tp = self.psum()
                        for kc in range(nk):
                            fw.op("pe", lambda e: e.matmul(ps[:], lhsT=wb[:, kc, j * 128:(j + 1) * 128], rhs=act(kc, half),
                                                           start=(kc == 0), stop=(kc == nk - 1)), R=[tw] + t_act, W=[tp])
                        evac((b0 // 128) + j, half, ps, tp)
        else:
            for fc in range(ncols // 128):
                pss = [self.psum(), self.psum()]
                for kb in range(0, nk, 16):
                    wb, tw = self.wload(wv[:, kb:kb + 16, c0 + fc * 128:c0 + (fc + 1) * 128], 16, 128)
                    for half in range(2):
                        ps, tp = pss[half]
                        for kc in range(16):
                            fw.op("pe", lambda e: e.matmul(ps[:], lhsT=wb[:, kc, :], rhs=act(kb + kc, half),
                                                           start=(kb + kc == 0), stop=(kb + kc == nk - 1)), R=[tw] + t_act, W=[tp])
                for half in range(2):
                    evac(fc, half, pss[half][0], pss[half][1])

    def rms_stats(self, src, t_src, nch=8, dim=D):
        fw = self.fw
        pss = [self.psum(6, 8), self.psum(6, 8)]
        for c in range(nch):
            sq, tsq = self.sq[c % 2], self.t_sq[c % 2]
            s = src(c)
            fw.op("act", lambda e: e.activation(out=sq[:], in_=s, func=AF.Square), R=t_src, W=[tsq])
            for half in range(2):
                ps, tp = pss[half]
                fw.op("pe", lambda e: e.matmul(ps[:], lhsT=self.ones_bf[:], rhs=sq[:, half * 512:(half + 1) * 512],
                                               start=(c == 0), stop=(c == nch - 1)), R=[tsq, self.t_const], W=[tp])
        for half in range(2):
            ps, tp = pss[half]
            r = self.rstd[:, half * 512:(half + 1) * 512]
            fw.op("act", lambda e: e.activation(out=r, in_=ps[:], func=AF.Sqrt, bias=self.epsc[:, 0:1], scale=1.0 / dim),
                  R=[tp, self.t_const], W=[self.t_rstd])
            fw.op("dve", lambda e: e.reciprocal(out=r, in_=r), R=[self.t_rstd], W=[self.t_rstd])

    def build(self):
        nc, fw = self.nc, self.fw
        sb = nc.alloc_sbuf_tensor
        self.epsc = sb("epsc", [128, 2], F32)
        fw.op("pool", lambda e: e.memset(self.epsc[:, 0:1], EPS), W=[self.t_const])
        fw.op("pool", lambda e: e.memset(self.epsc[:, 1:2], 64e-5), W=[self.t_const])
        fw.op("pool", lambda e: e.memset(self.ones_bf[:], 1.0), W=[self.t_const])
        fw.op("pool", lambda e: e.memset(self.zero_o[:], 0.0), W=[self.t_const])
        fw.op("pool", lambda e: e.memset(self.lru_fin[:], 0.0), W=[self.t_lrufin])
        fw.dma("sp", self.pv[:], self.pv_d[:, :], W=[self.t_pv])
        fw.dma("sp", self.cst[:], self.cst_d[:, :, :], W=[self.t_const])
        fw.op("pool", lambda e: e.memset(self.ones_f[:], 1.0), W=[self.t_const])
        import os
        self.stage = int(os.environ.get("KSTAGE", "9"))
        if self.stage >= 1:
            self.modulation()
        if self.stage >= 7:
            self.zero_rows()
            self.hg_bounds()
        for path in "PS":
            self.run_path(path)
        t_o = Tok()
        fw.dma("sp", self.o_lr_d[:, :], self.lru_fin[:], R=[self.t_lrufin], W=[t_o])
        self.outs.append(t_o)
        for od in ((self.o_rw_d,) if self.stage < 8 else ()) + ((self.o_hg_d,) if self.stage < 7 else ()) + ((self.o_ss_d,) if self.stage < 6 else ()):
            flat = od.rearrange("s l d h p v -> p (s l d h) v")
            n = flat.shape[1]
            for i in range(n):
                t = Tok()
                fw.dma("sp", flat[:, i, :], self.zero_o[:], R=[self.t_const], W=[t])
                self.outs.append(t)
        fw.finish(self.outs)
        return nc

    def modulation(self):
        fw = self.fw
        c0 = pvc("cond")
        sig = self.nc.alloc_sbuf_tensor("sigc", [128, 16], F32)
        t_s = Tok()
        fw.op("act", lambda e: e.activation(out=sig[:], in_=self.pv[:, c0:c0 + 16], func=AF.Sigmoid), R=[self.t_pv], W=[t_s])
        fw.op("dve", lambda e: e.tensor_tensor(out=self.scond[:], in0=sig[:], in1=self.pv[:, c0:c0 + 16], op=ALU.mult),
              R=[t_s, self.t_pv], W=[t_s])
        sc = self.scond[:].rearrange("p (n k) -> p k n", k=8)
        for l in range(self.depth):
            wv = self.modw_d[l].rearrange("(k p) c -> p k c", p=128)
            ps, tp = self.psum(6, 8)
            for b in range(24):
                wb, tw = self.wload(wv[:, :, b * 256:(b + 1) * 256], 8, 256, convert=False)
                for j in range(2):
                    fc = b * 2 + j
                    for kc in range(8):
                        fw.op("pe", lambda e: e.matmul(ps[:, fc * 2:fc * 2 + 2], lhsT=wb[:, kc, j * 128:(j + 1) * 128], rhs=sc[:, kc, :],
                                                       start=(kc == 0), stop=(kc == 7)), R=[tw, t_s], W=[tp])
            mb = pvc("mod_b", l)
            for cnd in range(2):
                src = ps[:, 0:96].rearrange("p (f n) -> p n f", n=2)[:, cnd, :]
                fw.op("dve", lambda e: e.tensor_tensor(out=self.modall[:, l, cnd, :], in0=src, in1=self.pv[:, mb:mb + 48], op=ALU.add),
                      R=[tp, self.t_pv], W=[self.t_mod])

    def layer_cols(self, l, cnd):
        fw = self.fw
        m = self.modall[:, l, cnd, :]
        for k, (gpre, gpost, o) in enumerate((("g_pre1", "g_post1", 0), ("g_pre2", "g_post2", 24))):
            a = pvc(gpre, l)
            b = pvc(gpost, l)
            sh, scl, g = m[:, o:o + 8], m[:, o + 8:o + 16], m[:, o + 16:o + 24]
            fw.op("dve", lambda e: e.scalar_tensor_tensor(out=self.cols[:, 0, o:o + 8], in0=scl, scalar=1.0, in1=self.pv[:, a:a + 8],
                                                          op0=ALU.add, op1=ALU.mult), R=[self.t_mod, self.t_pv], W=[self.t_cols])
            fw.op("dve", lambda e: e.tensor_copy(out=self.cols[:, 0, o + 8:o + 16], in_=sh), R=[self.t_mod], W=[self.t_cols])
            fw.op("dve", lambda e: e.tensor_tensor(out=self.cols[:, 0, o + 16:o + 24], in0=g, in1=self.pv[:, b:b + 8], op=ALU.mult),
                  R=[self.t_mod, self.t_pv], W=[self.t_cols])

    def norm_mod(self, o):
        fw = self.fw
        self.rms_stats(lambda c: self.x[:, c, :], [self.t_x])
        tmp = self.scr[:, 0:NT]
        for c in range(8):
            fw.op("dve", lambda e: e.tensor_tensor(out=tmp, in0=self.x[:, c, :], in1=self.rstd[:], op=ALU.mult),
                  R=[self.t_x, self.t_rstd], W=[self.t_scr])
            fw.op("act", lambda e: e.activation(out=self.h[:, c, :], in_=tmp, func=AF.Identity,
                                                bias=self.cols[:, 0, o + 8 + c:o + 9 + c], scale=self.cols[:, 0, o + c:o + c + 1]),
                  R=[self.t_scr, self.t_cols], W=[self.t_h])

    def post_resid(self, res, t_res, o):
        fw = self.fw
        self.rms_stats(res, t_res)
        for c in range(8):
            r = res(c)
            fw.op("dve", lambda e: e.tensor_tensor(out=r, in0=r, in1=self.rstd[:], op=ALU.mult), R=[self.t_rstd] + t_res, W=t_res)
            fw.op("dve", lambda e: e.scalar_tensor_tensor(out=self.x[:, c, :], in0=r, scalar=self.cols[:, 0, o + 16 + c:o + 17 + c],
                                                          in1=self.x[:, c, :], op0=ALU.mult, op1=ALU.add),
                  R=t_res + [self.t_cols], W=[self.t_x])

    def run_path(self, path):
        fw = self.fw
        cnd = 0 if path == "P" else 1
        nseg, tseg = (4, 256) if path == "P" else (1, 1024)
        fw.dma("sp", self.x[:], self.x_d[path].rearrange("(c p) t -> p c t", p=128), W=[self.t_x])
        hact = lambda kc, half: self.h[:, kc, half * 512:(half + 1) * 512]
        for l in range(self.depth if self.stage >= 2 else 0):
            self.layer_cols(l, cnd)
            self.norm_mod(0)
            if self.stage == 2:
                continue
            for c in range(4):
                if (c < 2 and self.stage < 8) or (c >= 2 and self.stage < 7):
                    fw.op("pool", lambda e: e.memset(self.mix[:, c, :], 0.0), W=[self.t_mix[c]])
            if self.stage >= 8:
                self.rw_mixer(path, l, nseg, tseg)
            if self.stage >= 7:
                self.hg_mixer(path, l, nseg, tseg)
            if self.stage >= 6:
                self.ssd_mixer(path, l, nseg, tseg)
            else:
                for c in range(4, 6):
                    fw.op("pool", lambda e: e.memset(self.mix[:, c, :], 0.0), W=[self.t_mix[c]])
            for c in range(6, 8):
                if self.stage < 4:
                    fw.op("pool", lambda e: e.memset(self.mix[:, c, :], 0.0), W=[self.t_mix[c]])
            if self.stage >= 4:
                self.lru_mixer(path, l, nseg, tseg)
            import os
            if path == "S" and l == 0 and os.environ.get("KDUMP"):
                dbg_d = self.nc.dram_tensor("dbg", [128, 8, NT], F32, kind="ExternalOutput").ap()
                for c in range(8):
                    fw.op("act", lambda e: e.activation(out=self.scr[:, 0:NT], in_=self.mix[:, c, :], func=AF.Copy), R=[self.t_mix[c]], W=[self.t_scr])
                    t_d = Tok()
                    fw.dma("sp", dbg_d[:, c, :], self.scr[:, 0:NT], R=[self.t_scr], W=[t_d])
                    self.outs.append(t_d)
            res = self.scr[:, 0:8 * NT].rearrange("p (c t) -> p c t", t=NT)
            t_res = [self.t_scr]

            def ev_out(fc, half, ps, tp):
                fw.op("act", lambda e: e.activation(out=res[:, fc, half * 512:(half + 1) * 512], in_=ps[:], func=AF.Copy),
                      R=[tp], W=t_res)
            mact = lambda kc, half: self.mix[:, kc, half * 512:(half + 1) * 512]
            self.dense_fm(self.w_out_d[l], D, 0, D, mact, self.t_mix, ev_out)
            self.post_resid(lambda c: res[:, c, :], t_res, 0)
            if self.stage in (3, 4):
                continue
            self.norm_mod(24)
            hid = self.big[:].bitcast(BF16).rearrange("p (c t) -> p c t", t=NT)
            t_hid = self.t_big

            def ev_h(fc, half, ps, tp):
                tmp = self.rtmp[half]
                fw.op("act", lambda e: e.activation(out=tmp, in_=ps[:], func=AF.Relu), R=[tp], W=[self.t_rtmp[half]])
                fw.op("pool", lambda e: e.tensor_tensor(out=hid[:, fc, half * 512:(half + 1) * 512], in0=tmp, in1=tmp,
                                                        op=ALU.mult), R=[self.t_rtmp[half]], W=[t_hid])
            self.dense_fm(self.w1_d[l], D, 0, DFF, hact, [self.t_h], ev_h)
            hidact = lambda kc, half: hid[:, kc, half * 512:(half + 1) * 512]
            self.dense_fm(self.w2_d[l], DFF, 0, D, hidact, [t_hid], ev_out)
            self.post_resid(lambda c: res[:, c, :], t_res, 24)
        t_y = Tok()
        fw.dma("sp", self.y_d[path].rearrange("(c p) t -> p c t", p=128), self.x[:], R=[self.t_x], W=[t_y])
        self.outs.append(t_y)


    def ssd_mixer(self, path, l, nseg, tseg):
        fw = self.fw
        B = self.big
        tb = self.t_big
        U = lambda i, n=1: B[:, i * NT:(i + n) * NT]
        z = U(0, 2).rearrange("p (c t) -> p c t", t=NT)
        raw = U(2, 2).rearrange("p (c t) -> p c t", t=NT)
        xbc = U(4, 6).rearrange("p (c t) -> p c t", t=NT)
        tok = U(10, 4).rearrange("p (i f) -> p i f", f=512)
        ysum = U(14, 2).rearrange("p (c t) -> p c t", t=NT)
        hact = lambda kc, half: self.h[:, kc, half * 512:(half + 1) * 512]
        seg = lambda ap: ap.rearrange("p (s t) -> p s t", t=tseg)
        cw, cb = pvc("ss_conv_w", l), pvc("ss_conv_b", l)
        ident, triF, triB, negF, negB = (self.cst[:, i, :] for i in range(5))

        def ev_z(fc, half, ps, tp):
            fw.op("act", lambda e: e.activation(out=z[:, fc, half * 512:(half + 1) * 512], in_=ps[:], func=AF.Copy), R=[tp], W=[tb])
        self.dense_fm(self.w_in_d[l], D, C_SS * 128, 256, hact, [self.t_h], ev_z)
        for pair in range(3):
            def ev_r(fc, half, ps, tp):
                fw.op("act", lambda e: e.activation(out=raw[:, fc, half * 512:(half + 1) * 512], in_=ps[:], func=AF.Copy), R=[tp], W=[tb])
            self.dense_fm(self.w_in_d[l], D, (C_SS + 2 + 2 * pair) * 128, 256, hact, [self.t_h], ev_r)
            for cc in range(2):
                c = 2 * pair + cc
                o = xbc[:, c, :]
                fw.op("dve", lambda e: e.tensor_scalar(out=o, in0=raw[:, cc, :], scalar1=self.pv[:, cw + 2 * 6 + c:cw + 2 * 6 + c + 1],
                                                       scalar2=self.pv[:, cb + c:cb + c + 1], op0=ALU.mult, op1=ALU.add), R=[tb, self.t_pv], W=[tb])
                for j, sft in ((0, -2), (1, -1), (3, 1)):
                    lo_o, hi_o = max(0, -sft), tseg - max(0, sft)
                    o_ap = seg(o)[:, :, lo_o:hi_o]
                    i_ap = seg(raw[:, cc, :])[:, :, lo_o + sft:hi_o + sft]
                    fw.op("dve", lambda e: e.scalar_tensor_tensor(out=o_ap, in0=i_ap, scalar=self.pv[:, cw + 6 * j + c:cw + 6 * j + c + 1],
                                                                  in1=o_ap, op0=ALU.mult, op1=ALU.add), R=[tb, self.t_pv], W=[tb])
                fw.op("act", lambda e: e.activation(out=raw[:, cc, :], in_=o, func=AF.Sigmoid), R=[tb], W=[tb])
                fw.op("dve", lambda e: e.tensor_tensor(out=o, in0=o, in1=raw[:, cc, :], op=ALU.mult), R=[tb], W=[tb])
        for c in range(2):
            fw.op("act", lambda e: e.activation(out=raw[:, c, :], in_=z[:, c, :], func=AF.Sigmoid), R=[tb], W=[tb])
            fw.op("dve", lambda e: e.tensor_tensor(out=z[:, c, :], in0=z[:, c, :], in1=raw[:, c, :], op=ALU.mult), R=[tb], W=[tb])
        t_w = Tok()
        fw.dma("sp", self.wdt[:], self.wdt_d[l].rearrange("(k p) c -> p k c", p=128), W=[t_w])
        fw.dma("sp", self.ssb[:], self.ssb_d[l], W=[t_w])
        fw.op("pool", lambda e: e.tensor_copy(out=self.wdt_bf[:], in_=self.wdt[:]), R=[t_w], W=[t_w])
        fw.op("act", lambda e: e.activation(out=self.ssb[:, 8:16], in_=self.ssb[:, 8:16], func=AF.Exp), R=[t_w], W=[t_w])
        fw.op("dve", lambda e: e.tensor_scalar(out=self.ssb[:, 8:16], in0=self.ssb[:, 8:16], scalar1=-1.0, scalar2=None, op0=ALU.mult), R=[t_w], W=[t_w])
        sm = self.ss_small
        t_sm = Tok()
        for i in range(8):
            tsl = slice(i * 128, (i + 1) * 128)
            ps, tp = self.psum()
            for q, c in enumerate((0, 1, 2, 3)):
                fw.op("pe", lambda e: e.transpose(ps[:, q * 128:(q + 1) * 128], xbc[:, c, tsl], ident), R=[tb, self.t_const], W=[tp])
            fw.op("act", lambda e: e.activation(out=tok[:, i, :], in_=ps[:], func=AF.Copy), R=[tp], W=[tb])
            ps2, tp2 = self.psum()
            for kc in range(8):
                fw.op("pe", lambda e: e.matmul(ps2[:, 0:8], lhsT=self.h[:, kc, tsl], rhs=self.wdt_bf[:, kc, :], start=(kc == 0), stop=(kc == 7)),
                      R=[self.t_h, t_w], W=[tp2])
            fw.op("dve", lambda e: e.tensor_tensor(out=sm[:, i, 0:8], in0=ps2[:, 0:8], in1=self.ssb[:, 0:8], op=ALU.add), R=[tp2, t_w], W=[t_sm])
            fw.op("act", lambda e: e.activation(out=sm[:, i, 0:8], in_=sm[:, i, 0:8], func=AF.Exp), R=[t_sm], W=[t_sm])
            fw.op("act", lambda e: e.activation(out=sm[:, i, 0:8], in_=sm[:, i, 0:8], func=AF.Ln, bias=self.onec[:, 0:1], scale=1.0), R=[t_sm, self.t_const], W=[t_sm])
            fw.op("dve", lambda e: e.tensor_tensor(out=sm[:, i, 8:16], in0=sm[:, i, 0:8], in1=self.ssb[:, 8:16], op=ALU.mult), R=[t_sm, t_w], W=[t_sm])
            ps3, tp3 = self.psum()
            fw.op("pe", lambda e: e.matmul(ps3[:, 0:4], lhsT=triF, rhs=sm[:, i, 8:12], start=True, stop=True), R=[t_sm, self.t_const], W=[tp3])
            fw.op("pe", lambda e: e.matmul(ps3[:, 4:8], lhsT=triB, rhs=sm[:, i, 12:16], start=True, stop=True), R=[t_sm, self.t_const], W=[tp3])
            fw.op("pe", lambda e: e.matmul(ps3[:, 8:16], lhsT=self.ones_f[:], rhs=sm[:, i, 8:16], start=True, stop=True), R=[t_sm, self.t_const], W=[tp3])
            fw.op("dve", lambda e: e.tensor_copy(out=sm[:, i, 16:32], in_=ps3[:, 0:16]), R=[tp3], W=[t_sm])
            fw.op("dve", lambda e: e.tensor_tensor(out=sm[:, i, 32:40], in0=sm[:, i, 24:32], in1=sm[:, i, 16:24], op=ALU.subtract), R=[t_sm], W=[t_sm])
            fw.op("act", lambda e: e.activation(out=sm[:, i, 32:48], in_=sm[:, i, 32:48] if False else sm[:, i, 32:48], func=AF.Copy), R=[t_sm], W=[t_sm]) if False else None
            fw.op("act", lambda e: e.activation(out=sm[:, i, 32:40], in_=sm[:, i, 32:40], func=AF.Exp), R=[t_sm], W=[t_sm])
            fw.op("act", lambda e: e.activation(out=sm[:, i, 40:48], in_=sm[:, i, 24:32], func=AF.Exp), R=[t_sm], W=[t_sm])
        SC = self.scr
        tmp = [[SC[:, (k * 8 + j) * 128:(k * 8 + j + 1) * 128] for j in range(8)] for k in range(2)]
        t_tmp = [[Tok() for j in range(8)] for k in range(2)]
        cbt = [SC[:, 2048 + k * 256:2048 + (k + 1) * 256].rearrange("p (g t) -> p g t", t=128) for k in range(2)]
        t_cbt = [Tok(), Tok()]
        cnt = 0
        tps = tseg // 128
        for d in range(2):
            neg = negF if d == 0 else negB
            order = list(range(8)) if d == 0 else list(range(7, -1, -1))
            for oi, i in enumerate(order):
                tsl = slice(i * 128, (i + 1) * 128)
                sg = i // tps
                first = (oi % tps == 0)
                last = (oi % tps == tps - 1)
                kcb = (d * 8 + oi) % 2
                psc, tpc = self.psum()
                for g in range(2):
                    fw.op("pe", lambda e: e.matmul(psc[:, g * 128:(g + 1) * 128], lhsT=xbc[:, 2 + g, tsl], rhs=xbc[:, 4 + g, tsl], start=True, stop=True),
                          R=[tb], W=[tpc])
                fw.op("act", lambda e: e.activation(out=cbt[kcb][:].rearrange("p g t -> p (g t)"), in_=psc[:, 0:256], func=AF.Copy), R=[tpc], W=[t_cbt[kcb]])
                for hh in range(4):
                    k = cnt % 2
                    cnt += 1
                    T, tt = tmp[k], t_tmp[k]
                    j = d * 4 + hh
                    g = hh // 2
                    St, tS = self.ss_S[:, j, :], self.t_ssS[j]
                    if first:
                        if path == "P":
                            fw.op("pool", lambda e: e.memset(St, 0.0), W=[tS])
                        else:
                            fw.dma("sp", St, self.st_ss_d[l, d, hh], W=[tS])
                    acol = sm[:, i, 16 + j:17 + j]
                    fw.op("dve", lambda e: e.tensor_scalar(out=T[0], in0=ident, scalar1=acol, scalar2=None, op0=ALU.mult), R=[t_sm, self.t_const], W=[tt[0]])
                    psr, tpr = self.psum()
                    fw.op("pe", lambda e: e.matmul(psr[:, 0:128], lhsT=self.ones_f[:], rhs=T[0], start=True, stop=True), R=[tt[0], self.t_const], W=[tpr])
                    fw.op("dve", lambda e: e.scalar_tensor_tensor(out=T[1], in0=psr[:, 0:128], scalar=acol, in1=neg, op0=ALU.subtract, op1=ALU.add),
                          R=[tpr, t_sm, self.t_const], W=[tt[1]])
                    fw.op("act", lambda e: e.activation(out=T[1], in_=T[1], func=AF.Exp), R=[tt[1]], W=[tt[1]])
                    fw.op("pool", lambda e: e.tensor_tensor(out=T[1], in0=T[1], in1=cbt[kcb][:, g, :], op=ALU.mult), R=[tt[1], t_cbt[kcb]], W=[tt[1]])
                    fw.op("act", lambda e: e.activation(out=T[2], in_=psr[:, 0:128], func=AF.Exp), R=[tpr], W=[tt[2]])
                    fw.op("pool", lambda e: e.tensor_tensor(out=T[2], in0=T[2], in1=xbc[:, 4 + g, tsl], op=ALU.mult), R=[tt[2], tb], W=[tt[2]])
                    xs_h = tok[:, i, hh * 64:(hh + 1) * 64]
                    xdt = T[3][:, 0:64]
                    xdd = T[3][:, 64:128]
                    fw.op("dve", lambda e: e.tensor_scalar(out=xdt, in0=xs_h, scalar1=sm[:, i, j:j + 1], scalar2=None, op0=ALU.mult), R=[tb, t_sm], W=[tt[3]])
                    fw.op("dve", lambda e: e.tensor_scalar(out=xdd, in0=xdt, scalar1=sm[:, i, 32 + j:33 + j], scalar2=None, op0=ALU.mult), R=[tt[3], t_sm], W=[tt[3]])
                    psy, tpy = self.psum()
                    fw.op("pe", lambda e: e.matmul(psy[0:64, 0:128], lhsT=xdt, rhs=T[1], start=True, stop=False), R=[tt[3], tt[1]], W=[tpy])
                    fw.op("pe", lambda e: e.matmul(psy[0:64, 0:128], lhsT=St, rhs=T[2], start=False, stop=True), R=[tS, tt[2]], W=[tpy])
                    ydst = ysum[:, hh // 2, tsl]
                    if hh % 2 == 0:
                        if d == 0:
                            fw.op("act", lambda e: e.activation(out=ydst[0:64, :], in_=psy[0:64, 0:128], func=AF.Copy), R=[tpy], W=[tb])
                        else:
                            fw.op("dve", lambda e: e.tensor_tensor(out=ydst[0:64, :], in0=ydst[0:64, :], in1=psy[0:64, 0:128], op=ALU.add), R=[tpy, tb], W=[tb])
                    else:
                        psy2, tpy2 = self.psum()
                        fw.op("pe", lambda e: e.matmul(psy2[64:128, 0:128], lhsT=xdt, rhs=T[1], start=True, stop=False), R=[tt[3], tt[1]], W=[tpy2])
                        fw.op("pe", lambda e: e.matmul(psy2[64:128, 0:128], lhsT=St, rhs=T[2], start=False, stop=True), R=[tS, tt[2]], W=[tpy2])
                        if d == 0:
                            fw.op("act", lambda e: e.activation(out=ydst[64:128, :], in_=psy2[64:128, 0:128], func=AF.Copy), R=[tpy2], W=[tb])
                        else:
                            fw.op("dve", lambda e: e.tensor_tensor(out=ydst[64:128, :], in0=ydst[64:128, :], in1=psy2[64:128, 0:128], op=ALU.add), R=[tpy2, tb], W=[tb])
                    pss, tpss = self.psum()
                    fw.op("pe", lambda e: e.matmul(pss[:, 0:64], lhsT=tok[:, i, 256 + g * 128:256 + (g + 1) * 128], rhs=xdd, start=True, stop=True),
                          R=[tb, tt[3]], W=[tpss])
                    fw.op("dve", lambda e: e.scalar_tensor_tensor(out=St, in0=St, scalar=sm[:, i, 40 + j:41 + j], in1=pss[:, 0:64], op0=ALU.mult, op1=ALU.add),
                          R=[tpss, t_sm], W=[tS])
                    if last and path == "P":
                        t_o = Tok()
                        fw.dma("sp", self.o_ss_d[sg, l, d, hh], St, R=[tS], W=[t_o])
                        self.outs.append(t_o)
        dcol, ncol = pvc("ss_D", l), pvc("ss_norm", l)
        for c in range(2):
            fw.op("dve", lambda e: e.scalar_tensor_tensor(out=ysum[:, c, :], in0=xbc[:, c, :], scalar=self.pv[:, dcol + c:dcol + c + 1], in1=ysum[:, c, :],
                                                          op0=ALU.mult, op1=ALU.add), R=[tb, self.t_pv], W=[tb])
            fw.op("dve", lambda e: e.tensor_tensor(out=ysum[:, c, :], in0=ysum[:, c, :], in1=z[:, c, :], op=ALU.mult), R=[tb], W=[tb])
        self.rms_stats(lambda c: ysum[:, c, :], [tb], nch=2, dim=256)
        for c in range(2):
            fw.op("dve", lambda e: e.tensor_tensor(out=ysum[:, c, :], in0=ysum[:, c, :], in1=self.rstd[:], op=ALU.mult), R=[tb, self.t_rstd], W=[tb])
            fw.op("act", lambda e: e.activation(out=self.mix[:, 4 + c, :], in_=ysum[:, c, :], func=AF.Copy, scale=self.pv[:, ncol + c:ncol + c + 1]),
                  R=[tb, self.t_pv], W=[self.t_mix[4 + c]])


    def zero_rows(self):
        fw = self.fw
        zt = self.big[:, 0:2048]
        fw.op("pool", lambda e: e.memset(zt, 0.0), W=[self.t_big])
        self.t_RAd = Tok()
        for m in ("hg", "rw"):
            flat = self.RAd[m].rearrange("a r t f -> (a r t f)").rearrange("(n p x) -> n p x", p=128, x=2048)
            for i in range(flat.shape[0]):
                fw.dma("sp", flat[i], zt, R=[self.t_big], W=[self.t_RAd])

    def hg_bounds(self):
        fw = self.fw
        t = self.t_hgl
        ex = self.nc.alloc_sbuf_tensor("hg_ex", [128, DEPTH + 2, 4], F32)
        col = lambda l: self.pv[:, pvc("hg_lb", l):pvc("hg_lb", l) + 4]
        mx, sm = ex[:, DEPTH, :], ex[:, DEPTH + 1, :]
        fw.op("dve", lambda e: e.tensor_tensor(out=mx, in0=col(0), in1=col(1), op=ALU.max), R=[self.t_pv], W=[t])
        for l in range(2, DEPTH):
            fw.op("dve", lambda e: e.tensor_tensor(out=mx, in0=mx, in1=col(l), op=ALU.max), R=[self.t_pv, t], W=[t])
        for l in range(DEPTH):
            fw.op("dve", lambda e: e.tensor_tensor(out=ex[:, l, :], in0=col(l), in1=mx, op=ALU.subtract), R=[self.t_pv, t], W=[t])
            fw.op("act", lambda e: e.activation(out=ex[:, l, :], in_=ex[:, l, :], func=AF.Exp), R=[t], W=[t])
        fw.op("dve", lambda e: e.tensor_tensor(out=sm, in0=ex[:, 0, :], in1=ex[:, 1, :], op=ALU.add), R=[t], W=[t])
        for l in range(2, DEPTH):
            fw.op("dve", lambda e: e.tensor_tensor(out=sm, in0=sm, in1=ex[:, l, :], op=ALU.add), R=[t], W=[t])
        fw.op("dve", lambda e: e.reciprocal(out=sm, in_=sm), R=[t], W=[t])
        fw.op("pool", lambda e: e.memset(self.hgl[:, 0, 0:4], 0.0), W=[t])
        for l in range(1, DEPTH):
            fw.op("dve", lambda e: e.tensor_tensor(out=ex[:, l, :], in0=ex[:, l, :], in1=sm, op=ALU.mult), R=[t], W=[t])
            fw.op("dve", lambda e: e.tensor_tensor(out=self.hgl[:, l, 0:4], in0=self.hgl[:, l - 1, 0:4], in1=ex[:, l, :], op=ALU.add), R=[t], W=[t])
        for l in range(DEPTH):
            fw.op("dve", lambda e: e.tensor_scalar(out=self.hgl[:, l, 4:8], in0=self.hgl[:, l, 0:4], scalar1=-1.0, scalar2=1.0, op0=ALU.mult, op1=ALU.add),
                  R=[t], W=[t])

    def to_rows(self, src_fm, t_src, i, dsts):
        fw = self.fw
        ident = self.cst[:, 0, :]
        ps, tp = self.psum()
        fw.op("pe", lambda e: e.transpose(ps[:, 0:128], src_fm[:, i * 128:(i + 1) * 128], ident), R=t_src + [self.t_const], W=[tp])
        k = self.ntt % 2
        self.ntt += 1
        tt, ttok = self.ttile[k], self.t_ttile[k]
        fw.op("act", lambda e: e.activation(out=tt[:], in_=ps[:, 0:128], func=AF.Copy), R=[tp], W=[ttok])
        for dst, half in dsts:
            fw.dma("pool", dst, tt[:, half * 64:(half + 1) * 64], R=[ttok], W=[self.t_RAd])

    def scan_group(self, T, pairs, Tc=4):
        fw = self.fw
        npair = len(pairs)
        assert npair <= 4
        SC = self.scr
        import os
        KSTEP = int(os.environ.get('KSTEP', '3'))
        nch = min(T // Tc, int(os.environ.get('KNCH', '100000')))
        for pi, p in enumerate(pairs):
            p["RA"] = [SC[0:6, (pi * 2 + b) * Tc * 128:(pi * 2 + b + 1) * Tc * 128].rearrange("p (t f) -> p t f", f=128) for b in range(2)]
            base = npair * 2 * Tc * 128
            p["RB"] = [SC[0:6, base + (pi * 2 + b) * Tc * 64:base + (pi * 2 + b + 1) * Tc * 64].rearrange("p (t f) -> p t f", f=64) for b in range(2)]
            base2 = base + npair * 2 * Tc * 64
            p["RV"] = [SC[0:2, base2 + (pi * 2 + b) * Tc * 64:base2 + (pi * 2 + b + 1) * Tc * 64].rearrange("p (t f) -> p t f", f=64) for b in range(2)]
            p["tRA"] = [Tok(), Tok()]
            p["tRB"] = [Tok(), Tok()]
            p["tRV"] = [Tok(), Tok()]
            p["S"] = self.sc_S[:, pi, :]
            p["tS"] = self.t_scS[pi]
            if not p["init"]:
                pass
            elif p["S0"] is None:
                fw.op("pool", lambda e: e.memset(p["S"], 0.0), W=[p["tS"]])
            else:
                fw.dma("sp", p["S"], p["S0"], W=[p["tS"]])

        def tokbase(p, c):
            return c * Tc if p["d"] == 0 else T - (c + 1) * Tc

        def load(p, c):
            if KSTEP == -1:
                return
            b = c % 2
            tb0 = tokbase(p, c)
            if p.get("mode") == "hg":
                fw.dma("pool", p["RA"][b][0:2], p["RAd"][4:6, tb0:tb0 + Tc, :], R=[self.t_RAd, p["tZ"]], W=[p["tRA"][b]])
                fw.dma("pool", p["RV"][b], p["RBd"][:, tb0:tb0 + Tc, :], R=[self.t_RAd, p["tZ"]], W=[p["tRV"][b]])
            else:
                fw.dma("pool", p["RA"][b], p["RAd"][:, tb0:tb0 + Tc, :], R=[self.t_RAd, p["tZ"]], W=[p["tRA"][b]])
                fw.dma("pool", p["RB"][b][4:6], p["RBd"][:, tb0:tb0 + Tc, :], R=[self.t_RAd, p["tZ"]], W=[p["tRB"][b]])
        for p in pairs:
            load(p, 0)
        for c in range(nch):
            b = c % 2
            if c + 1 < nch:
                for p in pairs:
                    load(p, c + 1)
            for j in range(Tc):
                i = c * Tc + j
                for pi, p in enumerate(pairs):
                    t = i if p["d"] == 0 else T - 1 - i
                    p["t"] = t
                    p["jb"] = t - tokbase(p, c)
                    if "b" in os.environ.get("KT", ""):
                        continue
                    fw.op("pe", lambda e: e.matmul(self.ps[pi][0:4, 0:64], lhsT=p["Z"][:, t + 1, :], rhs=p["S"], start=True, stop=True),
                          R=[p["tZ"], p["tS"]], W=[self.t_ps1[pi]])
                for pi, p in enumerate(pairs):
                    if KSTEP < 1:
                        continue
                    fw.op("act", lambda e: e.activation(out=p["RB"][b][0:4, p["jb"], :], in_=self.ps[pi][0:4, 0:64], func=AF.Copy),
                          R=[self.t_ps1[pi]], W=[p["tRB"][b]])
                for pi, p in enumerate(pairs):
                    q = i % 2
                    ps2 = self.ps[pi][:, 64 + 64 * q:128 + 64 * q]
                    if p.get("mode") == "hg":
                        fw.op("pe", lambda e: e.matmul(ps2, lhsT=p["RA"][b][0:2, p["jb"], :], rhs=p["RV"][b][0:2, p["jb"], :],
                                                       start=True, stop=True), R=[p["tRA"][b], p["tRV"][b]], W=[self.t_ps2[pi][q]])
                    else:
                        fw.op("pe", lambda e: e.matmul(ps2, lhsT=p["RA"][b][0:6, p["jb"], :], rhs=p["RB"][b][0:6, p["jb"], :],
                                                       start=True, stop=True), R=[p["tRA"][b], p["tRB"][b]], W=[self.t_ps2[pi][q]])
                for pi, p in enumerate(pairs):
                    t = p["t"]
                    q = i % 2
                    ps2 = self.ps[pi][:, 64 + 64 * q:128 + 64 * q]
                    fw.op("dve", lambda e: e.scalar_tensor_tensor(out=p["S"], in0=p["S"], scalar=p["W"][:, t:t + 1], in1=ps2,
                                                                  op0=ALU.mult, op1=ALU.add), R=[self.t_ps2[pi][q], p["tW"]], W=[p["tS"]])
            for p in pairs:
                tb0 = tokbase(p, c)
                if KSTEP in (-1, -2):
                    continue
                fw.dma("sp", p["Yd"][:, 1 + tb0:1 + tb0 + Tc, :], p["RB"][b][0:4], R=[p["tRB"][b]], W=[p["tY"]])
        KT = os.environ.get("KT", "")
        for pi, p in enumerate(pairs):
            u = T + 1 if p["d"] == 0 else 0
            if "a" in KT or not p["final"]:
                continue
            fw.op("pe", lambda e: e.matmul(self.ps[pi][0:4, 0:64], lhsT=p["Z"][:, u, :], rhs=p["S"], start=True, stop=True),
                  R=[p["tZ"], p["tS"]], W=[self.t_ps1[pi]])
            if "e" in KT:
                continue
            fw.op("act", lambda e: e.activation(out=self.sc_x[0:4, pi, :], in_=self.ps[pi][0:4, 0:64], func=AF.Copy),
                  R=[self.t_ps1[pi]], W=[self.t_scx[pi]])
            if "c" not in KT:
                fw.dma("sp", p["Yd"][:, u, :], self.sc_x[0:4, pi, :], R=[self.t_scx[pi]], W=[p["tY"]])
            if p["out"] is not None and "d" not in KT:
                t_o = Tok()
                fw.dma("sp", p["out"], p["S"], R=[p["tS"]], W=[t_o])
                self.outs.append(t_o)

    def build_Z(self, Z, tZ, T, d, kk_fm, r_fm, t_src, tok0, has_prev=False, has_next=False):
        fw = self.fw
        for half in range(2):
            pr = slice(half * 64, (half + 1) * 64)
            cr = 2 * half + 1
            if kk_fm is not None:
                fw.op("pool", lambda e: e.tensor_copy(out=Z[pr, 1:1 + T, 2 * half], in_=kk_fm[pr, tok0:tok0 + T]), R=t_src, W=[tZ])
            if d == 0:
                if has_prev:
                    fw.op("pool", lambda e: e.tensor_copy(out=Z[pr, 1:2 + T, cr], in_=r_fm[pr, tok0 - 1:tok0 + T]), R=t_src, W=[tZ])
                else:
                    fw.op("pool", lambda e: e.tensor_copy(out=Z[pr, 2:2 + T, cr], in_=r_fm[pr, tok0:tok0 + T]), R=t_src, W=[tZ])
                    fw.op("pool", lambda e: e.memset(Z[pr, 1:2, cr], 0.0), W=[tZ])
            else:
                if has_next:
                    fw.op("pool", lambda e: e.tensor_copy(out=Z[pr, 0:T + 1, cr], in_=r_fm[pr, tok0:tok0 + T + 1]), R=t_src, W=[tZ])
                else:
                    fw.op("pool", lambda e: e.tensor_copy(out=Z[pr, 0:T, cr], in_=r_fm[pr, tok0:tok0 + T]), R=t_src, W=[tZ])
                    fw.op("pool", lambda e: e.memset(Z[pr, T:T + 1, cr], 0.0), W=[tZ])

    def read_y(self, T, tok0, hp, dst_fm, t_dst, y0=0):
        fw = self.fw
        ident = self.cst[:, 0, :]
        for i in range(T // 128):
            k = self.ntt % 2
            self.ntt += 1
            yt, tyt = self.ytile[k], self.t_ytile[k]
            for d in range(2):
                off = 2 if d == 0 else 0
                for hh in range(2):
                    fw.dma("sp", yt[:, d, hh * 64:(hh + 1) * 64], self.Yd[hp * 2 + d, 1 + 2 * hh, y0 + off + i * 128: y0 + off + (i + 1) * 128, :],
                           R=[self.t_Y[hp * 2 + d]], W=[tyt])
            fw.op("dve", lambda e: e.tensor_tensor(out=yt[:, 0, :], in0=yt[:, 0, :], in1=yt[:, 1, :], op=ALU.add), R=[tyt], W=[tyt])
            ps, tp = self.psum()
            fw.op("pe", lambda e: e.transpose(ps[:, 0:128], yt[:, 0, :], ident), R=[tyt, self.t_const], W=[tp])
            fw.op("act", lambda e: e.activation(out=dst_fm[:, tok0 + i * 128: tok0 + (i + 1) * 128], in_=ps[:, 0:128], func=AF.Copy), R=[tp], W=t_dst)


    def rw_mixer(self, path, l, nseg, tseg):
        fw = self.fw
        B, SC = self.big, self.scr
        tb = self.t_big
        big_bf, scr_bf = B[:].bitcast(BF16), SC[:].bitcast(BF16)
        ch2 = lambda ap: ap.rearrange("p (c t) -> p c t", t=NT)
        UB = lambda i, n=1: B[:, i * NT:(i + n) * NT]
        US = lambda i, n=1: SC[:, i * NT:(i + n) * NT]
        nkk, w = ch2(UB(0, 2)), ch2(UB(2, 4))
        Zreg = B[:, 6 * NT:6 * NT + 4128]
        r = ch2(UB(11, 2))
        G = ch2(big_bf[:, 26624:28672])
        BG = ch2(big_bf[:, 28672:30720])
        free = [UB(i) for i in range(6, 10)] + [US(i) for i in range(8)]
        if path == "S":
            shifts = [(-1, 64, 0), (1, 64, 1), (-64, NT, 2), (64, NT, 3)]
            mbase, ndir = pvc("mS"), 4
        else:
            shifts = [(-1, 256, 0), (1, 256, 1)]
            mbase, ndir = pvc("mP"), 2
        alloc = lambda: free.pop()
        ident, bones = self.cst[:, 0, :], self.cst[:, 5, :]
        hact = lambda kc, half: self.h[:, kc, half * 512:(half + 1) * 512]
        mu = pvc("rw_mu", l)
        tc_ = self.t_rwc
        fw.op("dve", lambda e: e.tensor_scalar(out=self.rwc[:, 0:8], in0=self.pv[:, mu:mu + 8], scalar1=-1.0, scalar2=1.0, op0=ALU.mult, op1=ALU.add),
              R=[self.t_pv], W=[tc_])
        for k in range(ndir):
            fw.op("dve", lambda e: e.tensor_tensor(out=self.rwc[:, 8 + 8 * k:16 + 8 * k], in0=self.pv[:, mu:mu + 8], in1=self.pv[:, mbase + 8 * k:mbase + 8 * k + 8],
                                                   op=ALU.mult), R=[self.t_pv], W=[tc_])
        ka_c, kk_c = pvc("rw_ka", l), pvc("rw_kk", l)
        fw.op("dve", lambda e: e.tensor_scalar(out=self.rwc[:, 50:52], in0=self.pv[:, ka_c:ka_c + 2], scalar1=-1.0, scalar2=1.0, op0=ALU.mult, op1=ALU.add),
              R=[self.t_pv], W=[tc_])
        raw0, raw1 = alloc(), alloc()
        raws = [raw0, raw1]

        def proj_mix(cbase, dsts):
            def ev(fc, half, ps, tp):
                fw.op("act", lambda e: e.activation(out=raws[fc][:, half * 512:(half + 1) * 512], in_=ps[:], func=AF.Copy), R=[tp], W=[tb])
            self.dense_fm(self.w_in_d[l], D, (C_RW + cbase) * 128, 256, hact, [self.t_h], ev)
            for cc in range(2):
                c = cbase + cc
                o, p_ = dsts[cc], raws[cc]
                fw.op("dve", lambda e: e.tensor_scalar(out=o, in0=p_, scalar1=self.rwc[:, c:c + 1], scalar2=None, op0=ALU.mult), R=[tb, tc_], W=[tb])
                for sft, sl, k in shifts:
                    sgm = lambda ap: ap.rearrange("p (s t) -> p s t", t=sl)
                    lo_o, hi_o = max(0, -sft), sl - max(0, sft)
                    o_ap = sgm(o)[:, :, lo_o:hi_o]
                    i_ap = sgm(p_)[:, :, lo_o + sft:hi_o + sft]
                    fw.op("dve", lambda e: e.scalar_tensor_tensor(out=o_ap, in0=i_ap, scalar=self.rwc[:, 8 + 8 * k + c:9 + 8 * k + c], in1=o_ap,
                                                                  op0=ALU.mult, op1=ALU.add), R=[tb, tc_], W=[tb])
        LA, LB = alloc(), alloc()
        proj_mix(6, [LA, LB])
        lwA, lwB = alloc(), alloc()
        lw = [lwA[:, 0:256], lwA[:, 256:512], lwA[:, 512:768], lwB[:, 0:256], lwB[:, 256:512]]
        for i_ in range(5):
            fw.dma("sp", lw[i_], self.rwlw_d[l, i_], W=[tb])
        tmp1 = alloc()
        w0c = pvc("rw_w0", l)
        fw.op("act", lambda e: e.activation(out=tmp1, in_=LA, func=AF.Tanh), R=[tb], W=[tb])
        for d in range(2):
            for hp in range(2):
                for half in range(2):
                    sl_ = slice(half * 512, (half + 1) * 512)
                    ps, tp = self.psum()
                    fw.op("pe", lambda e: e.matmul(ps[:], lhsT=lw[d][:, hp * 128:(hp + 1) * 128], rhs=tmp1[:, sl_], start=True, stop=True), R=[tb], W=[tp])
                    dst = w[:, d * 2 + hp, sl_]
                    fw.op("act", lambda e: e.activation(out=dst, in_=ps[:], func=AF.Sigmoid, bias=self.pv[:, w0c + d * 2 + hp:w0c + d * 2 + hp + 1], scale=1.0),
                          R=[tp, self.t_pv], W=[tb])
                    fw.op("act", lambda e: e.activation(out=dst, in_=dst, func=AF.Exp, scale=-0.6065306597126334), R=[tb], W=[tb])
        fw.op("act", lambda e: e.activation(out=tmp1, in_=LA, func=AF.Sigmoid), R=[tb], W=[tb])
        for hp in range(2):
            for half in range(2):
                sl_ = slice(half * 512, (half + 1) * 512)
                ps, tp = self.psum()
                fw.op("pe", lambda e: e.matmul(ps[:], lhsT=lw[2][:, hp * 128:(hp + 1) * 128], rhs=tmp1[:, sl_], start=True, stop=True), R=[tb], W=[tp])
                fw.op("act", lambda e: e.activation(out=G[:, hp, sl_], in_=ps[:], func=AF.Copy), R=[tp], W=[tb])
        free.append(LA)
        K0, K1 = alloc(), alloc()
        Kc = [K0, K1]
        proj_mix(2, Kc)
        tmp2 = alloc()
        for hp in range(2):
            fw.op("dve", lambda e: e.tensor_scalar(out=tmp1, in0=Kc[hp], scalar1=self.pv[:, kk_c + hp:kk_c + hp + 1], scalar2=None, op0=ALU.mult), R=[tb, self.t_pv], W=[tb])
            fw.op("dve", lambda e: e.tensor_tensor(out=tmp2, in0=tmp1, in1=tmp1, op=ALU.mult), R=[tb], W=[tb])
            for half in range(2):
                sl_ = slice(half * 512, (half + 1) * 512)
                ps, tp = self.psum()
                fw.op("pe", lambda e: e.matmul(ps[:], lhsT=bones, rhs=tmp2[:, sl_], start=True, stop=True), R=[tb, self.t_const], W=[tp])
                fw.op("act", lambda e: e.activation(out=nkk[:, hp, sl_], in_=ps[:], func=AF.Sqrt), R=[tp], W=[tb])
            fw.op("dve", lambda e: e.tensor_scalar(out=nkk[:, hp, :], in0=nkk[:, hp, :], scalar1=1e-12, scalar2=None, op0=ALU.max), R=[tb], W=[tb])
            fw.op("dve", lambda e: e.reciprocal(out=nkk[:, hp, :], in_=nkk[:, hp, :]), R=[tb], W=[tb])
            fw.op("dve", lambda e: e.scalar_tensor_tensor(out=nkk[:, hp, :], in0=nkk[:, hp, :], scalar=-1.0, in1=tmp1, op0=ALU.mult, op1=ALU.mult), R=[tb], W=[tb])
        a0c = pvc("rw_a0", l)
        ks0, ks1 = alloc(), alloc()
        ksum = [ks0, ks1]
        for d in range(2):
            for hp in range(2):
                for half in range(2):
                    sl_ = slice(half * 512, (half + 1) * 512)
                    ps, tp = self.psum()
                    fw.op("pe", lambda e: e.matmul(ps[:], lhsT=lw[3 + d][:, hp * 128:(hp + 1) * 128], rhs=LB[:, sl_], start=True, stop=True), R=[tb], W=[tp])
                    fw.op("act", lambda e: e.activation(out=tmp1[:, sl_], in_=ps[:], func=AF.Sigmoid, bias=self.pv[:, a0c + d * 2 + hp:a0c + d * 2 + hp + 1], scale=1.0),
                          R=[tp, self.t_pv], W=[tb])
                fw.op("dve", lambda e: e.scalar_tensor_tensor(out=tmp2, in0=nkk[:, hp, :], scalar=-1.0, in1=tmp1, op0=ALU.mult, op1=ALU.mult), R=[tb], W=[tb])
                for i in range(8):
                    self.to_rows(tmp2, [tb], i, [(self.RAd["rw"][hp * 2 + d, 2 * hh, i * 128:(i + 1) * 128, hh * 64:(hh + 1) * 64], hh) for hh in range(2)])
                fw.op("dve", lambda e: e.tensor_scalar(out=tmp1, in0=tmp1, scalar1=self.pv[:, ka_c + hp:ka_c + hp + 1], scalar2=self.rwc[:, 50 + hp:51 + hp],
                                                       op0=ALU.mult, op1=ALU.add), R=[tb, self.t_pv, tc_], W=[tb])
                fw.op("dve", lambda e: e.tensor_tensor(out=tmp1, in0=tmp1, in1=Kc[hp], op=ALU.mult), R=[tb], W=[tb])
                for i in range(8):
                    self.to_rows(tmp1, [tb], i, [(self.RAd["rw"][hp * 2 + d, 4 + hh, i * 128:(i + 1) * 128, hh * 64:(hh + 1) * 64], hh) for hh in range(2)])
                if d == 0:
                    fw.op("pool", lambda e: e.tensor_copy(out=ksum[hp], in_=tmp1), R=[tb], W=[tb])
                else:
                    fw.op("pool", lambda e: e.tensor_tensor(out=ksum[hp], in0=ksum[hp], in1=tmp1, op=ALU.add), R=[tb], W=[tb])
        free.extend([LB, K0, K1])
        proj_mix(0, [r[:, 0, :], r[:, 1, :]])
        rkc = pvc("rw_rk", l)
        for hp in range(2):
            fw.op("dve", lambda e: e.scalar_tensor_tensor(out=tmp1, in0=r[:, hp, :], scalar=self.pv[:, rkc + hp:rkc + hp + 1], in1=ksum[hp], op0=ALU.mult, op1=ALU.mult),
                  R=[tb, self.t_pv], W=[tb])
            for half in range(2):
                sl_ = slice(half * 512, (half + 1) * 512)
                ps, tp = self.psum()
                fw.op("pe", lambda e: e.matmul(ps[:], lhsT=bones, rhs=tmp1[:, sl_], start=True, stop=True), R=[tb, self.t_const], W=[tp])
                fw.op("act", lambda e: e.activation(out=ksum[hp][:, sl_], in_=ps[:], func=AF.Copy), R=[tp], W=[tb])
        V0, V1 = alloc(), alloc()
        Vc = [V0, V1]
        proj_mix(4, Vc)
        for hp in range(2):
            for i in range(8):
                self.to_rows(Vc[hp], [tb], i, [(self.RBd[hp, hh, i * 128:(i + 1) * 128, :], hh) for hh in range(2)])
            fw.op("dve", lambda e: e.tensor_tensor(out=tmp1, in0=ksum[hp], in1=Vc[hp], op=ALU.mult), R=[tb], W=[tb])
            fw.op("dve", lambda e: e.tensor_tensor(out=BG[:, hp, :], in0=tmp1, in1=G[:, hp, :], op=ALU.mult), R=[tb], W=[tb])
        self.run_scans("rw", path, l, nseg, tseg, Zreg, nkk, r, w, [tb], self.st_rw_d, self.o_rw_d)
        y = ch2(UB(6, 2))
        ycn = UB(8)
        t2 = UB(9)
        lnw, lnb = pvc("rw_ln_w", l), pvc("rw_ln_b", l)
        for hp in range(2):
            for s_ in range(nseg):
                self.read_y(tseg, s_ * tseg, hp, y[:, hp, :], [tb], y0=s_ * (tseg + 2))
            for half in range(2):
                sl_ = slice(half * 512, (half + 1) * 512)
                ps, tp = self.psum()
                fw.op("pe", lambda e: e.matmul(ps[:], lhsT=bones, rhs=y[:, hp, sl_], start=True, stop=True), R=[tb, self.t_const], W=[tp])
                fw.op("dve", lambda e: e.scalar_tensor_tensor(out=ycn[:, sl_], in0=ps[:], scalar=-1.0 / 64, in1=y[:, hp, sl_], op0=ALU.mult, op1=ALU.add),
                      R=[tp, tb], W=[tb])
            fw.op("dve", lambda e: e.tensor_tensor(out=t2, in0=ycn, in1=ycn, op=ALU.mult), R=[tb], W=[tb])
            for half in range(2):
                sl_ = slice(half * 512, (half + 1) * 512)
                ps, tp = self.psum()
                fw.op("pe", lambda e: e.matmul(ps[:], lhsT=bones, rhs=t2[:, sl_], start=True, stop=True), R=[tb, self.t_const], W=[tp])
                fw.op("act", lambda e: e.activation(out=self.rstd[:, sl_], in_=ps[:], func=AF.Sqrt, bias=self.epsc[:, 1:2], scale=1.0 / 64),
                      R=[tp, self.t_const], W=[self.t_rstd])
            fw.op("dve", lambda e: e.reciprocal(out=self.rstd[:], in_=self.rstd[:]), R=[self.t_rstd], W=[self.t_rstd])
            fw.op("dve", lambda e: e.tensor_tensor(out=ycn, in0=ycn, in1=self.rstd[:], op=ALU.mult), R=[tb, self.t_rstd], W=[tb])
            fw.op("act", lambda e: e.activation(out=ycn, in_=ycn, func=AF.Identity, bias=self.pv[:, lnb + hp:lnb + hp + 1], scale=self.pv[:, lnw + hp:lnw + hp + 1]),
                  R=[tb, self.t_pv], W=[tb])
            fw.op("dve", lambda e: e.tensor_tensor(out=ycn, in0=ycn, in1=G[:, hp, :], op=ALU.mult), R=[tb], W=[tb])
            fw.op("dve", lambda e: e.tensor_tensor(out=self.mix[:, hp, :], in0=ycn, in1=BG[:, hp, :], op=ALU.add), R=[tb], W=[self.t_mix[hp]])

    def hg_mixer(self, path, l, nseg, tseg):
        fw = self.fw
        B = self.big
        tb = self.t_big
        U = lambda i, n=1: B[:, i * NT:(i + n) * NT]
        ch2 = lambda ap: ap.rearrange("p (c t) -> p c t", t=NT)
        r, f = ch2(U(0, 2)), U(2, 4).rearrange("p (c t) -> p c t", t=NT)
        sg = ch2(U(11, 2))
        v = ch2(U(6, 2))
        kd = ch2(U(8, 2))
        raw = ch2(U(13, 2))
        Zreg = B[:, 6 * NT:6 * NT + 4128]
        hact = lambda kc, half: self.h[:, kc, half * 512:(half + 1) * 512]

        def proj(cbase, dst, fn=AF.Copy):
            def ev(fc, half, ps, tp):
                fw.op("act", lambda e: e.activation(out=dst[:, fc, half * 512:(half + 1) * 512], in_=ps[:], func=fn), R=[tp], W=[tb])
            self.dense_fm(self.w_in_d[l], D, (C_HG + cbase) * 128, 256, hact, [self.t_h], ev)
        proj(0, raw)
        for c in range(2):
            fw.op("act", lambda e: e.activation(out=r[:, c, :], in_=raw[:, c, :], func=AF.Sigmoid), R=[tb], W=[tb])
            fw.op("dve", lambda e: e.tensor_tensor(out=r[:, c, :], in0=r[:, c, :], in1=raw[:, c, :], op=ALU.mult), R=[tb], W=[tb])
        proj(8, raw)
        for c in range(2):
            fw.op("act", lambda e: e.activation(out=sg[:, c, :], in_=raw[:, c, :], func=AF.Sigmoid), R=[tb], W=[tb])
            fw.op("dve", lambda e: e.tensor_tensor(out=sg[:, c, :], in0=sg[:, c, :], in1=raw[:, c, :], op=ALU.mult), R=[tb], W=[tb])
        proj(2, v)
        for hp in range(2):
            for i in range(8):
                self.to_rows(v[:, hp, :], [tb], i, [(self.RBd[hp, hh, i * 128:(i + 1) * 128, :], hh) for hh in range(2)])
        for d in range(2):
            proj(4 + 2 * d, raw, AF.Sigmoid)
            for hp in range(2):
                j = d * 2 + hp
                fw.op("dve", lambda e: e.tensor_scalar(out=f[:, j, :], in0=raw[:, hp, :], scalar1=self.hgl[:, l, 4 + j:5 + j], scalar2=self.hgl[:, l, j:j + 1],
                                                       op0=ALU.mult, op1=ALU.add), R=[tb, self.t_hgl], W=[tb])
                fw.op("dve", lambda e: e.tensor_scalar(out=kd[:, hp, :], in0=f[:, j, :], scalar1=-1.0, scalar2=1.0, op0=ALU.mult, op1=ALU.add), R=[tb], W=[tb])
                for i in range(8):
                    self.to_rows(kd[:, hp, :], [tb], i, [(self.RAd["hg"][hp * 2 + d, 4 + hh, i * 128:(i + 1) * 128, hh * 64:(hh + 1) * 64], hh) for hh in range(2)])
        self.run_scans("hg", path, l, nseg, tseg, Zreg, None, r, f, [tb], self.st_hg_d, self.o_hg_d)
        y = ch2(U(6, 2))
        for hp in range(2):
            for s_ in range(nseg):
                self.read_y(tseg, s_ * tseg, hp, y[:, hp, :], [tb], y0=s_ * (tseg + 2))
        self.rms_stats(lambda c: y[:, c, :], [tb], nch=2, dim=256)
        ncol = pvc("hg_norm", l)
        for c in range(2):
            fw.op("dve", lambda e: e.tensor_tensor(out=y[:, c, :], in0=y[:, c, :], in1=self.rstd[:], op=ALU.mult), R=[tb, self.t_rstd], W=[tb])
            fw.op("dve", lambda e: e.scalar_tensor_tensor(out=self.mix[:, 2 + c, :], in0=y[:, c, :], scalar=self.pv[:, ncol + c:ncol + c + 1], in1=sg[:, c, :],
                                                          op0=ALU.mult, op1=ALU.mult), R=[tb, self.t_pv], W=[self.t_mix[2 + c]])

    def run_scans(self, mname, path, l, nseg, tseg, Zreg, nkk, r, wdec, t_src, st_d, out_d):
        fw = self.fw
        tZ = self.t_big
        TB = 256
        fw.op("pool", lambda e: e.memset(Zreg, 0.0), R=t_src, W=[tZ])
        self.t_Y = [Tok() for _ in range(4)]
        nblk = NT // TB
        for g in range(nblk):
            pairs = []
            zi = 0
            for hp in range(2):
                for d in range(2):
                    if path == "P":
                        tok0, y0 = g * TB, g * (TB + 2)
                        hprev = hnext = False
                        init = final = True
                        S0, out = None, out_d[g, l, d, hp]
                    else:
                        tok0 = g * TB if d == 0 else (nblk - 1 - g) * TB
                        y0 = tok0
                        hprev, hnext = (d == 0 and g > 0), (d == 1 and g > 0)
                        init, final = (g == 0), (g == nblk - 1)
                        S0, out = st_d[l, d, hp], None
                    Z = Zreg[:, zi * (TB + 2) * 4:(zi + 1) * (TB + 2) * 4].rearrange("p (t c) -> p t c", c=4)
                    zi += 1
                    self.build_Z(Z, tZ, TB, d, None if nkk is None else nkk[:, hp, :], r[:, hp, :], t_src, tok0, hprev, hnext)
                    pairs.append(dict(d=d, Z=Z, tZ=tZ, W=wdec[:, d * 2 + hp, tok0:tok0 + TB], tW=t_src[0],
                                      RAd=self.RAd[mname][hp * 2 + d, :, tok0:tok0 + TB, :], RBd=self.RBd[hp, :, tok0:tok0 + TB, :],
                                      Yd=self.Yd[hp * 2 + d, :, y0:y0 + TB + 2, :], tY=self.t_Y[hp * 2 + d],
                                      S0=S0, out=out, init=init, final=final, mode=mname))
            import os
            if os.environ.get("KSCAN", "1") == "1":
                self.scan_group(TB, pairs)

    def lru_mixer(self, path, l, nseg, tseg):
        fw = self.fw
        S = self.big
        t_s = self.t_big
        off = [0]

        def carve(n):
            a = S[:, off[0]:off[0] + n]
            off[0] += n
            return a
        ybr = carve(2 * NT).rearrange("p (c t) -> p c t", t=NT)
        xbr = carve(2 * NT).rearrange("p (c t) -> p c t", t=NT)
        xc = carve(2 * NT).rearrange("p (c t) -> p c t", t=NT)
        ta = carve(NT)
        tb = carve(NT)
        tcv = carve(NT)
        hs = carve(2 * NT).rearrange("p (c t) -> p c t", t=NT)
        hact = lambda kc, half: self.h[:, kc, half * 512:(half + 1) * 512]

        def ev(fc, half, ps, tp):
            dst = (ybr if fc < 2 else xbr)[:, fc % 2, half * 512:(half + 1) * 512]
            fw.op("act", lambda e: e.activation(out=dst, in_=ps[:], func=AF.Copy), R=[tp], W=[t_s])
        self.dense_fm(self.w_in_d[l], D, C_LR * 128, 512, hact, [self.t_h], ev)
        cw, cb = pvc("lr_conv_w", l), pvc("lr_conv_b", l)
        seg = lambda ap: ap.rearrange("p (s t) -> p s t", t=tseg)
        for c in range(2):
            fw.op("dve", lambda e: e.tensor_scalar(out=xc[:, c, :], in0=xbr[:, c, :], scalar1=self.pv[:, cw + 2 * 2 + c:cw + 2 * 2 + c + 1],
                                                   scalar2=self.pv[:, cb + c:cb + c + 1], op0=ALU.mult, op1=ALU.add),
                  R=[t_s, self.t_pv], W=[t_s])
            for j, sft in ((0, -2), (1, -1), (3, 1)):
                lo_o, hi_o = max(0, -sft), tseg - max(0, sft)
                o_ap = seg(xc[:, c, :])[:, :, lo_o:hi_o]
                i_ap = seg(xbr[:, c, :])[:, :, lo_o + sft:hi_o + sft]
                fw.op("dve", lambda e: e.scalar_tensor_tensor(out=o_ap, in0=i_ap, scalar=self.pv[:, cw + 2 * j + c:cw + 2 * j + c + 1],
                                                              in1=o_ap, op0=ALU.mult, op1=ALU.add), R=[t_s, self.t_pv], W=[t_s])
        lw = self.scr[:, 0:1024].rearrange("p (a q) -> p a q", q=128)
        t_lw = self.t_scr
        fw.dma("sp", lw[:], self.lrw_d[l].rearrange("a d c p q -> p (a d c) q"), R=[self.t_big], W=[t_lw])
        lam, ba, bx = pvc("lr_lam", l), pvc("lr_ba", l), pvc("lr_bx", l)
        c1 = self.lr_c1
        fw.op("act", lambda e: e.activation(out=c1[:, 0:4], in_=self.pv[:, lam:lam + 4], func=AF.Exp, scale=-1.0), R=[self.t_pv], W=[t_lw])
        fw.op("act", lambda e: e.activation(out=c1[:, 0:4], in_=c1[:, 0:4], func=AF.Ln, bias=self.onec[:, 0:1], scale=1.0), R=[t_lw, self.t_const], W=[t_lw])
        fw.op("dve", lambda e: e.tensor_scalar(out=c1[:, 4:8], in0=c1[:, 0:4], scalar1=-16.0, scalar2=None, op0=ALU.mult), R=[t_lw], W=[t_lw])
        fw.op("dve", lambda e: e.tensor_scalar(out=c1[:, 0:4], in0=c1[:, 0:4], scalar1=-8.0, scalar2=None, op0=ALU.mult), R=[t_lw], W=[t_lw])
        cnd = 0 if path == "P" else 1
        for d in range(2):
            for c in range(2):
                k = d * 2 + c
                for half in range(2):
                    sl = slice(half * 512, (half + 1) * 512)
                    psa, tpa = self.psum()
                    psx, tpx = self.psum()
                    fw.op("pe", lambda e: e.matmul(psa[:], lhsT=lw[:, 0 * 4 + k, :], rhs=xc[:, c, sl], start=True, stop=True), R=[t_lw, t_s], W=[tpa])
                    fw.op("pe", lambda e: e.matmul(psx[:], lhsT=lw[:, 1 * 4 + k, :], rhs=xc[:, c, sl], start=True, stop=True), R=[t_lw, t_s], W=[tpx])
                    fw.op("act", lambda e: e.activation(out=ta[:, sl], in_=psa[:], func=AF.Sigmoid, bias=self.pv[:, ba + k:ba + k + 1], scale=1.0),
                          R=[tpa, self.t_pv], W=[t_s])
                    fw.op("act", lambda e: e.activation(out=tb[:, sl], in_=psx[:], func=AF.Sigmoid, bias=self.pv[:, bx + k:bx + k + 1], scale=1.0),
                          R=[tpx, self.t_pv], W=[t_s])
                fw.op("act", lambda e: e.activation(out=tcv, in_=ta, func=AF.Exp, scale=c1[:, 4 + k:5 + k]), R=[t_s, t_lw], W=[t_s])
                fw.op("act", lambda e: e.activation(out=ta, in_=ta, func=AF.Exp, scale=c1[:, k:k + 1]), R=[t_s, t_lw], W=[t_s])
                fw.op("dve", lambda e: e.tensor_scalar(out=tcv, in0=tcv, scalar1=-1.0, scalar2=1.0, op0=ALU.mult, op1=ALU.add), R=[t_s], W=[t_s])
                fw.op("dve", lambda e: e.tensor_scalar(out=tcv, in0=tcv, scalar1=0.0, scalar2=None, op0=ALU.max), R=[t_s], W=[t_s])
                fw.op("act", lambda e: e.activation(out=tcv, in_=tcv, func=AF.Sqrt), R=[t_s], W=[t_s])
                fw.op("dve", lambda e: e.tensor_tensor(out=tb, in0=tb, in1=xc[:, c, :], op=ALU.mult), R=[t_s], W=[t_s])
                fw.op("dve", lambda e: e.tensor_tensor(out=tb, in0=tb, in1=tcv, op=ALU.mult), R=[t_s], W=[t_s])
                sgn = 1 if d == 0 else -1
                tsc = 256
                segs = lambda ap: ap.rearrange("p (s t) -> p s t", t=tsc)
                sh = 1
                A, B, A2, B2 = ta, tb, tcv, carve_tmp(self, S)
                while sh < tsc:
                    lo, hi = (sh, tsc) if d == 0 else (0, tsc - sh)
                    cur = lambda ap: segs(ap)[:, :, lo:hi]
                    prv = lambda ap: segs(ap)[:, :, lo - sgn * sh:hi - sgn * sh]
                    keep = lambda ap: segs(ap)[:, :, 0:sh] if d == 0 else segs(ap)[:, :, tsc - sh:tsc]
                    fw.op("dve", lambda e: e.tensor_tensor(out=cur(B2), in0=cur(A), in1=prv(B), op=ALU.mult), R=[t_s], W=[t_s])
                    fw.op("dve", lambda e: e.tensor_tensor(out=cur(B2), in0=cur(B2), in1=cur(B), op=ALU.add), R=[t_s], W=[t_s])
                    fw.op("pool", lambda e: e.tensor_tensor(out=cur(A2), in0=cur(A), in1=prv(A), op=ALU.mult), R=[t_s], W=[t_s])
                    fw.op("pool", lambda e: e.tensor_copy(out=keep(B2), in_=keep(B)), R=[t_s], W=[t_s])
                    fw.op("pool", lambda e: e.tensor_copy(out=keep(A2), in_=keep(A)), R=[t_s], W=[t_s])
                    A, A2 = A2, A
                    B, B2 = B2, B
                    sh *= 2
                if path == "S":
                    st0 = pvc("st_lr") + (l * 2 + d) * 2 + c
                    hin = self.pv[:, st0:st0 + 1]
                    for sgi in (range(4) if d == 0 else range(3, -1, -1)):
                        bsl = slice(sgi * tsc, (sgi + 1) * tsc)
                        hcol = hin
                        fw.op("dve", lambda e: e.scalar_tensor_tensor(out=B[:, bsl], in0=A[:, bsl], scalar=hcol, in1=B[:, bsl], op0=ALU.mult, op1=ALU.add),
                              R=[t_s, self.t_pv], W=[t_s])
                        endpos = sgi * tsc + (tsc - 1 if d == 0 else 0)
                        hin = B[:, endpos:endpos + 1]
                if d == 0:
                    fw.op("dve", lambda e: e.tensor_copy(out=hs[:, c, :], in_=B), R=[t_s], W=[t_s])
                else:
                    fw.op("dve", lambda e: e.tensor_tensor(out=hs[:, c, :], in0=hs[:, c, :], in1=B, op=ALU.add), R=[t_s], W=[t_s])
                if path == "P":
                    for s in range(4):
                        pos = s * tseg + (tseg - 1 if d == 0 else 0)
                        col = ((s * DEPTH + l) * 2 + d) * 2 + c
                        fw.op("act", lambda e: e.activation(out=self.lru_fin[:, col:col + 1], in_=B[:, pos:pos + 1], func=AF.Copy),
                              R=[t_s], W=[self.t_lrufin])
        for c in range(2):
            y = ybr[:, c, :]
            fw.op("dve", lambda e: e.tensor_tensor(out=ta, in0=y, in1=y, op=ALU.mult), R=[t_s], W=[t_s])
            fw.op("dve", lambda e: e.tensor_scalar(out=ta, in0=ta, scalar1=0.044715, scalar2=1.0, op0=ALU.mult, op1=ALU.add), R=[t_s], W=[t_s])
            fw.op("dve", lambda e: e.tensor_tensor(out=ta, in0=ta, in1=y, op=ALU.mult), R=[t_s], W=[t_s])
            fw.op("act", lambda e: e.activation(out=ta, in_=ta, func=AF.Sigmoid, scale=1.5957691216057308), R=[t_s], W=[t_s])
            fw.op("dve", lambda e: e.tensor_tensor(out=ta, in0=ta, in1=y, op=ALU.mult), R=[t_s], W=[t_s])
            fw.op("dve", lambda e: e.tensor_tensor(out=self.mix[:, 6 + c, :], in0=ta, in1=hs[:, c, :], op=ALU.mult), R=[t_s], W=[self.t_mix[6 + c]])


def carve_tmp(prog, S):
    return S[:, 15 * NT:16 * NT]


def build_program(depth=DEPTH):
    p = Prog(depth)
    nc = p.nc
    p.lr_c1 = nc.alloc_sbuf_tensor("lr_c1", [128, 8], F32)
    p.onec = nc.alloc_sbuf_tensor("onec", [128, 1], F32)
    p.fw.op("pool", lambda e: e.memset(p.onec[:], 1.0), W=[p.t_const])
    p.build()
    return nc


def _cols(v):
    v = np.asarray(v, np.float32).reshape(-1, 128)
    return np.ascontiguousarray(v.T)


def _host_prep(inp):
    f = np.float32
    w_in = np.asarray(inp["w_in"], f)
    w_in_p = np.zeros((DEPTH, D, WINP), f)
    for dst, src, n in _win_colmap():
        w_in_p[:, :, dst:dst + n] = w_in[:, :, src:src + n]
    lr_w = np.zeros((DEPTH, 2, 2, 2, 128, 128), f)
    for ai, nm in enumerate(("lr_wa", "lr_wx")):
        w = np.asarray(inp[nm], f)
        for c in range(2):
            for hh in range(2):
                lr_w[:, ai, :, c, hh * 64:(hh + 1) * 64, hh * 64:(hh + 1) * 64] = w[:, :, 2 * c + hh]
    ssb = np.zeros((DEPTH, 128, 16), f)
    ssb[:, :, 0:8] = np.asarray(inp["ss_dt_bias"], f).reshape(DEPTH, 1, 8)
    ssb[:, :, 8:16] = np.asarray(inp["ss_A_log"], f).reshape(DEPTH, 1, 8)
    ii = np.arange(128)
    le = (ii[:, None] <= ii[None, :])
    blk = (ii[:, None] // 64 == ii[None, :] // 64)
    cst = np.stack([np.eye(128), le, le.T, np.where(le, 0.0, -30000.0), np.where(le.T, 0.0, -30000.0), blk], 1).astype(f)
    rw_lw = np.zeros((DEPTH, 5, 128, 256), f)
    rw_lw[:, 0, 0:16] = np.asarray(inp["rw_w2"], f)[:, 0]
    rw_lw[:, 1, 32:48] = np.asarray(inp["rw_w2"], f)[:, 1]
    rw_lw[:, 2, 64:96] = np.asarray(inp["rw_g2"], f)
    rw_lw[:, 3, 0:16] = np.asarray(inp["rw_a2"], f)[:, 0]
    rw_lw[:, 4, 32:48] = np.asarray(inp["rw_a2"], f)[:, 1]
    shared = dict(rw_lw=rw_lw, w_dt=np.ascontiguousarray(w_in[:, :, 3168:3176]), ssb=ssb, cst=np.ascontiguousarray(cst), w_in_p=w_in_p, w_out=np.asarray(inp["w_out"], f), w1=np.asarray(inp["mlp_w1"], f),
                  w2=np.asarray(inp["mlp_w2"], f), mod_w=np.asarray(inp["mod_w"], f), lr_w=lr_w)
    pv0 = np.zeros((128, _NPV), f)
    ng = np.asarray(inp["norm_g"], f)
    for l in range(DEPTH):
        for i, nm in enumerate(("g_pre1", "g_post1", "g_pre2", "g_post2")):
            pv0[:, pvc(nm, l):pvc(nm, l) + 8] = _cols(ng[l, i])
        pv0[:, pvc("mod_b", l):pvc("mod_b", l) + 48] = _cols(inp["mod_b"][l])
        pv0[:, pvc("lr_conv_w", l):pvc("lr_conv_w", l) + 8] = _cols(np.asarray(inp["lr_conv_w"][l]).reshape(-1))
        pv0[:, pvc("lr_conv_b", l):pvc("lr_conv_b", l) + 2] = _cols(inp["lr_conv_b"][l])
        for nm in ("lr_ba", "lr_bx", "lr_lam", "hg_lb"):
            pv0[:, pvc(nm, l):pvc(nm, l) + 4] = _cols(np.asarray(inp[nm][l]).reshape(-1))
        pv0[:, pvc("ss_conv_w", l):pvc("ss_conv_w", l) + 24] = _cols(np.asarray(inp["ss_conv_w"][l]).reshape(-1))
        pv0[:, pvc("ss_conv_b", l):pvc("ss_conv_b", l) + 6] = _cols(inp["ss_conv_b"][l])
        pv0[:, pvc("ss_D", l):pvc("ss_D", l) + 2] = _cols(np.repeat(np.asarray(inp["ss_D"][l], f), 64))
        pv0[:, pvc("ss_norm", l):pvc("ss_norm", l) + 2] = _cols(inp["ss_norm"][l])
        pv0[:, pvc("hg_norm", l):pvc("hg_norm", l) + 2] = _cols(inp["hg_norm"][l])
        mu_p = np.zeros(1024, f)
        for dst, src, n in _win_colmap():
            if dst < 1024:
                mu_p[dst:dst + n] = np.asarray(inp["rw_mu"], f)[l, src:src + n]
        pv0[:, pvc("rw_mu", l):pvc("rw_mu", l) + 8] = _cols(mu_p)
        for nm in ("rw_w0", "rw_a0"):
            pv0[:, pvc(nm, l):pvc(nm, l) + 4] = _cols(np.asarray(inp[nm][l]).reshape(-1))
        for nm in ("rw_kk", "rw_ka", "rw_rk", "rw_ln_w", "rw_ln_b"):
            pv0[:, pvc(nm, l):pvc(nm, l) + 2] = _cols(np.asarray(inp[nm][l]).reshape(-1))
    ch = np.full(1024, -1, np.int64)
    for dst, src, n in _win_colmap():
        if dst < 1024:
            ch[dst:dst + n] = np.arange(src, src + n)
    mP = np.stack([(ch >= 0) & (ch < 432), (ch >= 432)], 0).astype(f)
    mS = np.stack([(ch >= 0) & (ch < 216), (ch >= 216) & (ch < 432), (ch >= 432) & (ch < 648), (ch >= 648)], 0).astype(f)
    pv0[:, pvc("mP"):pvc("mP") + 16] = _cols(mP.reshape(-1))
    pv0[:, pvc("mS"):pvc("mS") + 32] = _cols(mS.reshape(-1))
    if True:
        pass
    return shared, pv0


_NC_CACHE = {}


def kernel(**inp):
    f = np.float32
    shared, pv0 = _host_prep(inp)
    xp = np.asarray(inp["x_prompt"], f)
    xs = np.asarray(inp["x_sample"], f)
    in_maps = []
    for c in range(NCORES):
        b = c % 4
        pv = pv0.copy()
        pv[:, pvc("cond"):pvc("cond") + 8] = _cols(inp["c_ctx"])
        pv[:, pvc("cond") + 8:pvc("cond") + 16] = _cols(inp["c"][b])
        pv[:, pvc("st_lr"):pvc("st_lr") + DEPTH * 4] = _cols(np.asarray(inp["state_lru"][b], f).reshape(-1))
        m = dict(shared)
        m["pv"] = pv
        m["xP"] = np.ascontiguousarray(xp[4 * c:4 * c + 4].reshape(NT, D).T)
        m["xS"] = np.ascontiguousarray(xs[b].T)
        m["st_ss"] = np.ascontiguousarray(np.asarray(inp["state_ssd"], f)[b].transpose(0, 1, 2, 4, 3))
        m["st_hg"] = np.ascontiguousarray(np.asarray(inp["state_hgrn"], f)[b].reshape(DEPTH, 2, 2, 128, 64))
        m["st_rw"] = np.ascontiguousarray(np.asarray(inp["state_rwkv"], f)[b].transpose(0, 1, 2, 4, 3).reshape(DEPTH, 2, 2, 128, 64))
        in_maps.append(m)
    if "nc" not in _NC_CACHE:
        _NC_CACHE["nc"] = build_program()
    res = run_bass_kernel_spmd(_NC_CACHE["nc"], in_maps, core_ids=list(range(NCORES)))
    R = res.results
    y_prompt = np.concatenate([R[c]["yP"].T.reshape(4, 256, D) for c in range(NCORES)], 0)
    y_sample = np.stack([R[b]["yS"].T for b in range(4)], 0)
    o_rw = np.concatenate([R[c]["o_rw"] for c in range(NCORES)], 0).reshape(32, DEPTH, 2, 2, 2, 64, 64)
    new_rw = np.ascontiguousarray(o_rw.reshape(32, DEPTH, 2, 4, 64, 64).transpose(0, 1, 2, 3, 5, 4))
    o_hg = np.concatenate([R[c]["o_hg"] for c in range(NCORES)], 0)
    new_hg = np.ascontiguousarray(o_hg.reshape(32, DEPTH, 2, 4, 64, 64))
    o_ss = np.concatenate([R[c]["o_ss"] for c in range(NCORES)], 0)
    new_ss = np.ascontiguousarray(o_ss.transpose(0, 1, 2, 3, 5, 4))
    o_lr = np.stack([R[c]["o_lr"] for c in range(NCORES)], 0)
    new_lr = o_lr.reshape(NCORES, 128, 4, DEPTH, 2, 2).transpose(0, 2, 3, 4, 5, 1).reshape(32, DEPTH, 2, 256)
    return (y_prompt.astype(f), y_sample.astype(f), new_rw.astype(f), new_hg.astype(f), new_ss.astype(f),
            np.ascontiguousarray(new_lr).astype(f))
```

```python
import numpy as np
import concourse.bass as bass
import concourse.mybir as mybir
from concourse.bass_utils import run_bass_kernel_spmd

F32 = mybir.dt.float32
BF16 = mybir.dt.bfloat16
ALU = mybir.AluOpType
AF = mybir.ActivationFunctionType

D = 1024
DEPTH = 4
NT = 1024
DFF = 4096
NCORES = 8
WINP = 3840
EPS = 1e-6

C_RW, C_HG, C_SS, C_LR = 0, 8, 18, 26


def _win_colmap():
    m = []
    m.append((0, 0, 768))
    m.append((768 + 0, 768, 16))
    m.append((768 + 32, 784, 16))
    m.append((768 + 64, 832, 32))
    m.append((896 + 0, 800, 16))
    m.append((896 + 32, 816, 16))
    m.append((1024, 864, 1280))
    m.append((2304, 2144, 1024))
    m.append((3328, 3176, 512))
    return m


_PV = {}
_NPV = 0


def _pv_add(name, ncols):
    global _NPV
    _PV[name] = (_NPV, ncols)
    _NPV += ncols


for _l in range(DEPTH):
    for _n, _c in (("g_pre1", 8), ("g_post1", 8), ("g_pre2", 8), ("g_post2", 8), ("mod_b", 48),
                   ("lr_conv_w", 8), ("lr_conv_b", 2), ("lr_ba", 4), ("lr_bx", 4), ("lr_lam", 4),
                   ("ss_conv_w", 24), ("ss_conv_b", 6), ("ss_D", 2), ("ss_norm", 2),
                   ("hg_lb", 4), ("hg_norm", 2)):
        _pv_add(f"{_n}{_l}", _c)
for _l in range(DEPTH):
    for _n, _c in (("rw_mu", 8), ("rw_w0", 4), ("rw_a0", 4), ("rw_kk", 2), ("rw_ka", 2), ("rw_rk", 2), ("rw_ln_w", 2), ("rw_ln_b", 2)):
        _pv_add(f"{_n}{_l}", _c)
_pv_add("mP", 16)
_pv_add("mS", 32)
_pv_add("cond", 16)
_pv_add("st_lr", DEPTH * 4)


def pvc(name, l=None):
    return _PV[name if l is None else f"{name}{l}"][0]


class Tok:
    __slots__ = ("w", "r")

    def __init__(self):
        self.w = None
        self.r = {}


class FW:
    CH = 30000

    def __init__(self, nc):
        self.nc = nc
        self.eng = {"pe": nc.tensor, "act": nc.scalar, "dve": nc.vector, "pool": nc.gpsimd, "sp": nc.sync}
        self.sems = {k: [] for k in self.eng}
        self.cnt = {k: 0 for k in self.eng}
        self.waited = {k: {} for k in self.eng}
        self.dq = {}
        self.ndma = {}
        self.nsem = 0

    def _newsem(self, name):
        self.nsem += 1
        return self.nc.alloc_semaphore(f"{name}_{self.nsem}")

    def _wait(self, e, dep):
        if dep is None:
            return
        sem, val, owner = dep
        if owner == "pe" and e == "pe":
            return
        key = id(sem)
        if self.waited[e].get(key, 0) >= val:
            return
        self.eng[e].wait_ge(sem, val)
        self.waited[e][key] = val

    def _deps(self, e, R, W):
        for t in R:
            self._wait(e, t.w)
        for t in W:
            self._wait(e, t.w)
            for d in t.r.values():
                self._wait(e, d)

    def _commit(self, dep, R, W):
        k = id(dep[0])
        for t in R:
            o = t.r.get(k)
            if o is None or o[1] < dep[1]:
                t.r[k] = dep
        for t in W:
            t.w = dep
            t.r = {}

    def op(self, e, fn, R=(), W=()):
        self._deps(e, R, W)
        n = self.cnt[e]
        ch, loc = divmod(n, self.CH)
        if ch >= len(self.sems[e]):
            self.sems[e].append(self._newsem(e))
        sem = self.sems[e][ch]
        inst = fn(self.eng[e])
        inst.then_inc(sem, 1)
        self.cnt[e] = n + 1
        dep = (sem, loc + 1, e)
        self._commit(dep, R, W)
        return dep

    def dma(self, q, out, in_, R=(), W=(), P=16, **kw):
        self._deps(q, R, W)
        if q not in self.dq:
            self.dq[q] = [[self._newsem("d" + q), 0] for _ in range(P)]
            self.ndma[q] = 0
        i = self.ndma[q] % P
        self.ndma[q] += 1
        slot = self.dq[q][i]
        if slot[1] > 0:
            self._wait(q, (slot[0], slot[1], "dma"))
        inst = self.eng[q].dma_start(out=out, in_=in_, **kw)
        slot[1] += 16
        inst.then_inc(slot[0], 16)
        dep = (slot[0], slot[1], "dma")
        self._commit(dep, R, W)
        return dep

    def finish(self, toks, e="sp"):
        for t in toks:
            self._wait(e, t.w)


class Prog:
    def __init__(self, depth=DEPTH):
        self.depth = depth
        nc = self.nc = bass.Bass("TRN2", target_bir_lowering=False)
        self.fw = FW(nc)
        self.outs = []
        dt = nc.dram_tensor
        self.x_d = {p: dt("x" + p, [D, NT], F32, kind="ExternalInput").ap() for p in "PS"}
        self.y_d = {p: dt("y" + p, [D, NT], F32, kind="ExternalOutput").ap() for p in "PS"}
        self.pv_d = dt("pv", [128, _NPV], F32, kind="ExternalInput").ap()
        self.w_in_d = dt("w_in_p", [DEPTH, D, WINP], F32, kind="ExternalInput").ap()
        self.w_out_d = dt("w_out", [DEPTH, D, D], F32, kind="ExternalInput").ap()
        self.w1_d = dt("w1", [DEPTH, D, DFF], F32, kind="ExternalInput").ap()
        self.w2_d = dt("w2", [DEPTH, DFF, D], F32, kind="ExternalInput").ap()
        self.modw_d = dt("mod_w", [DEPTH, D, 6 * D], F32, kind="ExternalInput").ap()
        self.lrw_d = dt("lr_w", [DEPTH, 2, 2, 2, 128, 128], F32, kind="ExternalInput").ap()
        self.wdt_d = dt("w_dt", [DEPTH, D, 8], F32, kind="ExternalInput").ap()
        self.ssb_d = dt("ssb", [DEPTH, 128, 16], F32, kind="ExternalInput").ap()
        self.cst_d = dt("cst", [128, 6, 128], F32, kind="ExternalInput").ap()
        self.rwlw_d = dt("rw_lw", [DEPTH, 5, 128, 256], F32, kind="ExternalInput").ap()
        self.st_ss_d = dt("st_ss", [DEPTH, 2, 4, 128, 64], F32, kind="ExternalInput").ap()
        self.st_hg_d = dt("st_hg", [DEPTH, 2, 2, 128, 64], F32, kind="ExternalInput").ap()
        self.st_rw_d = dt("st_rw", [DEPTH, 2, 2, 128, 64], F32, kind="ExternalInput").ap()
        self.RAd = {m: dt("RAd_" + m, [4, 6, NT, 128], F32).ap() for m in ("hg", "rw")}
        self.RBd = dt("RBd", [2, 2, NT, 64], F32).ap()
        self.Yd = dt("Yd", [4, 4, NT + 8, 64], F32).ap()
        self.o_lr_d = dt("o_lr", [128, 4 * DEPTH * 4], F32, kind="ExternalOutput").ap()
        self.o_rw_d = dt("o_rw", [4, DEPTH, 2, 2, 128, 64], F32, kind="ExternalOutput").ap()
        self.o_hg_d = dt("o_hg", [4, DEPTH, 2, 2, 128, 64], F32, kind="ExternalOutput").ap()
        self.o_ss_d = dt("o_ss", [4, DEPTH, 2, 4, 128, 64], F32, kind="ExternalOutput").ap()
        sb = nc.alloc_sbuf_tensor
        self.pv = sb("pv_sb", [128, _NPV], F32)
        self.t_pv = Tok()
        self.x = sb("x_sb", [128, 8, NT], F32)
        self.t_x = Tok()
        self.h = sb("h_sb", [128, 8, NT], BF16)
        self.t_h = Tok()
        self.mix = sb("mix_sb", [128, 8, NT], BF16)
        self.t_mix = [Tok() for _ in range(8)]
        self.wst = [sb(f"wst{i}", [128, 2048], F32) for i in range(2)]
        self.wbf = [sb(f"wbf{i}", [128, 2048], BF16) for i in range(2)]
        self.t_wst = [Tok(), Tok()]
        self.t_wbf = [Tok(), Tok()]
        self.nw = 0
        self.scr = sb("scr", [128, 8 * NT], F32)
        self.big = sb("big", [128, 16 * NT], F32)
        self.t_scr = Tok()
        self.t_big = Tok()
        self.rstd = sb("rstd", [128, NT], F32)
        self.t_rstd = Tok()
        self.rtmp = [self.rstd[:, 0:512], self.rstd[:, 512:1024]]
        self.t_rtmp = [self.t_rstd, self.t_rstd]
        self.sq = [sb("sq0", [128, NT], BF16)] * 2
        self.t_sq = [Tok()] * 2
        self.ones_bf = sb("ones_bf", [128, 128], BF16)
        self.t_const = Tok()
        self.modall = sb("modall", [128, DEPTH, 2, 48], F32)
        self.t_mod = Tok()
        self.cols = sb("cols", [128, 2, 48], F32)
        self.t_cols = Tok()
        self.scond = sb("scond", [128, 16], F32)
        self.lru_fin = sb("lru_fin", [128, 4 * DEPTH * 4], F32)
        self.t_lrufin = Tok()
        self.zero_o = sb("zero_o", [128, 64], F32)
        self.cst = sb("cst_sb", [128, 6, 128], F32)
        self.ones_f = sb("ones_f", [128, 128], F32)
        self.ss_S = sb("ss_S", [128, 8, 64], F32)
        self.t_ssS = [Tok() for _ in range(8)]
        self.ss_small = self.scr[:, 4096:4608].rearrange("p (i f) -> p i f", f=64)
        self.rwc = sb("rwc", [128, 64], F32)
        self.t_rwc = Tok()
        self.ssb = sb("ssb_sb", [128, 16], F32)
        self.wdt = sb("wdt_sb", [128, 8, 8], F32)
        self.wdt_bf = sb("wdt_bf", [128, 8, 8], BF16)
        self.sc_S = sb("sc_S", [128, 4, 64], F32)
        self.t_scS = [Tok() for _ in range(4)]
        self.sc_x = self.big[0:4, 16128:16384].rearrange("p (a f) -> p a f", f=64)
        self.t_scx = [Tok() for _ in range(4)]
        self.t_ps1 = [Tok() for _ in range(4)]
        self.t_ps2 = [Tok() for _ in range(4)]
        self.hgl = sb("hgl", [128, DEPTH, 8], F32)
        self.t_hgl = Tok()
        self.ttile = [sb(f"ttile{i}", [128, 128], F32) for i in range(2)]
        self.t_ttile = [Tok(), Tok()]
        self.ytile = [sb(f"ytile{i}", [128, 2, 128], F32) for i in range(2)]
        self.t_ytile = [Tok(), Tok()]
        self.ntt = 0
        self.ps = [nc.alloc_psum_tensor(f"ps{i}", [128, 512], F32) for i in range(8)]
        self.t_ps = [Tok() for _ in range(8)]
        self.nps = 0

    def psum(self, lo=4, hi=6):
        i = lo + self.nps % (hi - lo)
        self.nps += 1
        return self.ps[i], self.t_ps[i]

    def wload(self, view, nk, ncl, convert=True):
        fw = self.fw
        i = self.nw % 2
        self.nw += 1
        st = self.wst[i][:, 0:nk * ncl].rearrange("p (k c) -> p k c", c=ncl)
        fw.dma("sp", st, view, W=[self.t_wst[i]])
        if not convert:
            return st, self.t_wst[i]
        bf = self.wbf[i][:, 0:nk * ncl].rearrange("p (k c) -> p k c", c=ncl)
        fw.op("pool", lambda e: e.tensor_copy(out=bf, in_=st), R=[self.t_wst[i]], W=[self.t_wbf[i]])
        return bf, self.t_wbf[i]

    def dense_fm(self, wl, K, c0, ncols, act, t_act, evac):
        fw = self.fw
        nk = K // 128
        wv = wl.rearrange("(k p) c -> p k c", p=128)
        if nk <= 8:
            bc = 2048 // nk
            bc = min(bc, ncols)
            for b0 in range(0, ncols, bc):
                wb, tw = self.wload(wv[:, :, c0 + b0:c0 + b0 + bc], nk, bc)
                for j in range(bc // 128):
                    for half in range(2):
                        ps, tp = self.psum()
                        for kc in range(nk):
                            fw.op("pe", lambda e: e.matmul(ps[:], lhsT=wb[:, kc, j * 128:(j + 1) * 128], rhs=act(kc, half),
                                                           start=(kc == 0), stop=(kc == nk - 1)), R=[tw] + t_act, W=[tp])
                        evac((b0 // 128) + j, half, ps, tp)
        else:
            for fc in range(ncols // 128):
                pss = [self.psum(), self.psum()]
                for kb in range(0, nk, 16):
                    wb, tw = self.wload(wv[:, kb:kb + 16, c0 + fc * 128:c0 + (fc + 1) * 128], 16, 128)
                    for half in range(2):
                        ps, tp = pss[half]
                        for kc in range(16):
                            fw.op("pe", lambda e: e.matmul(ps[:], lhsT=wb[:, kc, :], rhs=act(kb + kc, half),
                                                           start=(kb + kc == 0), stop=(kb + kc == nk - 1)), R=[tw] + t_act, W=[tp])
                for half in range(2):
                    evac(fc, half, pss[half][0], pss[half][1])

    def rms_stats(self, src, t_src, nch=8, dim=D):
        fw = self.fw
        pss = [self.psum(6, 8), self.psum(6, 8)]
        for c in range(nch):
            sq, tsq = self.sq[c % 2], self.t_sq[c % 2]
            s = src(c)
            fw.op("act", lambda e: e.activation(out=sq[:], in_=s, func=AF.Square), R=t_src, W=[tsq])
            for half in range(2):
                ps, tp = pss[half]
                fw.op("pe", lambda e: e.matmul(ps[:], lhsT=self.ones_bf[:], rhs=sq[:, half * 512:(half + 1) * 512],
                                               start=(c == 0), stop=(c == nch - 1)), R=[tsq, self.t_const], W=[tp])
        for half in range(2):
            ps, tp = pss[half]
            r = self.rstd[:, half * 512:(half + 1) * 512]
            fw.op("act", lambda e: e.activation(out=r, in_=ps[:], func=AF.Sqrt, bias=self.epsc[:, 0:1], scale=1.0 / dim),
                  R=[tp, self.t_const], W=[self.t_rstd])
            fw.op("dve", lambda e: e.reciprocal(out=r, in_=r), R=[self.t_rstd], W=[self.t_rstd])

    def build(self):
        nc, fw = self.nc, self.fw
        sb = nc.alloc_sbuf_tensor
        self.epsc = sb("epsc", [128, 2], F32)
        fw.op("pool", lambda e: e.memset(self.epsc[:, 0:1], EPS), W=[self.t_const])
        fw.op("pool", lambda e: e.memset(self.epsc[:, 1:2], 64e-5), W=[self.t_const])
        fw.op("pool", lambda e: e.memset(self.ones_bf[:], 1.0), W=[self.t_const])
        fw.op("pool", lambda e: e.memset(self.zero_o[:], 0.0), W=[self.t_const])
        fw.op("pool", lambda e: e.memset(self.lru_fin[:], 0.0), W=[self.t_lrufin])
        fw.dma("sp", self.pv[:], self.pv_d[:, :], W=[self.t_pv])
        fw.dma("sp", self.cst[:], self.cst_d[:, :, :], W=[self.t_const])
        fw.op("pool", lambda e: e.memset(self.ones_f[:], 1.0), W=[self.t_const])
        import os
        self.stage = int(os.environ.get("KSTAGE", "9"))
        if self.stage >= 1:
            self.modulation()
        if self.stage >= 7:
            self.zero_rows()
            self.hg_bounds()
        for path in "PS":
            self.run_path(path)
        t_o = Tok()
        fw.dma("sp", self.o_lr_d[:, :], self.lru_fin[:], R=[self.t_lrufin], W=[t_o])
        self.outs.append(t_o)
        for od in ((self.o_rw_d,) if self.stage < 8 else ()) + ((self.o_hg_d,) if self.stage < 7 else ()) + ((self.o_ss_d,) if self.stage < 6 else ()):
            flat = od.rearrange("s l d h p v -> p (s l d h) v")
            n = flat.shape[1]
            for i in range(n):
                t = Tok()
                fw.dma("sp", flat[:, i, :], self.zero_o[:], R=[self.t_const], W=[t])
                self.outs.append(t)
        fw.finish(self.outs)
        return nc

    def modulation(self):
        fw = self.fw
        c0 = pvc("cond")
        sig = self.nc.alloc_sbuf_tensor("sigc", [128, 16], F32)
        t_s = Tok()
        fw.op("act", lambda e: e.activation(out=sig[:], in_=self.pv[:, c0:c0 + 16], func=AF.Sigmoid), R=[self.t_pv], W=[t_s])
        fw.op("dve", lambda e: e.tensor_tensor(out=self.scond[:], in0=sig[:], in1=self.pv[:, c0:c0 + 16], op=ALU.mult),
              R=[t_s, self.t_pv], W=[t_s])
        sc = self.scond[:].rearrange("p (n k) -> p k n", k=8)
        for l in range(self.depth):
            wv = self.modw_d[l].rearrange("(k p) c -> p k c", p=128)
            ps, tp = self.psum(6, 8)
            for b in range(24):
                wb, tw = self.wload(wv[:, :, b * 256:(b + 1) * 256], 8, 256, convert=False)
                for j in range(2):
                    fc = b * 2 + j
                    for kc in range(8):
                        fw.op("pe", lambda e: e.matmul(ps[:, fc * 2:fc * 2 + 2], lhsT=wb[:, kc, j * 128:(j + 1) * 128], rhs=sc[:, kc, :],
                                                       start=(kc == 0), stop=(kc == 7)), R=[tw, t_s], W=[tp])
            mb = pvc("mod_b", l)
            for cnd in range(2):
                src = ps[:, 0:96].rearrange("p (f n) -> p n f", n=2)[:, cnd, :]
                fw.op("dve", lambda e: e.tensor_tensor(out=self.modall[:, l, cnd, :], in0=src, in1=self.pv[:, mb:mb + 48], op=ALU.add),
                      R=[tp, self.t_pv], W=[self.t_mod])

    def layer_cols(self, l, cnd):
        fw = self.fw
        m = self.modall[:, l, cnd, :]
        for k, (gpre, gpost, o) in enumerate((("g_pre1", "g_post1", 0), ("g_pre2", "g_post2", 24))):
            a = pvc(gpre, l)
            b = pvc(gpost, l)
            sh, scl, g = m[:, o:o + 8], m[:, o + 8:o + 16], m[:, o + 16:o + 24]
            fw.op("dve", lambda e: e.scalar_tensor_tensor(out=self.cols[:, 0, o:o + 8], in0=scl, scalar=1.0, in1=self.pv[:, a:a + 8],
                                                          op0=ALU.add, op1=ALU.mult), R=[self.t_mod, self.t_pv], W=[self.t_cols])
            fw.op("dve", lambda e: e.tensor_copy(out=self.cols[:, 0, o + 8:o + 16], in_=sh), R=[self.t_mod], W=[self.t_cols])
            fw.op("dve", lambda e: e.tensor_tensor(out=self.cols[:, 0, o + 16:o + 24], in0=g, in1=self.pv[:, b:b + 8], op=ALU.mult),
                  R=[self.t_mod, self.t_pv], W=[self.t_cols])

    def norm_mod(self, o):
        fw = self.fw
        self.rms_stats(lambda c: self.x[:, c, :], [self.t_x])
        tmp = self.scr[:, 0:NT]
        for c in range(8):
            fw.op("dve", lambda e: e.tensor_tensor(out=tmp, in0=self.x[:, c, :], in1=self.rstd[:], op=ALU.mult),
                  R=[self.t_x, self.t_rstd], W=[self.t_scr])
            fw.op("act", lambda e: e.activation(out=self.h[:, c, :], in_=tmp, func=AF.Identity,
                                                bias=self.cols[:, 0, o + 8 + c:o + 9 + c], scale=self.cols[:, 0, o + c:o + c + 1]),
                  R=[self.t_scr, self.t_cols], W=[self.t_h])

    def post_resid(self, res, t_res, o):
        fw = self.fw
        self.rms_stats(res, t_res)
        for c in range(8):
            r = res(c)
            fw.op("dve", lambda e: e.tensor_tensor(out=r, in0=r, in1=self.rstd[:], op=ALU.mult), R=[self.t_rstd] + t_res, W=t_res)
            fw.op("dve", lambda e: e.scalar_tensor_tensor(out=self.x[:, c, :], in0=r, scalar=self.cols[:, 0, o + 16 + c:o + 17 + c],
                                                          in1=self.x[:, c, :], op0=ALU.mult, op1=ALU.add),
                  R=t_res + [self.t_cols], W=[self.t_x])

    def run_path(self, path):
        fw = self.fw
        cnd = 0 if path == "P" else 1
        nseg, tseg = (4, 256) if path == "P" else (1, 1024)
        fw.dma("sp", self.x[:], self.x_d[path].rearrange("(c p) t -> p c t", p=128), W=[self.t_x])
        hact = lambda kc, half: self.h[:, kc, half * 512:(half + 1) * 512]
        for l in range(self.depth if self.stage >= 2 else 0):
            self.layer_cols(l, cnd)
            self.norm_mod(0)
            if self.stage == 2:
                continue
            for c in range(4):
                if (c < 2 and self.stage < 8) or (c >= 2 and self.stage < 7):
                    fw.op("pool", lambda e: e.memset(self.mix[:, c, :], 0.0), W=[self.t_mix[c]])
            if self.stage >= 8:
                self.rw_mixer(path, l, nseg, tseg)
            if self.stage >= 7:
                self.hg_mixer(path, l, nseg, tseg)
            if self.stage >= 6:
                self.ssd_mixer(path, l, nseg, tseg)
            else:
                for c in range(4, 6):
                    fw.op("pool", lambda e: e.memset(self.mix[:, c, :], 0.0), W=[self.t_mix[c]])
            for c in range(6, 8):
                if self.stage < 4:
                    fw.op("pool", lambda e: e.memset(self.mix[:, c, :], 0.0), W=[self.t_mix[c]])
            if self.stage >= 4:
                self.lru_mixer(path, l, nseg, tseg)
            import os
            if path == "S" and l == 0 and os.environ.get("KDUMP"):
                dbg_d = self.nc.dram_tensor("dbg", [128, 8, NT], F32, kind="ExternalOutput").ap()
                for c in range(8):
                    fw.op("act", lambda e: e.activation(out=self.scr[:, 0:NT], in_=self.mix[:, c, :], func=AF.Copy), R=[self.t_mix[c]], W=[self.t_scr])
                    t_d = Tok()
                    fw.dma("sp", dbg_d[:, c, :], self.scr[:, 0:NT], R=[self.t_scr], W=[t_d])
                    self.outs.append(t_d)
            res = self.scr[:, 0:8 * NT].rearrange("p (c t) -> p c t", t=NT)
            t_res = [self.t_scr]

            def ev_out(fc, half, ps, tp):
                fw.op("act", lambda e: e.activation(out=res[:, fc, half * 512:(half + 1) * 512], in_=ps[:], func=AF.Copy),
                      R=[tp], W=t_res)
            mact = lambda kc, half: self.mix[:, kc, half * 512:(half + 1) * 512]
            self.dense_fm(self.w_out_d[l], D, 0, D, mact, self.t_mix, ev_out)
            self.post_resid(lambda c: res[:, c, :], t_res, 0)
            if self.stage in (3, 4):
                continue
            self.norm_mod(24)
            hid = self.big[:].bitcast(BF16).rearrange("p (c t) -> p c t", t=NT)
            t_hid = self.t_big

            def ev_h(fc, half, ps, tp):
                tmp = self.rtmp[half]
                fw.op("act", lambda e: e.activation(out=tmp, in_=ps[:], func=AF.Relu), R=[tp], W=[self.t_rtmp[half]])
                fw.op("pool", lambda e: e.tensor_tensor(out=hid[:, fc, half * 512:(half + 1) * 512], in0=tmp, in1=tmp,
                                                        op=ALU.mult), R=[self.t_rtmp[half]], W=[t_hid])
            self.dense_fm(self.w1_d[l], D, 0, DFF, hact, [self.t_h], ev_h)
            hidact = lambda kc, half: hid[:, kc, half * 512:(half + 1) * 512]
            self.dense_fm(self.w2_d[l], DFF, 0, D, hidact, [t_hid], ev_out)
            self.post_resid(lambda c: res[:, c, :], t_res, 24)
        t_y = Tok()
        fw.dma("sp", self.y_d[path].rearrange("(c p) t -> p c t", p=128), self.x[:], R=[self.t_x], W=[t_y])
        self.outs.append(t_y)


    def ssd_mixer(self, path, l, nseg, tseg):
        fw = self.fw
        B = self.big
        tb = self.t_big
        U = lambda i, n=1: B[:, i * NT:(i + n) * NT]
        z = U(0, 2).rearrange("p (c t) -> p c t", t=NT)
        raw = U(2, 2).rearrange("p (c t) -> p c t", t=NT)
        xbc = U(4, 6).rearrange("p (c t) -> p c t", t=NT)
        tok = U(10, 4).rearrange("p (i f) -> p i f", f=512)
        ysum = U(14, 2).rearrange("p (c t) -> p c t", t=NT)
        hact = lambda kc, half: self.h[:, kc, half * 512:(half + 1) * 512]
        seg = lambda ap: ap.rearrange("p (s t) -> p s t", t=tseg)
        cw, cb = pvc("ss_conv_w", l), pvc("ss_conv_b", l)
        ident, triF, triB, negF, negB = (self.cst[:, i, :] for i in range(5))

        def ev_z(fc, half, ps, tp):
            fw.op("act", lambda e: e.activation(out=z[:, fc, half * 512:(half + 1) * 512], in_=ps[:], func=AF.Copy), R=[tp], W=[tb])
        self.dense_fm(self.w_in_d[l], D, C_SS * 128, 256, hact, [self.t_h], ev_z)
        for pair in range(3):
            def ev_r(fc, half, ps, tp):
                fw.op("act", lambda e: e.activation(out=raw[:, fc, half * 512:(half + 1) * 512], in_=ps[:], func=AF.Copy), R=[tp], W=[tb])
            self.dense_fm(self.w_in_d[l], D, (C_SS + 2 + 2 * pair) * 128, 256, hact, [self.t_h], ev_r)
            for cc in range(2):
                c = 2 * pair + cc
                o = xbc[:, c, :]
                fw.op("dve", lambda e: e.tensor_scalar(out=o, in0=raw[:, cc, :], scalar1=self.pv[:, cw + 2 * 6 + c:cw + 2 * 6 + c + 1],
                                                       scalar2=self.pv[:, cb + c:cb + c + 1], op0=ALU.mult, op1=ALU.add), R=[tb, self.t_pv], W=[tb])
                for j, sft in ((0, -2), (1, -1), (3, 1)):
                    lo_o, hi_o = max(0, -sft), tseg - max(0, sft)
                    o_ap = seg(o)[:, :, lo_o:hi_o]
                    i_ap = seg(raw[:, cc, :])[:, :, lo_o + sft:hi_o + sft]
                    fw.op("dve", lambda e: e.scalar_tensor_tensor(out=o_ap, in0=i_ap, scalar=self.pv[:, cw + 6 * j + c:cw + 6 * j + c + 1],
                                                                  in1=o_ap, op0=ALU.mult, op1=ALU.add), R=[tb, self.t_pv], W=[tb])
                fw.op("act", lambda e: e.activation(out=raw[:, cc, :], in_=o, func=AF.Sigmoid), R=[tb], W=[tb])
                fw.op("dve", lambda e: e.tensor_tensor(out=o, in0=o, in1=raw[:, cc, :], op=ALU.mult), R=[tb], W=[tb])
        for c in range(2):
            fw.op("act", lambda e: e.activation(out=raw[:, c, :], in_=z[:, c, :], func=AF.Sigmoid), R=[tb], W=[tb])
            fw.op("dve", lambda e: e.tensor_tensor(out=z[:, c, :], in0=z[:, c, :], in1=raw[:, c, :], op=ALU.mult), R=[tb], W=[tb])
        t_w = Tok()
        fw.dma("sp", self.wdt[:], self.wdt_d[l].rearrange("(k p) c -> p k c", p=128), W=[t_w])
        fw.dma("sp", self.ssb[:], self.ssb_d[l], W=[t_w])
        fw.op("pool", lambda e: e.tensor_copy(out=self.wdt_bf[:], in_=self.wdt[:]), R=[t_w], W=[t_w])
        fw.op("act", lambda e: e.activation(out=self.ssb[:, 8:16], in_=self.ssb[:, 8:16], func=AF.Exp), R=[t_w], W=[t_w])
        fw.op("dve", lambda e: e.tensor_scalar(out=self.ssb[:, 8:16], in0=self.ssb[:, 8:16], scalar1=-1.0, scalar2=None, op0=ALU.mult), R=[t_w], W=[t_w])
        sm = self.ss_small
        t_sm = Tok()
        for i in range(8):
            tsl = slice(i * 128, (i + 1) * 128)
            ps, tp = self.psum()
            for q, c in enumerate((0, 1, 2, 3)):
                fw.op("pe", lambda e: e.transpose(ps[:, q * 128:(q + 1) * 128], xbc[:, c, tsl], ident), R=[tb, self.t_const], W=[tp])
            fw.op("act", lambda e: e.activation(out=tok[:, i, :], in_=ps[:], func=AF.Copy), R=[tp], W=[tb])
            ps2, tp2 = self.psum()
            for kc in range(8):
                fw.op("pe", lambda e: e.matmul(ps2[:, 0:8], lhsT=self.h[:, kc, tsl], rhs=self.wdt_bf[:, kc, :], start=(kc == 0), stop=(kc == 7)),
                      R=[self.t_h, t_w], W=[tp2])
            fw.op("dve", lambda e: e.tensor_tensor(out=sm[:, i, 0:8], in0=ps2[:, 0:8], in1=self.ssb[:, 0:8], op=ALU.add), R=[tp2, t_w], W=[t_sm])
            fw.op("act", lambda e: e.activation(out=sm[:, i, 0:8], in_=sm[:, i, 0:8], func=AF.Exp), R=[t_sm], W=[t_sm])
            fw.op("act", lambda e: e.activation(out=sm[:, i, 0:8], in_=sm[:, i, 0:8], func=AF.Ln, bias=self.onec[:, 0:1], scale=1.0), R=[t_sm, self.t_const], W=[t_sm])
            fw.op("dve", lambda e: e.tensor_tensor(out=sm[:, i, 8:16], in0=sm[:, i, 0:8], in1=self.ssb[:, 8:16], op=ALU.mult), R=[t_sm, t_w], W=[t_sm])
            ps3, tp3 = self.psum()
            fw.op("pe", lambda e: e.matmul(ps3[:, 0:4], lhsT=triF, rhs=sm[:, i, 8:12], start=True, stop=True), R=[t_sm, self.t_const], W=[tp3])
            fw.op("pe", lambda e: e.matmul(ps3[:, 4:8], lhsT=triB, rhs=sm[:, i, 12:16], start=True, stop=True), R=[t_sm, self.t_const], W=[tp3])
            fw.op("pe", lambda e: e.matmul(ps3[:, 8:16], lhsT=self.ones_f[:], rhs=sm[:, i, 8:16], start=True, stop=True), R=[t_sm, self.t_const], W=[tp3])
            fw.op("dve", lambda e: e.tensor_copy(out=sm[:, i, 16:32], in_=ps3[:, 0:16]), R=[tp3], W=[t_sm])
            fw.op("dve", lambda e: e.tensor_tensor(out=sm[:, i, 32:40], in0=sm[:, i, 24:32], in1=sm[:, i, 16:24], op=ALU.subtract), R=[t_sm], W=[t_sm])
            fw.op("act", lambda e: e.activation(out=sm[:, i, 32:48], in_=sm[:, i, 32:48] if False else sm[:, i, 32:48], func=AF.Copy), R=[t_sm], W=[t_sm]) if False else None
            fw.op("act", lambda e: e.activation(out=sm[:, i, 32:40], in_=sm[:, i, 32:40], func=AF.Exp), R=[t_sm], W=[t_sm])
            fw.op("act", lambda e: e.activation(out=sm[:, i, 40:48], in_=sm[:, i, 24:32], func=AF.Exp), R=[t_sm], W=[t_sm])
        SC = self.scr
        tmp = [[SC[:, (k * 8 + j) * 128:(k * 8 + j + 1) * 128] for j in range(8)] for k in range(2)]
        t_tmp = [[Tok() for j in range(8)] for k in range(2)]
        cbt = [SC[:, 2048 + k * 256:2048 + (k + 1) * 256].rearrange("p (g t) -> p g t", t=128) for k in range(2)]
        t_cbt = [Tok(), Tok()]
        cnt = 0
        tps = tseg // 128
        for d in range(2):
            neg = negF if d == 0 else negB
            order = list(range(8)) if d == 0 else list(range(7, -1, -1))
            for oi, i in enumerate(order):
                tsl = slice(i * 128, (i + 1) * 128)
                sg = i // tps
                first = (oi % tps == 0)
                last = (oi % tps == tps - 1)
                kcb = (d * 8 + oi) % 2
                psc, tpc = self.psum()
                for g in range(2):
                    fw.op("pe", lambda e: e.matmul(psc[:, g * 128:(g + 1) * 128], lhsT=xbc[:, 2 + g, tsl], rhs=xbc[:, 4 + g, tsl], start=True, stop=True),
                          R=[tb], W=[tpc])
                fw.op("act", lambda e: e.activation(out=cbt[kcb][:].rearrange("p g t -> p (g t)"), in_=psc[:, 0:256], func=AF.Copy), R=[tpc], W=[t_cbt[kcb]])
                for hh in range(4):
                    k = cnt % 2
                    cnt += 1
                    T, tt = tmp[k], t_tmp[k]
                    j = d * 4 + hh
                    g = hh // 2
                    St, tS = self.ss_S[:, j, :], self.t_ssS[j]
                    if first:
                        if path == "P":
                            fw.op("pool", lambda e: e.memset(St, 0.0), W=[tS])
                        else:
                            fw.dma("sp", St, self.st_ss_d[l, d, hh], W=[tS])
                    acol = sm[:, i, 16 + j:17 + j]
                    fw.op("dve", lambda e: e.tensor_scalar(out=T[0], in0=ident, scalar1=acol, scalar2=None, op0=ALU.mult), R=[t_sm, self.t_const], W=[tt[0]])
                    psr, tpr = self.psum()
                    fw.op("pe", lambda e: e.matmul(psr[:, 0:128], lhsT=self.ones_f[:], rhs=T[0], start=True, stop=True), R=[tt[0], self.t_const], W=[tpr])
                    fw.op("dve", lambda e: e.scalar_tensor_tensor(out=T[1], in0=psr[:, 0:128], scalar=acol, in1=neg, op0=ALU.subtract, op1=ALU.add),
                          R=[tpr, t_sm, self.t_const], W=[tt[1]])
                    fw.op("act", lambda e: e.activation(out=T[1], in_=T[1], func=AF.Exp), R=[tt[1]], W=[tt[1]])
                    fw.op("pool", lambda e: e.tensor_tensor(out=T[1], in0=T[1], in1=cbt[kcb][:, g, :], op=ALU.mult), R=[tt[1], t_cbt[kcb]], W=[tt[1]])
                    fw.op("act", lambda e: e.activation(out=T[2], in_=psr[:, 0:128], func=AF.Exp), R=[tpr], W=[tt[2]])
                    fw.op("pool", lambda e: e.tensor_tensor(out=T[2], in0=T[2], in1=xbc[:, 4 + g, tsl], op=ALU.mult), R=[tt[2], tb], W=[tt[2]])
                    xs_h = tok[:, i, hh * 64:(hh + 1) * 64]
                    xdt = T[3][:, 0:64]
                    xdd = T[3][:, 64:128]
                    fw.op("dve", lambda e: e.tensor_scalar(out=xdt, in0=xs_h, scalar1=sm[:, i, j:j + 1], scalar2=None, op0=ALU.mult), R=[tb, t_sm], W=[tt[3]])
                    fw.op("dve", lambda e: e.tensor_scalar(out=xdd, in0=xdt, scalar1=sm[:, i, 32 + j:33 + j], scalar2=None, op0=ALU.mult), R=[tt[3], t_sm], W=[tt[3]])
                    psy, tpy = self.psum()
                    fw.op("pe", lambda e: e.matmul(psy[0:64, 0:128], lhsT=xdt, rhs=T[1], start=True, stop=False), R=[tt[3], tt[1]], W=[tpy])
                    fw.op("pe", lambda e: e.matmul(psy[0:64, 0:128], lhsT=St, rhs=T[2], start=False, stop=True), R=[tS, tt[2]], W=[tpy])
                    ydst = ysum[:, hh // 2, tsl]
                    if hh % 2 == 0:
                        if d == 0:
                            fw.op("act", lambda e: e.activation(out=ydst[0:64, :], in_=psy[0:64, 0:128], func=AF.Copy), R=[tpy], W=[tb])
                        else:
                            fw.op("dve", lambda e: e.tensor_tensor(out=ydst[0:64, :], in0=ydst[0:64, :], in1=psy[0:64, 0:128], op=ALU.add), R=[tpy, tb], W=[tb])
                    else:
                        psy2, tpy2 = self.psum()
                        fw.op("pe", lambda e: e.matmul(psy2[64:128, 0:128], lhsT=xdt, rhs=T[1], start=True, stop=False), R=[tt[3], tt[1]], W=[tpy2])
                        fw.op("pe", lambda e: e.matmul(psy2[64:128, 0:128], lhsT=St, rhs=T[2], start=False, stop=True), R=[tS, tt[2]], W=[tpy2])
                        if d == 0:
                            fw.op("act", lambda e: e.activation(out=ydst[64:128, :], in_=psy2[64:128, 0:128], func=AF.Copy), R=[tpy2], W=[tb])
                        else:
                            fw.op("dve", lambda e: e.tensor_tensor(out=ydst[64:128, :], in0=ydst[64:128, :], in1=psy2[64:128, 0:128], op=ALU.add), R=[tpy2, tb], W=[tb])
                    pss, tpss = self.psum()
                    fw.op("pe", lambda e: e.matmul(pss[:, 0:64], lhsT=tok[:, i, 256 + g * 128:256 + (g + 1) * 128], rhs=xdd, start=True, stop=True),
                          R=[tb, tt[3]], W=[tpss])
                    fw.op("dve", lambda e: e.scalar_tensor_tensor(out=St, in0=St, scalar=sm[:, i, 40 + j:41 + j], in1=pss[:, 0:64], op0=ALU.mult, op1=ALU.add),
                          R=[tpss, t_sm], W=[tS])
                    if last and path == "P":
                        t_o = Tok()
                        fw.dma("sp", self.o_ss_d[sg, l, d, hh], St, R=[tS], W=[t_o])
                        self.outs.append(t_o)
        dcol, ncol = pvc("ss_D", l), pvc("ss_norm", l)
        for c in range(2):
            fw.op("dve", lambda e: e.scalar_tensor_tensor(out=ysum[:, c, :], in0=xbc[:, c, :], scalar=self.pv[:, dcol + c:dcol + c + 1], in1=ysum[:, c, :],
                                                          op0=ALU.mult, op1=ALU.add), R=[tb, self.t_pv], W=[tb])
            fw.op("dve", lambda e: e.tensor_tensor(out=ysum[:, c, :], in0=ysum[:, c, :], in1=z[:, c, :], op=ALU.mult), R=[tb], W=[tb])
        self.rms_stats(lambda c: ysum[:, c, :], [tb], nch=2, dim=256)
        for c in range(2):
            fw.op("dve", lambda e: e.tensor_tensor(out=ysum[:, c, :], in0=ysum[:, c, :], in1=self.rstd[:], op=ALU.mult), R=[tb, self.t_rstd], W=[tb])
            fw.op("act", lambda e: e.activation(out=self.mix[:, 4 + c, :], in_=ysum[:, c, :], func=AF.Copy, scale=self.pv[:, ncol + c:ncol + c + 1]),
                  R=[tb, self.t_pv], W=[self.t_mix[4 + c]])


    def zero_rows(self):
        fw = self.fw
        zt = self.big[:, 0:2048]
        fw.op("pool", lambda e: e.memset(zt, 0.0), W=[self.t_big])
        self.t_RAd = Tok()
        for m in ("hg", "rw"):
            flat = self.RAd[m].rearrange("a r t f -> (a r t f)").rearrange("(n p x) -> n p x", p=128, x=2048)
            for i in range(flat.shape[0]):
                fw.dma("sp", flat[i], zt, R=[self.t_big], W=[self.t_RAd])

    def hg_bounds(self):
        fw = self.fw
        t = self.t_hgl
        ex = self.nc.alloc_sbuf_tensor("hg_ex", [128, DEPTH + 2, 4], F32)
        col = lambda l: self.pv[:, pvc("hg_lb", l):pvc("hg_lb", l) + 4]
        mx, sm = ex[:, DEPTH, :], ex[:, DEPTH + 1, :]
        fw.op("dve", lambda e: e.tensor_tensor(out=mx, in0=col(0), in1=col(1), op=ALU.max), R=[self.t_pv], W=[t])
        for l in range(2, DEPTH):
            fw.op("dve", lambda e: e.tensor_tensor(out=mx, in0=mx, in1=col(l), op=ALU.max), R=[self.t_pv, t], W=[t])
        for l in range(DEPTH):
            fw.op("dve", lambda e: e.tensor_tensor(out=ex[:, l, :], in0=col(l), in1=mx, op=ALU.subtract), R=[self.t_pv, t], W=[t])
            fw.op("act", lambda e: e.activation(out=ex[:, l, :], in_=ex[:, l, :], func=AF.Exp), R=[t], W=[t])
        fw.op("dve", lambda e: e.tensor_tensor(out=sm, in0=ex[:, 0, :], in1=ex[:, 1, :], op=ALU.add), R=[t], W=[t])
        for l in range(2, DEPTH):
            fw.op("dve", lambda e: e.tensor_tensor(out=sm, in0=sm, in1=ex[:, l, :], op=ALU.add), R=[t], W=[t])
        fw.op("dve", lambda e: e.reciprocal(out=sm, in_=sm), R=[t], W=[t])
        fw.op("pool", lambda e: e.memset(self.hgl[:, 0, 0:4], 0.0), W=[t])
        for l in range(1, DEPTH):
            fw.op("dve", lambda e: e.tensor_tensor(out=ex[:, l, :], in0=ex[:, l, :], in1=sm, op=ALU.mult), R=[t], W=[t])
            fw.op("dve", lambda e: e.tensor_tensor(out=self.hgl[:, l, 0:4], in0=self.hgl[:, l - 1, 0:4], in1=ex[:, l, :], op=ALU.add), R=[t], W=[t])
        for l in range(DEPTH):
            fw.op("dve", lambda e: e.tensor_scalar(out=self.hgl[:, l, 4:8], in0=self.hgl[:, l, 0:4], scalar1=-1.0, scalar2=1.0, op0=ALU.mult, op1=ALU.add),
                  R=[t], W=[t])

    def to_rows(self, src_fm, t_src, i, dsts):
        fw = self.fw
        ident = self.cst[:, 0, :]
        ps, tp = self.psum()
        fw.op("pe", lambda e: e.transpose(ps[:, 0:128], src_fm[:, i * 128:(i + 1) * 128], ident), R=t_src + [self.t_const], W=[tp])
        k = self.ntt % 2
        self.ntt += 1
        tt, ttok = self.ttile[k], self.t_ttile[k]
        fw.op("act", lambda e: e.activation(out=tt[:], in_=ps[:, 0:128], func=AF.Copy), R=[tp], W=[ttok])
        for dst, half in dsts:
            fw.dma("pool", dst, tt[:, half * 64:(half + 1) * 64], R=[ttok], W=[self.t_RAd])

    def scan_group(self, T, pairs, Tc=4):
        fw = self.fw
        npair = len(pairs)
        assert npair <= 4
        SC = self.scr
        import os
        KSTEP = int(os.environ.get('KSTEP', '3'))
        nch = min(T // Tc, int(os.environ.get('KNCH', '100000')))
        for pi, p in enumerate(pairs):
            p["RA"] = [SC[0:6, (pi * 2 + b) * Tc * 128:(pi * 2 + b + 1) * Tc * 128].rearrange("p (t f) -> p t f", f=128) for b in range(2)]
            base = npair * 2 * Tc * 128
            p["RB"] = [SC[0:6, base + (pi * 2 + b) * Tc * 64:base + (pi * 2 + b + 1) * Tc * 64].rearrange("p (t f) -> p t f", f=64) for b in range(2)]
            p["tRA"] = [Tok(), Tok()]
            p["tRB"] = [Tok(), Tok()]
            p["S"] = self.sc_S[:, pi, :]
            p["tS"] = self.t_scS[pi]
            if not p["init"]:
                pass
            elif p["S0"] is None:
                fw.op("pool", lambda e: e.memset(p["S"], 0.0), W=[p["tS"]])
            else:
                fw.dma("sp", p["S"], p["S0"], W=[p["tS"]])

        def tokbase(p, c):
            return c * Tc if p["d"] == 0 else T - (c + 1) * Tc

        def load(p, c):
            if KSTEP == -1:
                return
            b = c % 2
            tb0 = tokbase(p, c)
            fw.dma("sp", p["RA"][b], p["RAd"][:, tb0:tb0 + Tc, :], R=[self.t_RAd, p["tZ"]], W=[p["tRA"][b]])
            fw.dma("sp", p["RB"][b][4:6], p["RBd"][:, tb0:tb0 + Tc, :], R=[self.t_RAd, p["tZ"]], W=[p["tRB"][b]])
        for p in pairs:
            load(p, 0)
        for c in range(nch):
            b = c % 2
            if c + 1 < nch:
                for p in pairs:
                    load(p, c + 1)
            for j in range(Tc):
                i = c * Tc + j
                for pi, p in enumerate(pairs):
                    t = i if p["d"] == 0 else T - 1 - i
                    p["t"] = t
                    p["jb"] = t - tokbase(p, c)
                    if "b" in os.environ.get("KT", ""):
                        continue
                    fw.op("pe", lambda e: e.matmul(self.ps[pi][0:4, 0:64], lhsT=p["Z"][:, t + 1, :], rhs=p["S"], start=True, stop=True),
                          R=[p["tZ"], p["tS"]], W=[self.t_ps1[pi]])
                for pi, p in enumerate(pairs):
                    if KSTEP < 1:
                        continue
                    fw.op("act", lambda e: e.activation(out=p["RB"][b][0:4, p["jb"], :], in_=self.ps[pi][0:4, 0:64], func=AF.Copy),
                          R=[self.t_ps1[pi]], W=[p["tRB"][b]])
                for pi, p in enumerate(pairs):
                    if KSTEP < 2:
                        continue
                    fw.op("pe", lambda e: e.matmul(self.ps[pi][:, 64:128], lhsT=p["RA"][b][0:6, p["jb"], :], rhs=p["RB"][b][0:6, p["jb"], :],
                                                   start=True, stop=True), R=[p["tRA"][b], p["tRB"][b]], W=[self.t_ps2[pi]])
                for pi, p in enumerate(pairs):
                    t = p["t"]
                    if KSTEP < 3:
                        continue
                    fw.op("dve", lambda e: e.scalar_tensor_tensor(out=p["S"], in0=p["S"], scalar=p["W"][:, t:t + 1], in1=self.ps[pi][:, 64:128],
                                                                  op0=ALU.mult, op1=ALU.add), R=[self.t_ps2[pi], p["tW"]], W=[p["tS"]])
            for p in pairs:
                tb0 = tokbase(p, c)
                if KSTEP in (-1, -2):
                    continue
                fw.dma("sp", p["Yd"][:, 1 + tb0:1 + tb0 + Tc, :], p["RB"][b][0:4], R=[p["tRB"][b]], W=[p["tY"]])
        KT = os.environ.get("KT", "")
        for pi, p in enumerate(pairs):
            u = T + 1 if p["d"] == 0 else 0
            if "a" in KT or not p["final"]:
                continue
            fw.op("pe", lambda e: e.matmul(self.ps[pi][0:4, 0:64], lhsT=p["Z"][:, u, :], rhs=p["S"], start=True, stop=True),
                  R=[p["tZ"], p["tS"]], W=[self.t_ps1[pi]])
            if "e" in KT:
                continue
            fw.op("act", lambda e: e.activation(out=self.sc_x[0:4, pi, :], in_=self.ps[pi][0:4, 0:64], func=AF.Copy),
                  R=[self.t_ps1[pi]], W=[self.t_scx[pi]])
            if "c" not in KT:
                fw.dma("sp", p["Yd"][:, u, :], self.sc_x[0:4, pi, :], R=[self.t_scx[pi]], W=[p["tY"]])
            if p["out"] is not None and "d" not in KT:
                t_o = Tok()
                fw.dma("sp", p["out"], p["S"], R=[p["tS"]], W=[t_o])
                self.outs.append(t_o)

    def build_Z(self, Z, tZ, T, d, kk_fm, r_fm, t_src, tok0, has_prev=False, has_next=False):
        fw = self.fw
        for half in range(2):
            pr = slice(half * 64, (half + 1) * 64)
            cr = 2 * half + 1
            if kk_fm is not None:
                fw.op("pool", lambda e: e.tensor_copy(out=Z[pr, 1:1 + T, 2 * half], in_=kk_fm[pr, tok0:tok0 + T]), R=t_src, W=[tZ])
            if d == 0:
                if has_prev:
                    fw.op("pool", lambda e: e.tensor_copy(out=Z[pr, 1:2 + T, cr], in_=r_fm[pr, tok0 - 1:tok0 + T]), R=t_src, W=[tZ])
                else:
                    fw.op("pool", lambda e: e.tensor_copy(out=Z[pr, 2:2 + T, cr], in_=r_fm[pr, tok0:tok0 + T]), R=t_src, W=[tZ])
                    fw.op("pool", lambda e: e.memset(Z[pr, 1:2, cr], 0.0), W=[tZ])
            else:
                if has_next:
                    fw.op("pool", lambda e: e.tensor_copy(out=Z[pr, 0:T + 1, cr], in_=r_fm[pr, tok0:tok0 + T + 1]), R=t_src, W=[tZ])
                else:
                    fw.op("pool", lambda e: e.tensor_copy(out=Z[pr, 0:T, cr], in_=r_fm[pr, tok0:tok0 + T]), R=t_src, W=[tZ])
                    fw.op("pool", lambda e: e.memset(Z[pr, T:T + 1, cr], 0.0), W=[tZ])

    def read_y(self, T, tok0, hp, dst_fm, t_dst, y0=0):
        fw = self.fw
        ident = self.cst[:, 0, :]
        for i in range(T // 128):
            k = self.ntt % 2
            self.ntt += 1
            yt, tyt = self.ytile[k], self.t_ytile[k]
            for d in range(2):
                off = 2 if d == 0 else 0
                for hh in range(2):
                    fw.dma("sp", yt[:, d, hh * 64:(hh + 1) * 64], self.Yd[hp * 2 + d, 1 + 2 * hh, y0 + off + i * 128: y0 + off + (i + 1) * 128, :],
                           R=[self.t_Y[hp * 2 + d]], W=[tyt])
            fw.op("dve", lambda e: e.tensor_tensor(out=yt[:, 0, :], in0=yt[:, 0, :], in1=yt[:, 1, :], op=ALU.add), R=[tyt], W=[tyt])
            ps, tp = self.psum()
            fw.op("pe", lambda e: e.transpose(ps[:, 0:128], yt[:, 0, :], ident), R=[tyt, self.t_const], W=[tp])
            fw.op("act", lambda e: e.activation(out=dst_fm[:, tok0 + i * 128: tok0 + (i + 1) * 128], in_=ps[:, 0:128], func=AF.Copy), R=[tp], W=t_dst)


    def rw_mixer(self, path, l, nseg, tseg):
        fw = self.fw
        B, SC = self.big, self.scr
        tb = self.t_big
        big_bf, scr_bf = B[:].bitcast(BF16), SC[:].bitcast(BF16)
        ch2 = lambda ap: ap.rearrange("p (c t) -> p c t", t=NT)
        UB = lambda i, n=1: B[:, i * NT:(i + n) * NT]
        US = lambda i, n=1: SC[:, i * NT:(i + n) * NT]
        nkk, w = ch2(UB(0, 2)), ch2(UB(2, 4))
        Zreg = B[:, 6 * NT:6 * NT + 4128]
        r = ch2(UB(11, 2))
        G = ch2(big_bf[:, 26624:28672])
        BG = ch2(big_bf[:, 28672:30720])
        free = [UB(i) for i in range(6, 10)] + [US(i) for i in range(8)]
        if path == "S":
            shifts = [(-1, 64, 0), (1, 64, 1), (-64, NT, 2), (64, NT, 3)]
            mbase, ndir = pvc("mS"), 4
        else:
            shifts = [(-1, 256, 0), (1, 256, 1)]
            mbase, ndir = pvc("mP"), 2
        alloc = lambda: free.pop()
        ident, bones = self.cst[:, 0, :], self.cst[:, 5, :]
        hact = lambda kc, half: self.h[:, kc, half * 512:(half + 1) * 512]
        mu = pvc("rw_mu", l)
        tc_ = self.t_rwc
        fw.op("dve", lambda e: e.tensor_scalar(out=self.rwc[:, 0:8], in0=self.pv[:, mu:mu + 8], scalar1=-1.0, scalar2=1.0, op0=ALU.mult, op1=ALU.add),
              R=[self.t_pv], W=[tc_])
        for k in range(ndir):
            fw.op("dve", lambda e: e.tensor_tensor(out=self.rwc[:, 8 + 8 * k:16 + 8 * k], in0=self.pv[:, mu:mu + 8], in1=self.pv[:, mbase + 8 * k:mbase + 8 * k + 8],
                                                   op=ALU.mult), R=[self.t_pv], W=[tc_])
        ka_c, kk_c = pvc("rw_ka", l), pvc("rw_kk", l)
        fw.op("dve", lambda e: e.tensor_scalar(out=self.rwc[:, 50:52], in0=self.pv[:, ka_c:ka_c + 2], scalar1=-1.0, scalar2=1.0, op0=ALU.mult, op1=ALU.add),
              R=[self.t_pv], W=[tc_])
        raw0, raw1 = alloc(), alloc()
        raws = [raw0, raw1]

        def proj_mix(cbase, dsts):
            def ev(fc, half, ps, tp):
                fw.op("act", lambda e: e.activation(out=raws[fc][:, half * 512:(half + 1) * 512], in_=ps[:], func=AF.Copy), R=[tp], W=[tb])
            self.dense_fm(self.w_in_d[l], D, (C_RW + cbase) * 128, 256, hact, [self.t_h], ev)
            for cc in range(2):
                c = cbase + cc
                o, p_ = dsts[cc], raws[cc]
                fw.op("dve", lambda e: e.tensor_scalar(out=o, in0=p_, scalar1=self.rwc[:, c:c + 1], scalar2=None, op0=ALU.mult), R=[tb, tc_], W=[tb])
                for sft, sl, k in shifts:
                    sgm = lambda ap: ap.rearrange("p (s t) -> p s t", t=sl)
                    lo_o, hi_o = max(0, -sft), sl - max(0, sft)
                    o_ap = sgm(o)[:, :, lo_o:hi_o]
                    i_ap = sgm(p_)[:, :, lo_o + sft:hi_o + sft]
                    fw.op("dve", lambda e: e.scalar_tensor_tensor(out=o_ap, in0=i_ap, scalar=self.rwc[:, 8 + 8 * k + c:9 + 8 * k + c], in1=o_ap,
                                                                  op0=ALU.mult, op1=ALU.add), R=[tb, tc_], W=[tb])
        LA, LB = alloc(), alloc()
        proj_mix(6, [LA, LB])
        lwA, lwB = alloc(), alloc()
        lw = [lwA[:, 0:256], lwA[:, 256:512], lwA[:, 512:768], lwB[:, 0:256], lwB[:, 256:512]]
        for i_ in range(5):
            fw.dma("sp", lw[i_], self.rwlw_d[l, i_], W=[tb])
        tmp1 = alloc()
        w0c = pvc("rw_w0", l)
        fw.op("act", lambda e: e.activation(out=tmp1, in_=LA, func=AF.Tanh), R=[tb], W=[tb])
        for d in range(2):
            for hp in range(2):
                for half in range(2):
                    sl_ = slice(half * 512, (half + 1) * 512)
                    ps, tp = self.psum()
                    fw.op("pe", lambda e: e.matmul(ps[:], lhsT=lw[d][:, hp * 128:(hp + 1) * 128], rhs=tmp1[:, sl_], start=True, stop=True), R=[tb], W=[tp])
                    dst = w[:, d * 2 + hp, sl_]
                    fw.op("act", lambda e: e.activation(out=dst, in_=ps[:], func=AF.Sigmoid, bias=self.pv[:, w0c + d * 2 + hp:w0c + d * 2 + hp + 1], scale=1.0),
                          R=[tp, self.t_pv], W=[tb])
                    fw.op("act", lambda e: e.activation(out=dst, in_=dst, func=AF.Exp, scale=-0.6065306597126334), R=[tb], W=[tb])
        fw.op("act", lambda e: e.activation(out=tmp1, in_=LA, func=AF.Sigmoid), R=[tb], W=[tb])
        for hp in range(2):
            for half in range(2):
                sl_ = slice(half * 512, (half + 1) * 512)
                ps, tp = self.psum()
                fw.op("pe", lambda e: e.matmul(ps[:], lhsT=lw[2][:, hp * 128:(hp + 1) * 128], rhs=tmp1[:, sl_], start=True, stop=True), R=[tb], W=[tp])
                fw.op("act", lambda e: e.activation(out=G[:, hp, sl_], in_=ps[:], func=AF.Copy), R=[tp], W=[tb])
        free.append(LA)
        K0, K1 = alloc(), alloc()
        Kc = [K0, K1]
        proj_mix(2, Kc)
        tmp2 = alloc()
        for hp in range(2):
            fw.op("dve", lambda e: e.tensor_scalar(out=tmp1, in0=Kc[hp], scalar1=self.pv[:, kk_c + hp:kk_c + hp + 1], scalar2=None, op0=ALU.mult), R=[tb, self.t_pv], W=[tb])
            fw.op("dve", lambda e: e.tensor_tensor(out=tmp2, in0=tmp1, in1=tmp1, op=ALU.mult), R=[tb], W=[tb])
            for half in range(2):
                sl_ = slice(half * 512, (half + 1) * 512)
                ps, tp = self.psum()
                fw.op("pe", lambda e: e.matmul(ps[:], lhsT=bones, rhs=tmp2[:, sl_], start=True, stop=True), R=[tb, self.t_const], W=[tp])
                fw.op("act", lambda e: e.activation(out=nkk[:, hp, sl_], in_=ps[:], func=AF.Sqrt), R=[tp], W=[tb])
            fw.op("dve", lambda e: e.tensor_scalar(out=nkk[:, hp, :], in0=nkk[:, hp, :], scalar1=1e-12, scalar2=None, op0=ALU.max), R=[tb], W=[tb])
            fw.op("dve", lambda e: e.reciprocal(out=nkk[:, hp, :], in_=nkk[:, hp, :]), R=[tb], W=[tb])
            fw.op("dve", lambda e: e.scalar_tensor_tensor(out=nkk[:, hp, :], in0=nkk[:, hp, :], scalar=-1.0, in1=tmp1, op0=ALU.mult, op1=ALU.mult), R=[tb], W=[tb])
        a0c = pvc("rw_a0", l)
        ks0, ks1 = alloc(), alloc()
        ksum = [ks0, ks1]
        for d in range(2):
            for hp in range(2):
                for half in range(2):
                    sl_ = slice(half * 512, (half + 1) * 512)
                    ps, tp = self.psum()
                    fw.op("pe", lambda e: e.matmul(ps[:], lhsT=lw[3 + d][:, hp * 128:(hp + 1) * 128], rhs=LB[:, sl_], start=True, stop=True), R=[tb], W=[tp])
                    fw.op("act", lambda e: e.activation(out=tmp1[:, sl_], in_=ps[:], func=AF.Sigmoid, bias=self.pv[:, a0c + d * 2 + hp:a0c + d * 2 + hp + 1], scale=1.0),
                          R=[tp, self.t_pv], W=[tb])
                fw.op("dve", lambda e: e.scalar_tensor_tensor(out=tmp2, in0=nkk[:, hp, :], scalar=-1.0, in1=tmp1, op0=ALU.mult, op1=ALU.mult), R=[tb], W=[tb])
                for i in range(8):
                    self.to_rows(tmp2, [tb], i, [(self.RAd["rw"][hp * 2 + d, 2 * hh, i * 128:(i + 1) * 128, hh * 64:(hh + 1) * 64], hh) for hh in range(2)])
                fw.op("dve", lambda e: e.tensor_scalar(out=tmp1, in0=tmp1, scalar1=self.pv[:, ka_c + hp:ka_c + hp + 1], scalar2=self.rwc[:, 50 + hp:51 + hp],
                                                       op0=ALU.mult, op1=ALU.add), R=[tb, self.t_pv, tc_], W=[tb])
                fw.op("dve", lambda e: e.tensor_tensor(out=tmp1, in0=tmp1, in1=Kc[hp], op=ALU.mult), R=[tb], W=[tb])
                for i in range(8):
                    self.to_rows(tmp1, [tb], i, [(self.RAd["rw"][hp * 2 + d, 4 + hh, i * 128:(i + 1) * 128, hh * 64:(hh + 1) * 64], hh) for hh in range(2)])
                if d == 0:
                    fw.op("pool", lambda e: e.tensor_copy(out=ksum[hp], in_=tmp1), R=[tb], W=[tb])
                else:
                    fw.op("pool", lambda e: e.tensor_tensor(out=ksum[hp], in0=ksum[hp], in1=tmp1, op=ALU.add), R=[tb], W=[tb])
        free.extend([LB, K0, K1])
        proj_mix(0, [r[:, 0, :], r[:, 1, :]])
        rkc = pvc("rw_rk", l)
        for hp in range(2):
            fw.op("dve", lambda e: e.scalar_tensor_tensor(out=tmp1, in0=r[:, hp, :], scalar=self.pv[:, rkc + hp:rkc + hp + 1], in1=ksum[hp], op0=ALU.mult, op1=ALU.mult),
                  R=[tb, self.t_pv], W=[tb])
            for half in range(2):
                sl_ = slice(half * 512, (half + 1) * 512)
                ps, tp = self.psum()
                fw.op("pe", lambda e: e.matmul(ps[:], lhsT=bones, rhs=tmp1[:, sl_], start=True, stop=True), R=[tb, self.t_const], W=[tp])
                fw.op("act", lambda e: e.activation(out=ksum[hp][:, sl_], in_=ps[:], func=AF.Copy), R=[tp], W=[tb])
        V0, V1 = alloc(), alloc()
        Vc = [V0, V1]
        proj_mix(4, Vc)
        for hp in range(2):
            for i in range(8):
                self.to_rows(Vc[hp], [tb], i, [(self.RBd[hp, hh, i * 128:(i + 1) * 128, :], hh) for hh in range(2)])
            fw.op("dve", lambda e: e.tensor_tensor(out=tmp1, in0=ksum[hp], in1=Vc[hp], op=ALU.mult), R=[tb], W=[tb])
            fw.op("dve", lambda e: e.tensor_tensor(out=BG[:, hp, :], in0=tmp1, in1=G[:, hp, :], op=ALU.mult), R=[tb], W=[tb])
        self.run_scans("rw", path, l, nseg, tseg, Zreg, nkk, r, w, [tb], self.st_rw_d, self.o_rw_d)
        y = ch2(UB(6, 2))
        ycn = UB(8)
        t2 = UB(9)
        lnw, lnb = pvc("rw_ln_w", l), pvc("rw_ln_b", l)
        for hp in range(2):
            for s_ in range(nseg):
                self.read_y(tseg, s_ * tseg, hp, y[:, hp, :], [tb], y0=s_ * (tseg + 2))
            for half in range(2):
                sl_ = slice(half * 512, (half + 1) * 512)
                ps, tp = self.psum()
                fw.op("pe", lambda e: e.matmul(ps[:], lhsT=bones, rhs=y[:, hp, sl_], start=True, stop=True), R=[tb, self.t_const], W=[tp])
                fw.op("dve", lambda e: e.scalar_tensor_tensor(out=ycn[:, sl_], in0=ps[:], scalar=-1.0 / 64, in1=y[:, hp, sl_], op0=ALU.mult, op1=ALU.add),
                      R=[tp, tb], W=[tb])
            fw.op("dve", lambda e: e.tensor_tensor(out=t2, in0=ycn, in1=ycn, op=ALU.mult), R=[tb], W=[tb])
            for half in range(2):
                sl_ = slice(half * 512, (half + 1) * 512)
                ps, tp = self.psum()
                fw.op("pe", lambda e: e.matmul(ps[:], lhsT=bones, rhs=t2[:, sl_], start=True, stop=True), R=[tb, self.t_const], W=[tp])
                fw.op("act", lambda e: e.activation(out=self.rstd[:, sl_], in_=ps[:], func=AF.Sqrt, bias=self.epsc[:, 1:2], scale=1.0 / 64),
                      R=[tp, self.t_const], W=[self.t_rstd])
            fw.op("dve", lambda e: e.reciprocal(out=self.rstd[:], in_=self.rstd[:]), R=[self.t_rstd], W=[self.t_rstd])
            fw.op("dve", lambda e: e.tensor_tensor(out=ycn, in0=ycn, in1=self.rstd[:], op=ALU.mult), R=[tb, self.t_rstd], W=[tb])
            fw.op("act", lambda e: e.activation(out=ycn, in_=ycn, func=AF.Identity, bias=self.pv[:, lnb + hp:lnb + hp + 1], scale=self.pv[:, lnw + hp:lnw + hp + 1]),
                  R=[tb, self.t_pv], W=[tb])
            fw.op("dve", lambda e: e.tensor_tensor(out=ycn, in0=ycn, in1=G[:, hp, :], op=ALU.mult), R=[tb], W=[tb])
            fw.op("dve", lambda e: e.tensor_tensor(out=self.mix[:, hp, :], in0=ycn, in1=BG[:, hp, :], op=ALU.add), R=[tb], W=[self.t_mix[hp]])

    def hg_mixer(self, path, l, nseg, tseg):
        fw = self.fw
        B = self.big
        tb = self.t_big
        U = lambda i, n=1: B[:, i * NT:(i + n) * NT]
        ch2 = lambda ap: ap.rearrange("p (c t) -> p c t", t=NT)
        r, f = ch2(U(0, 2)), U(2, 4).rearrange("p (c t) -> p c t", t=NT)
        sg = ch2(self.scr[:, 6144:8192])
        v = ch2(U(8, 2))
        kd = ch2(U(10, 2))
        raw = ch2(U(12, 2))
        Zreg = B[:, 6 * NT:6 * NT + 8208 + 16]
        hact = lambda kc, half: self.h[:, kc, half * 512:(half + 1) * 512]

        def proj(cbase, dst, fn=AF.Copy):
            def ev(fc, half, ps, tp):
                fw.op("act", lambda e: e.activation(out=dst[:, fc, half * 512:(half + 1) * 512], in_=ps[:], func=fn), R=[tp], W=[tb])
            self.dense_fm(self.w_in_d[l], D, (C_HG + cbase) * 128, 256, hact, [self.t_h], ev)
        proj(0, raw)
        for c in range(2):
            fw.op("act", lambda e: e.activation(out=r[:, c, :], in_=raw[:, c, :], func=AF.Sigmoid), R=[tb], W=[tb])
            fw.op("dve", lambda e: e.tensor_tensor(out=r[:, c, :], in0=r[:, c, :], in1=raw[:, c, :], op=ALU.mult), R=[tb], W=[tb])
        proj(8, raw)
        for c in range(2):
            fw.op("act", lambda e: e.activation(out=sg[:, c, :], in_=raw[:, c, :], func=AF.Sigmoid), R=[tb], W=[tb])
            fw.op("dve", lambda e: e.tensor_tensor(out=sg[:, c, :], in0=sg[:, c, :], in1=raw[:, c, :], op=ALU.mult), R=[tb], W=[tb])
        proj(2, v)
        for hp in range(2):
            for i in range(8):
                self.to_rows(v[:, hp, :], [tb], i, [(self.RBd[hp, hh, i * 128:(i + 1) * 128, :], hh) for hh in range(2)])
        for d in range(2):
            proj(4 + 2 * d, raw, AF.Sigmoid)
            for hp in range(2):
                j = d * 2 + hp
                fw.op("dve", lambda e: e.tensor_scalar(out=f[:, j, :], in0=raw[:, hp, :], scalar1=self.hgl[:, l, 4 + j:5 + j], scalar2=self.hgl[:, l, j:j + 1],
                                                       op0=ALU.mult, op1=ALU.add), R=[tb, self.t_hgl], W=[tb])
                fw.op("dve", lambda e: e.tensor_scalar(out=kd[:, hp, :], in0=f[:, j, :], scalar1=-1.0, scalar2=1.0, op0=ALU.mult, op1=ALU.add), R=[tb], W=[tb])
                for i in range(8):
                    self.to_rows(kd[:, hp, :], [tb], i, [(self.RAd["hg"][hp * 2 + d, 4 + hh, i * 128:(i + 1) * 128, hh * 64:(hh + 1) * 64], hh) for hh in range(2)])
        self.run_scans("hg", path, l, nseg, tseg, Zreg, None, r, f, [tb], self.st_hg_d, self.o_hg_d)
        y = ch2(U(8, 2))
        for hp in range(2):
            for s_ in range(nseg):
                self.read_y(tseg, s_ * tseg, hp, y[:, hp, :], [tb], y0=s_ * (tseg + 2))
        self.rms_stats(lambda c: y[:, c, :], [tb], nch=2, dim=256)
        ncol = pvc("hg_norm", l)
        for c in range(2):
            fw.op("dve", lambda e: e.tensor_tensor(out=y[:, c, :], in0=y[:, c, :], in1=self.rstd[:], op=ALU.mult), R=[tb, self.t_rstd], W=[tb])
            fw.op("dve", lambda e: e.scalar_tensor_tensor(out=self.mix[:, 2 + c, :], in0=y[:, c, :], scalar=self.pv[:, ncol + c:ncol + c + 1], in1=sg[:, c, :],
                                                          op0=ALU.mult, op1=ALU.mult), R=[tb, self.t_pv], W=[self.t_mix[2 + c]])

    def run_scans(self, mname, path, l, nseg, tseg, Zreg, nkk, r, wdec, t_src, st_d, out_d):
        fw = self.fw
        tZ = self.t_big
        TB = 256
        fw.op("pool", lambda e: e.memset(Zreg, 0.0), R=t_src, W=[tZ])
        self.t_Y = [Tok() for _ in range(4)]
        nblk = NT // TB
        for g in range(nblk):
            pairs = []
            zi = 0
            for hp in range(2):
                for d in range(2):
                    if path == "P":
                        tok0, y0 = g * TB, g * (TB + 2)
                        hprev = hnext = False
                        init = final = True
                        S0, out = None, out_d[g, l, d, hp]
                    else:
                        tok0 = g * TB if d == 0 else (nblk - 1 - g) * TB
                        y0 = tok0
                        hprev, hnext = (d == 0 and g > 0), (d == 1 and g > 0)
                        init, final = (g == 0), (g == nblk - 1)
                        S0, out = st_d[l, d, hp], None
                    Z = Zreg[:, zi * (TB + 2) * 4:(zi + 1) * (TB + 2) * 4].rearrange("p (t c) -> p t c", c=4)
                    zi += 1
                    self.build_Z(Z, tZ, TB, d, None if nkk is None else nkk[:, hp, :], r[:, hp, :], t_src, tok0, hprev, hnext)
                    pairs.append(dict(d=d, Z=Z, tZ=tZ, W=wdec[:, d * 2 + hp, tok0:tok0 + TB], tW=t_src[0],
                                      RAd=self.RAd[mname][hp * 2 + d, :, tok0:tok0 + TB, :], RBd=self.RBd[hp, :, tok0:tok0 + TB, :],
                                      Yd=self.Yd[hp * 2 + d, :, y0:y0 + TB + 2, :], tY=self.t_Y[hp * 2 + d],
                                      S0=S0, out=out, init=init, final=final))
            import os
            if os.environ.get("KSCAN", "1") == "1":
                self.scan_group(TB, pairs)

    def lru_mixer(self, path, l, nseg, tseg):
        fw = self.fw
        S = self.big
        t_s = self.t_big
        off = [0]

        def carve(n):
            a = S[:, off[0]:off[0] + n]
            off[0] += n
            return a
        ybr = carve(2 * NT).rearrange("p (c t) -> p c t", t=NT)
        xbr = carve(2 * NT).rearrange("p (c t) -> p c t", t=NT)
        xc = carve(2 * NT).rearrange("p (c t) -> p c t", t=NT)
        ta = carve(NT)
        tb = carve(NT)
        tcv = carve(NT)
        hs = carve(2 * NT).rearrange("p (c t) -> p c t", t=NT)
        hact = lambda kc, half: self.h[:, kc, half * 512:(half + 1) * 512]

        def ev(fc, half, ps, tp):
            dst = (ybr if fc < 2 else xbr)[:, fc % 2, half * 512:(half + 1) * 512]
            fw.op("act", lambda e: e.activation(out=dst, in_=ps[:], func=AF.Copy), R=[tp], W=[t_s])
        self.dense_fm(self.w_in_d[l], D, C_LR * 128, 512, hact, [self.t_h], ev)
        cw, cb = pvc("lr_conv_w", l), pvc("lr_conv_b", l)
        seg = lambda ap: ap.rearrange("p (s t) -> p s t", t=tseg)
        for c in range(2):
            fw.op("dve", lambda e: e.tensor_scalar(out=xc[:, c, :], in0=xbr[:, c, :], scalar1=self.pv[:, cw + 2 * 2 + c:cw + 2 * 2 + c + 1],
                                                   scalar2=self.pv[:, cb + c:cb + c + 1], op0=ALU.mult, op1=ALU.add),
                  R=[t_s, self.t_pv], W=[t_s])
            for j, sft in ((0, -2), (1, -1), (3, 1)):
                lo_o, hi_o = max(0, -sft), tseg - max(0, sft)
                o_ap = seg(xc[:, c, :])[:, :, lo_o:hi_o]
                i_ap = seg(xbr[:, c, :])[:, :, lo_o + sft:hi_o + sft]
                fw.op("dve", lambda e: e.scalar_tensor_tensor(out=o_ap, in0=i_ap, scalar=self.pv[:, cw + 2 * j + c:cw + 2 * j + c + 1],
                                                              in1=o_ap, op0=ALU.mult, op1=ALU.add), R=[t_s, self.t_pv], W=[t_s])
        lw = self.scr[:, 0:1024].rearrange("p (a q) -> p a q", q=128)
        t_lw = self.t_scr
        fw.dma("sp", lw[:], self.lrw_d[l].rearrange("a d c p q -> p (a d c) q"), R=[self.t_big], W=[t_lw])
        lam, ba, bx = pvc("lr_lam", l), pvc("lr_ba", l), pvc("lr_bx", l)
        c1 = self.lr_c1
        fw.op("act", lambda e: e.activation(out=c1[:, 0:4], in_=self.pv[:, lam:lam + 4], func=AF.Exp, scale=-1.0), R=[self.t_pv], W=[t_lw])
        fw.op("act", lambda e: e.activation(out=c1[:, 0:4], in_=c1[:, 0:4], func=AF.Ln, bias=self.onec[:, 0:1], scale=1.0), R=[t_lw, self.t_const], W=[t_lw])
        fw.op("dve", lambda e: e.tensor_scalar(out=c1[:, 4:8], in0=c1[:, 0:4], scalar1=-16.0, scalar2=None, op0=ALU.mult), R=[t_lw], W=[t_lw])
        fw.op("dve", lambda e: e.tensor_scalar(out=c1[:, 0:4], in0=c1[:, 0:4], scalar1=-8.0, scalar2=None, op0=ALU.mult), R=[t_lw], W=[t_lw])
        cnd = 0 if path == "P" else 1
        for d in range(2):
            for c in range(2):
                k = d * 2 + c
                for half in range(2):
                    sl = slice(half * 512, (half + 1) * 512)
                    psa, tpa = self.psum()
                    psx, tpx = self.psum()
                    fw.op("pe", lambda e: e.matmul(psa[:], lhsT=lw[:, 0 * 4 + k, :], rhs=xc[:, c, sl], start=True, stop=True), R=[t_lw, t_s], W=[tpa])
                    fw.op("pe", lambda e: e.matmul(psx[:], lhsT=lw[:, 1 * 4 + k, :], rhs=xc[:, c, sl], start=True, stop=True), R=[t_lw, t_s], W=[tpx])
                    fw.op("act", lambda e: e.activation(out=ta[:, sl], in_=psa[:], func=AF.Sigmoid, bias=self.pv[:, ba + k:ba + k + 1], scale=1.0),
                          R=[tpa, self.t_pv], W=[t_s])
                    fw.op("act", lambda e: e.activation(out=tb[:, sl], in_=psx[:], func=AF.Sigmoid, bias=self.pv[:, bx + k:bx + k + 1], scale=1.0),
                          R=[tpx, self.t_pv], W=[t_s])
                fw.op("act", lambda e: e.activation(out=tcv, in_=ta, func=AF.Exp, scale=c1[:, 4 + k:5 + k]), R=[t_s, t_lw], W=[t_s])
                fw.op("act", lambda e: e.activation(out=ta, in_=ta, func=AF.Exp, scale=c1[:, k:k + 1]), R=[t_s, t_lw], W=[t_s])
                fw.op("dve", lambda e: e.tensor_scalar(out=tcv, in0=tcv, scalar1=-1.0, scalar2=1.0, op0=ALU.mult, op1=ALU.add), R=[t_s], W=[t_s])
                fw.op("dve", lambda e: e.tensor_scalar(out=tcv, in0=tcv, scalar1=0.0, scalar2=None, op0=ALU.max), R=[t_s], W=[t_s])
                fw.op("act", lambda e: e.activation(out=tcv, in_=tcv, func=AF.Sqrt), R=[t_s], W=[t_s])
                fw.op("dve", lambda e: e.tensor_tensor(out=tb, in0=tb, in1=xc[:, c, :], op=ALU.mult), R=[t_s], W=[t_s])
                fw.op("dve", lambda e: e.tensor_tensor(out=tb, in0=tb, in1=tcv, op=ALU.mult), R=[t_s], W=[t_s])
                sgn = 1 if d == 0 else -1
                tsc = 256
                segs = lambda ap: ap.rearrange("p (s t) -> p s t", t=tsc)
                sh = 1
                A, B, A2, B2 = ta, tb, tcv, carve_tmp(self, S)
                while sh < tsc:
                    lo, hi = (sh, tsc) if d == 0 else (0, tsc - sh)
                    cur = lambda ap: segs(ap)[:, :, lo:hi]
                    prv = lambda ap: segs(ap)[:, :, lo - sgn * sh:hi - sgn * sh]
                    keep = lambda ap: segs(ap)[:, :, 0:sh] if d == 0 else segs(ap)[:, :, tsc - sh:tsc]
                    fw.op("dve", lambda e: e.tensor_tensor(out=cur(B2), in0=cur(A), in1=prv(B), op=ALU.mult), R=[t_s], W=[t_s])
                    fw.op("dve", lambda e: e.tensor_tensor(out=cur(B2), in0=cur(B2), in1=cur(B), op=ALU.add), R=[t_s], W=[t_s])
                    fw.op("pool", lambda e: e.tensor_tensor(out=cur(A2), in0=cur(A), in1=prv(A), op=ALU.mult), R=[t_s], W=[t_s])
                    fw.op("pool", lambda e: e.tensor_copy(out=keep(B2), in_=keep(B)), R=[t_s], W=[t_s])
                    fw.op("pool", lambda e: e.tensor_copy(out=keep(A2), in_=keep(A)), R=[t_s], W=[t_s])
                    A, A2 = A2, A
                    B, B2 = B2, B
                    sh *= 2
                if path == "S":
                    st0 = pvc("st_lr") + (l * 2 + d) * 2 + c
                    hin = self.pv[:, st0:st0 + 1]
                    for sgi in (range(4) if d == 0 else range(3, -1, -1)):
                        bsl = slice(sgi * tsc, (sgi + 1) * tsc)
                        hcol = hin
                        fw.op("dve", lambda e: e.scalar_tensor_tensor(out=B[:, bsl], in0=A[:, bsl], scalar=hcol, in1=B[:, bsl], op0=ALU.mult, op1=ALU.add),
                              R=[t_s, self.t_pv], W=[t_s])
                        endpos = sgi * tsc + (tsc - 1 if d == 0 else 0)
                        hin = B[:, endpos:endpos + 1]
                if d == 0:
                    fw.op("dve", lambda e: e.tensor_copy(out=hs[:, c, :], in_=B), R=[t_s], W=[t_s])
                else:
                    fw.op("dve", lambda e: e.tensor_tensor(out=hs[:, c, :], in0=hs[:, c, :], in1=B, op=ALU.add), R=[t_s], W=[t_s])
                if path == "P":
                    for s in range(4):
                        pos = s * tseg + (tseg - 1 if d == 0 else 0)
                        col = ((s * DEPTH + l) * 2 + d) * 2 + c
                        fw.op("act", lambda e: e.activation(out=self.lru_fin[:, col:col + 1], in_=B[:, pos:pos + 1], func=AF.Copy),
                              R=[t_s], W=[self.t_lrufin])
        for c in range(2):
            y = ybr[:, c, :]
            fw.op("dve", lambda e: e.tensor_tensor(out=ta, in0=y, in1=y, op=ALU.mult), R=[t_s], W=[t_s])
            fw.op("dve", lambda e: e.tensor_scalar(out=ta, in0=ta, scalar1=0.044715, scalar2=1.0, op0=ALU.mult, op1=ALU.add), R=[t_s], W=[t_s])
            fw.op("dve", lambda e: e.tensor_tensor(out=ta, in0=ta, in1=y, op=ALU.mult), R=[t_s], W=[t_s])
            fw.op("act", lambda e: e.activation(out=ta, in_=ta, func=AF.Sigmoid, scale=1.5957691216057308), R=[t_s], W=[t_s])
            fw.op("dve", lambda e: e.tensor_tensor(out=ta, in0=ta, in1=y, op=ALU.mult), R=[t_s], W=[t_s])
            fw.op("dve", lambda e: e.tensor_tensor(out=self.mix[:, 6 + c, :], in0=ta, in1=hs[:, c, :], op=ALU.mult), R=[t_s], W=[self.t_mix[6 + c]])


def carve_tmp(prog, S):
    return S[:, 15 * NT:16 * NT]


def build_program(depth=DEPTH):
    p = Prog(depth)
    nc = p.nc
    p.lr_c1 = nc.alloc_sbuf_tensor("lr_c1", [128, 8], F32)
    p.onec = nc.alloc_sbuf_tensor("onec", [128, 1], F32)
    p.fw.op("pool", lambda e: e.memset(p.onec[:], 1.0), W=[p.t_const])
    p.build()
    return nc


def _cols(v):
    v = np.asarray(v, np.float32).reshape(-1, 128)
    return np.ascontiguousarray(v.T)


def _host_prep(inp):
    f = np.float32
    w_in = np.asarray(inp["w_in"], f)
    w_in_p = np.zeros((DEPTH, D, WINP), f)
    for dst, src, n in _win_colmap():
        w_in_p[:, :, dst:dst + n] = w_in[:, :, src:src + n]
    lr_w = np.zeros((DEPTH, 2, 2, 2, 128, 128), f)
    for ai, nm in enumerate(("lr_wa", "lr_wx")):
        w = np.asarray(inp[nm], f)
        for c in range(2):
            for hh in range(2):
                lr_w[:, ai, :, c, hh * 64:(hh + 1) * 64, hh * 64:(hh + 1) * 64] = w[:, :, 2 * c + hh]
    ssb = np.zeros((DEPTH, 128, 16), f)
    ssb[:, :, 0:8] = np.asarray(inp["ss_dt_bias"], f).reshape(DEPTH, 1, 8)
    ssb[:, :, 8:16] = np.asarray(inp["ss_A_log"], f).reshape(DEPTH, 1, 8)
    ii = np.arange(128)
    le = (ii[:, None] <= ii[None, :])
    blk = (ii[:, None] // 64 == ii[None, :] // 64)
    cst = np.stack([np.eye(128), le, le.T, np.where(le, 0.0, -30000.0), np.where(le.T, 0.0, -30000.0), blk], 1).astype(f)
    rw_lw = np.zeros((DEPTH, 5, 128, 256), f)
    rw_lw[:, 0, 0:16] = np.asarray(inp["rw_w2"], f)[:, 0]
    rw_lw[:, 1, 32:48] = np.asarray(inp["rw_w2"], f)[:, 1]
    rw_lw[:, 2, 64:96] = np.asarray(inp["rw_g2"], f)
    rw_lw[:, 3, 0:16] = np.asarray(inp["rw_a2"], f)[:, 0]
    rw_lw[:, 4, 32:48] = np.asarray(inp["rw_a2"], f)[:, 1]
    shared = dict(rw_lw=rw_lw, w_dt=np.ascontiguousarray(w_in[:, :, 3168:3176]), ssb=ssb, cst=np.ascontiguousarray(cst), w_in_p=w_in_p, w_out=np.asarray(inp["w_out"], f), w1=np.asarray(inp["mlp_w1"], f),
                  w2=np.asarray(inp["mlp_w2"], f), mod_w=np.asarray(inp["mod_w"], f), lr_w=lr_w)
    pv0 = np.zeros((128, _NPV), f)
    ng = np.asarray(inp["norm_g"], f)
    for l in range(DEPTH):
        for i, nm in enumerate(("g_pre1", "g_post1", "g_pre2", "g_post2")):
            pv0[:, pvc(nm, l):pvc(nm, l) + 8] = _cols(ng[l, i])
        pv0[:, pvc("mod_b", l):pvc("mod_b", l) + 48] = _cols(inp["mod_b"][l])
        pv0[:, pvc("lr_conv_w", l):pvc("lr_conv_w", l) + 8] = _cols(np.asarray(inp["lr_conv_w"][l]).reshape(-1))
        pv0[:, pvc("lr_conv_b", l):pvc("lr_conv_b", l) + 2] = _cols(inp["lr_conv_b"][l])
        for nm in ("lr_ba", "lr_bx", "lr_lam", "hg_lb"):
            pv0[:, pvc(nm, l):pvc(nm, l) + 4] = _cols(np.asarray(inp[nm][l]).reshape(-1))
        pv0[:, pvc("ss_conv_w", l):pvc("ss_conv_w", l) + 24] = _cols(np.asarray(inp["ss_conv_w"][l]).reshape(-1))
        pv0[:, pvc("ss_conv_b", l):pvc("ss_conv_b", l) + 6] = _cols(inp["ss_conv_b"][l])
        pv0[:, pvc("ss_D", l):pvc("ss_D", l) + 2] = _cols(np.repeat(np.asarray(inp["ss_D"][l], f), 64))
        pv0[:, pvc("ss_norm", l):pvc("ss_norm", l) + 2] = _cols(inp["ss_norm"][l])
        pv0[:, pvc("hg_norm", l):pvc("hg_norm", l) + 2] = _cols(inp["hg_norm"][l])
        mu_p = np.zeros(1024, f)
        for dst, src, n in _win_colmap():
            if dst < 1024:
                mu_p[dst:dst + n] = np.asarray(inp["rw_mu"], f)[l, src:src + n]
        pv0[:, pvc("rw_mu", l):pvc("rw_mu", l) + 8] = _cols(mu_p)
        for nm in ("rw_w0", "rw_a0"):
            pv0[:, pvc(nm, l):pvc(nm, l) + 4] = _cols(np.asarray(inp[nm][l]).reshape(-1))
        for nm in ("rw_kk", "rw_ka", "rw_rk", "rw_ln_w", "rw_ln_b"):
            pv0[:, pvc(nm, l):pvc(nm, l) + 2] = _cols(np.asarray(inp[nm][l]).reshape(-1))
    ch = np.full(1024, -1, np.int64)
    for dst, src, n in _win_colmap():
        if dst < 1024:
            ch[dst:dst + n] = np.arange(src, src + n)
    mP = np.stack([(ch >= 0) & (ch < 432), (ch >= 432)], 0).astype(f)
    mS = np.stack([(ch >= 0) & (ch < 216), (ch >= 216) & (ch < 432), (ch >= 432) & (ch < 648), (ch >= 648)], 0).astype(f)
    pv0[:, pvc("mP"):pvc("mP") + 16] = _cols(mP.reshape(-1))
    pv0[:, pvc("mS"):pvc("mS") + 32] = _cols(mS.reshape(-1))
    if True:
        pass
    return shared, pv0


_NC_CACHE = {}


def kernel(**inp):
    f = np.float32
    shared, pv0 = _host_prep(inp)
    xp = np.asarray(inp["x_prompt"], f)
    xs = np.asarray(inp["x_sample"], f)
    in_maps = []
    for c in range(NCORES):
        b = c % 4
        pv = pv0.copy()
        pv[:, pvc("cond"):pvc("cond") + 8] = _cols(inp["c_ctx"])
        pv[:, pvc("cond") + 8:pvc("cond") + 16] = _cols(inp["c"][b])
        pv[:, pvc("st_lr"):pvc("st_lr") + DEPTH * 4] = _cols(np.asarray(inp["state_lru"][b], f).reshape(-1))
        m = dict(shared)
        m["pv"] = pv
        m["xP"] = np.ascontiguousarray(xp[4 * c:4 * c + 4].reshape(NT, D).T)
        m["xS"] = np.ascontiguousarray(xs[b].T)
        m["st_ss"] = np.ascontiguousarray(np.asarray(inp["state_ssd"], f)[b].transpose(0, 1, 2, 4, 3))
        m["st_hg"] = np.ascontiguousarray(np.asarray(inp["state_hgrn"], f)[b].reshape(DEPTH, 2, 2, 128, 64))
        m["st_rw"] = np.ascontiguousarray(np.asarray(inp["state_rwkv"], f)[b].transpose(0, 1, 2, 4, 3).reshape(DEPTH, 2, 2, 128, 64))
        in_maps.append(m)
    if "nc" not in _NC_CACHE:
        _NC_CACHE["nc"] = build_program()
    res = run_bass_kernel_spmd(_NC_CACHE["nc"], in_maps, core_ids=list(range(NCORES)))
    R = res.results
    y_prompt = np.concatenate([R[c]["yP"].T.reshape(4, 256, D) for c in range(NCORES)], 0)
    y_sample = np.stack([R[b]["yS"].T for b in range(4)], 0)
    o_rw = np.concatenate([R[c]["o_rw"] for c in range(NCORES)], 0).reshape(32, DEPTH, 2, 2, 2, 64, 64)
    new_rw = np.ascontiguousarray(o_rw.reshape(32, DEPTH, 2, 4, 64, 64).transpose(0, 1, 2, 3, 5, 4))
    o_hg = np.concatenate([R[c]["o_hg"] for c in range(NCORES)], 0)
    new_hg = np.ascontiguousarray(o_hg.reshape(32, DEPTH, 2, 4, 64, 64))
    o_ss = np.concatenate([R[c]["o_ss"] for c in range(NCORES)], 0)
    new_ss = np.ascontiguousarray(o_ss.transpose(0, 1, 2, 3, 5, 4))
    o_lr = np.stack([R[c]["o_lr"] for c in range(NCORES)], 0)
    new_lr = o_lr.reshape(NCORES, 128, 4, DEPTH, 2, 2).transpose(0, 2, 3, 4, 5, 1).reshape(32, DEPTH, 2, 256)
    return (y_prompt.astype(f), y_sample.astype(f), new_rw.astype(f), new_hg.astype(f), new_ss.astype(f),
            np.ascontiguousarray(new_lr).astype(f))
```
